# Optimizing a Trainium2 kernel written in Bass

```python
import jax, jax.numpy as jnp
from jax import lax
import numpy as np

D_MODEL = 4096
BATCH = 2
SEQ = 8192
DEPTH = 1

MEM_LEN = 256
D_FF = 11008
POOL_WINDOWS = (2, 4, 8, 16)
N_POOL_GROUPS = len(POOL_WINDOWS)
POOL_GROUP = D_MODEL // 8
POOL_WIDTH = POOL_GROUP * N_POOL_GROUPS
MLA_HEADS = D_MODEL // 256
Q_LORA = D_MODEL // 4
KV_LORA = 512
QK_NOPE = 128
QK_ROPE = 64
V_HEAD = 128
QK_HEAD = QK_NOPE + QK_ROPE
MLA_WIDTH = MLA_HEADS * V_HEAD
X_HEADS = 4
X_HEAD_DIM = 128
X_WIDTH = X_HEADS * X_HEAD_DIM
ROPE_THETA = 10000.0
EPS = 1e-6
Q_BLOCK = 128
IN_SPLITS = (POOL_WIDTH, Q_LORA, KV_LORA, QK_ROPE, D_MODEL, D_MODEL)
IN_WIDTH = sum(IN_SPLITS)

kernel_name = "hybrid_pool_mla_macaron_block"


def rmsnorm(x, g):
    xf = x.astype(jnp.float32)
    y = xf * lax.rsqrt(jnp.mean(xf * xf, axis=-1, keepdims=True) + EPS)
    return (y * g.astype(jnp.float32)).astype(x.dtype)


def rope_tables(positions):
    half = QK_ROPE // 2
    inv = 1.0 / (ROPE_THETA ** (jnp.arange(half, dtype=jnp.float32) * (2.0 / QK_ROPE)))
    ang = positions.astype(jnp.float32)[:, :, None] * inv[None, None, :]
    return jnp.cos(ang)[:, :, None, :], jnp.sin(ang)[:, :, None, :]


def apply_rope(x, cos, sin):
    x1, x2 = jnp.split(x.astype(jnp.float32), 2, axis=-1)
    return jnp.concatenate([x1 * cos - x2 * sin, x2 * cos + x1 * sin], axis=-1).astype(x.dtype)


def swiglu(x, w_gu, w_down):
    g, u = jnp.split(x @ w_gu, 2, axis=-1)
    return (jax.nn.silu(g) * u) @ w_down


def causal_multiscale_pool(p, w_pool, pool_scale):
    B, S, _ = p.shape
    t = jnp.arange(S)
    outs = []
    for g, w in zip(jnp.split(p.astype(jnp.float32), N_POOL_GROUPS, axis=-1), POOL_WINDOWS):
        c = jnp.cumsum(g, axis=1)
        c_lag = jnp.pad(c, ((0, 0), (w, 0), (0, 0)))[:, :S]
        cnt = jnp.minimum(t + 1, w).astype(jnp.float32)[None, :, None]
        outs.append((c - c_lag) / cnt - g)
    mixed = jnp.stack(outs, axis=2).astype(p.dtype)
    mixed = jnp.einsum('bsgc,gcd->bsgd', mixed, w_pool)
    return mixed.reshape(B, S, POOL_WIDTH) * pool_scale


def causal_attention(q, k, v):
    B, H, S, Dk = q.shape
    nb = S // Q_BLOCK
    qb = q.reshape(B, H, nb, Q_BLOCK, Dk).transpose(2, 0, 1, 3, 4)
    kpos = jnp.arange(S)
    scale = Dk ** -0.5

    def one_block(args):
        qi, i = args
        s = jnp.einsum('bhqd,bhkd->bhqk', qi, k).astype(jnp.float32) * scale
        qpos = i * Q_BLOCK + jnp.arange(Q_BLOCK)
        s = jnp.where(kpos[None, :] <= qpos[:, None], s, -jnp.inf)
        pr = jax.nn.softmax(s, axis=-1).astype(v.dtype)
        return jnp.einsum('bhqk,bhkd->bhqd', pr, v)

    out = lax.map(one_block, (qb, jnp.arange(nb)))
    return out.transpose(1, 2, 0, 3, 4).reshape(B, H, S, v.shape[-1])


def setup_inputs(seed: int = 0) -> dict:
    key = jax.random.key(seed)
    ks = iter(jax.random.split(key, 40))

    def w(shape, fan_in):
        return jax.random.normal(next(ks), shape, jnp.float32) * (fan_in ** -0.5)

    def gain(n):
        return 1.0 + 0.02 * jax.random.normal(next(ks), (DEPTH, n), jnp.float32)

    L = DEPTH
    x = jax.random.normal(next(ks), (BATCH, SEQ, D_MODEL), jnp.float32)
    mem = jax.random.normal(next(ks), (BATCH, MEM_LEN, D_MODEL), jnp.float32)
    offset = jax.random.randint(next(ks), (BATCH, 1), 0, 1024, dtype=jnp.int32)
    positions = offset + jnp.arange(SEQ, dtype=jnp.int32)[None, :]
    return {
        "x": x, "mem": mem, "positions": positions,
        "ffn1_norm": gain(D_MODEL),
        "ffn1_w_gu": w((L, D_MODEL, 2 * D_FF), D_MODEL),
        "ffn1_w_down": w((L, D_FF, D_MODEL), D_FF),
        "mix_norm": gain(D_MODEL),
        "w_in": w((L, D_MODEL, IN_WIDTH), D_MODEL),
        "w_pool": w((L, N_POOL_GROUPS, POOL_GROUP, POOL_GROUP), POOL_GROUP),
        "pool_scale": gain(POOL_WIDTH),
        "q_latent_norm": gain(Q_LORA),
        "kv_latent_norm": gain(KV_LORA),
        "w_uq": w((L, Q_LORA, MLA_HEADS * QK_HEAD), Q_LORA),
        "w_ukv": w((L, KV_LORA, MLA_HEADS * (QK_NOPE + V_HEAD)), KV_LORA),
        "q_nope_norm": gain(QK_NOPE),
        "k_nope_norm": gain(QK_NOPE),
        "q_rope_norm": gain(QK_ROPE),
        "k_rope_norm": gain(QK_ROPE),
        "w_branch_pool": w((L, POOL_WIDTH, D_MODEL), POOL_WIDTH),
        "w_branch_mla": w((L, MLA_WIDTH, D_MODEL), MLA_WIDTH),
        "w_out": w((L, D_MODEL, D_MODEL), D_MODEL),
        "x_norm": gain(D_MODEL),
        "mem_norm": gain(D_MODEL),
        "w_xq": w((L, D_MODEL, X_WIDTH), D_MODEL),
        "w_xkv": w((L, D_MODEL, 2 * X_WIDTH), D_MODEL),
        "xq_norm": gain(X_HEAD_DIM),
        "xk_norm": gain(X_HEAD_DIM),
        "w_xo": w((L, X_WIDTH, D_MODEL), X_WIDTH),
        "ffn2_norm": gain(D_MODEL),
        "ffn2_w_gu": w((L, D_MODEL, 2 * D_FF), D_MODEL),
        "ffn2_w_down": w((L, D_FF, D_MODEL), D_FF),
    }


def reference(x, mem, positions, ffn1_norm, ffn1_w_gu, ffn1_w_down, mix_norm, w_in, w_pool,
              pool_scale, q_latent_norm, kv_latent_norm, w_uq, w_ukv, q_nope_norm, k_nope_norm,
              q_rope_norm, k_rope_norm, w_branch_pool, w_branch_mla, w_out, x_norm, mem_norm,
              w_xq, w_xkv, xq_norm, xk_norm, w_xo, ffn2_norm, ffn2_w_gu, ffn2_w_down):
    B, S, _ = x.shape
    M = mem.shape[1]
    cos, sin = rope_tables(positions)
    split_idx = [int(v) for v in np.cumsum(IN_SPLITS)[:-1]]
    h = x
    for l in range(DEPTH):
        h = h + 0.5 * swiglu(rmsnorm(h, ffn1_norm[l]), ffn1_w_gu[l], ffn1_w_down[l])

        u = rmsnorm(h, mix_norm[l])
        z_pool, z_q, z_kv, z_kr, g_pool, g_mla = jnp.split(u @ w_in[l], split_idx, axis=-1)

        a_out = causal_multiscale_pool(z_pool, w_pool[l], pool_scale[l])

        c_q = rmsnorm(z_q, q_latent_norm[l])
        q = (c_q @ w_uq[l]).reshape(B, S, MLA_HEADS, QK_HEAD)
        q_nope = rmsnorm(q[..., :QK_NOPE], q_nope_norm[l])
        q_rope = apply_rope(rmsnorm(q[..., QK_NOPE:], q_rope_norm[l]), cos, sin)
        c_kv = rmsnorm(z_kv, kv_latent_norm[l])
        kv = (c_kv @ w_ukv[l]).reshape(B, S, MLA_HEADS, QK_NOPE + V_HEAD)
        k_nope = rmsnorm(kv[..., :QK_NOPE], k_nope_norm[l])
        v = kv[..., QK_NOPE:]
        k_rope = apply_rope(rmsnorm(z_kr, k_rope_norm[l])[:, :, None, :], cos, sin)
        q_full = jnp.concatenate([q_nope, q_rope], axis=-1)
        k_full = jnp.concatenate([k_nope, jnp.broadcast_to(k_rope, (B, S, MLA_HEADS, QK_ROPE))], axis=-1)
        attn = causal_attention(q_full.transpose(0, 2, 1, 3), k_full.transpose(0, 2, 1, 3),
                                v.transpose(0, 2, 1, 3))
        b_out = attn.transpose(0, 2, 1, 3).reshape(B, S, MLA_WIDTH)

        merged = (jax.nn.sigmoid(g_pool) * (a_out @ w_branch_pool[l])
                  + jax.nn.sigmoid(g_mla) * (b_out @ w_branch_mla[l]))
        h = h + merged @ w_out[l]

        uq = rmsnorm(h, x_norm[l])
        mn = rmsnorm(mem, mem_norm[l])
        xq = rmsnorm((uq @ w_xq[l]).reshape(B, S, X_HEADS, X_HEAD_DIM), xq_norm[l])
        xk, xv = jnp.split(mn @ w_xkv[l], 2, axis=-1)
        xk = rmsnorm(xk.reshape(B, M, X_HEADS, X_HEAD_DIM), xk_norm[l])
        xv = xv.reshape(B, M, X_HEADS, X_HEAD_DIM)
        s = jnp.einsum('bshd,bmhd->bhsm', xq, xk).astype(jnp.float32) * (X_HEAD_DIM ** -0.5)
        pr = jax.nn.softmax(s, axis=-1).astype(xv.dtype)
        xo = jnp.einsum('bhsm,bmhd->bshd', pr, xv).reshape(B, S, X_WIDTH)
        h = h + xo @ w_xo[l]

        h = h + 0.5 * swiglu(rmsnorm(h, ffn2_norm[l]), ffn2_w_gu[l], ffn2_w_down[l])
    return h
```

```python
import numpy as np
import ml_dtypes
import concourse.bass as bass
import concourse.mybir as mybir
from concourse.bass_utils import run_bass_kernel_spmd
from contextlib import ExitStack

F32 = mybir.dt.float32
BF16 = mybir.dt.bfloat16
I32 = mybir.dt.int32
AF = mybir.ActivationFunctionType
ALU = mybir.AluOpType

ENGS = ("pe", "act", "dve", "pool", "sp")

D = 4096
DC = 32
DFF = 11008
T = 512
NH = 16
EPS = 1e-6
IN_POOL, IN_Q, IN_KV, IN_KR, IN_GP, IN_GM = 0, 2048, 3072, 3584, 3648, 3648 + 4096
G_F1, G_MIX, G_X, G_F2, G_MEM, G_PS, G_QL, G_KVL = 0, 32, 64, 96, 128, 160, 176, 184
G_QN, G_KN, G_XQ, G_XK, G_QR, G_QRS, G_KR, G_KRS, G_INV, G_SGN, G_BIAS, G_SEL = 188, 189, 190, 191, 192, 193, 194, 195, 196, 197, 198, 202
NG = 208
NEG = -30000.0
TWO_PI = 6.283185307179586
C1 = 6.28125
C2 = TWO_PI - C1
PI_SAFE = 3.1415925


class Sem:
    __slots__ = ("h", "count", "name")

    def __init__(self, h, name):
        self.h = h
        self.count = 0
        self.name = name


class Prog:
    def __init__(self, nc, es):
        self.nc = nc
        self.es = es
        self.ops = {e: [] for e in ENGS}
        self.prog_sem = {}
        self.waited = {e: {} for e in ENGS}
        self.W = {}
        self.R = {}
        self.nsem = 0
        self.ninstr = {e: 0 for e in ENGS}
        for e in ENGS:
            self.prog_sem[e] = self.new_sem("p_" + e)

    def new_sem(self, name):
        self.nsem += 1
        h = self.es.enter_context(self.nc.semaphore(f"{name}_{self.nsem}"))
        return Sem(h, name)

    def new_epoch(self):
        for e in ENGS:
            self.prog_sem[e] = self.new_sem("p_" + e)

    def _deps(self, eng, reads, writes, skip_sem=None, extra=None):
        need = {}
        if extra:
            for s, v in extra:
                if need.get(s, 0) < v:
                    need[s] = v
        for k in reads:
            w = self.W.get(k)
            if w:
                for s, v in w.items():
                    if need.get(s, 0) < v:
                        need[s] = v
        for k in writes:
            for d in (self.W.get(k), self.R.get(k)):
                if d:
                    for s, v in d.items():
                        if need.get(s, 0) < v:
                            need[s] = v
        wd = self.waited[eng]
        out = []
        for s, v in need.items():
            if s is skip_sem:
                continue
            if wd.get(s, 0) < v:
                wd[s] = v
                out.append((s, v))
        return out

    def _record(self, reads, writes, sem, val):
        for k in reads:
            d = self.R.get(k)
            if d is None:
                d = self.R[k] = {}
            if d.get(sem, 0) < val:
                d[sem] = val
        for k in writes:
            d = self.W.get(k)
            if d is None:
                d = self.W[k] = {}
            if d.get(sem, 0) < val:
                d[sem] = val

    def op(self, eng, ins, reads=(), writes=()):
        ps = self.prog_sem[eng]
        waits = self._deps(eng, reads, writes, skip_sem=ps if eng == "pe" else None)
        ps.count += 1
        self._record(reads, writes, ps, ps.count)
        self.ops[eng].append((waits, ins, ps, 1))
        self.ninstr[eng] += 1

    def pe_group(self, inss, reads=(), writes=()):
        ps = self.prog_sem["pe"]
        waits = self._deps("pe", reads, writes, skip_sem=ps)
        ps.count += 1
        self._record(reads, writes, ps, ps.count)
        lst = self.ops["pe"]
        n = len(inss)
        for i, ins in enumerate(inss):
            lst.append((waits if i == 0 else (), ins, ps if i == n - 1 else None, 1))
        self.ninstr["pe"] += n

    def dma(self, eng, ins, dsem, reads=(), writes=(), inc=16, guard=False):
        extra = [(dsem, dsem.count)] if (guard and dsem.count > 0) else None
        waits = self._deps(eng, reads, writes, extra=extra)
        dsem.count += inc
        self._record(reads, writes, dsem, dsem.count)
        self.ops[eng].append((waits, ins, dsem, inc))
        self.ninstr[eng] += 1

    def wait_all(self, eng, keys):
        waits = self._deps(eng, keys, ())
        self.ops[eng].append((waits, None, None, 0))

    def replay(self, eng, e):
        for waits, ins, sem, inc in self.ops[eng]:
            for s, v in waits:
                e.wait_ge(s.h, v)
            if ins is None:
                continue
            meth, args, kw = ins
            r = getattr(e, meth)(*args, **kw)
            if sem is not None:
                r.then_inc(sem.h, inc)

    def run_block(self):
        nc = self.nc
        with nc.Block() as block:
            @block.tensor
            def _(e):
                self.replay("pe", e)

            @block.scalar
            def _(e):
                self.replay("act", e)

            @block.vector
            def _(e):
                self.replay("dve", e)

            @block.gpsimd
            def _(e):
                self.replay("pool", e)

            @block.sync
            def _(e):
                self.replay("sp", e)


def I(meth, *args, **kw):
    return (meth, args, kw)


def build(NSTEP=4, debug=False, dff=DFF):
    SEQ = NSTEP * 2048
    KTOT = SEQ + 512
    NKB = KTOT // 128
    nc = bass.Bass("TRN2", target_bir_lowering=False)
    din = lambda name, shape, dt=F32: nc.dram_tensor(name, shape, dt, kind="ExternalInput").ap()
    dint = lambda name, shape, dt=F32: nc.dram_tensor(name, shape, dt).ap()
    x_d = din("x", [NSTEP * T, D])
    pos_d = din("pos", [NSTEP, T], I32)
    mem_d = din("mem", [256, D])
    g_d = din("gpack", [128, NG])
    cst_d = din("cst", [128, 192])
    tri_d = din("tri", [128, 4 * T], BF16)
    invc_d = din("invc", [128, 64])
    w_gu1 = din("ffn1_w_gu", [D, 2 * dff]); w_d1 = din("ffn1_w_down", [dff, D])
    w_gu2 = din("ffn2_w_gu", [D, 2 * dff]); w_d2 = din("ffn2_w_down", [dff, D])
    w_in = din("w_in", [D, 11840]); w_pool = din("w_pool", [2048, 512])
    w_uq = din("w_uq", [1024, 3072]); w_ukv = din("w_ukv", [512, 4096])
    w_pa = din("w_branch_pool", [2048, D]); w_pb = din("w_branch_mla", [2048, D])
    w_out = din("w_out", [D, D]); w_xq = din("w_xq", [D, 512]); w_xkv = din("w_xkv", [D, 1024]); w_xo = din("w_xo", [512, D])
    out_d = nc.dram_tensor("out", [NSTEP * T, D], F32, kind="ExternalOutput").ap()
    NDBG = 6
    dbg_d = nc.dram_tensor("dbg", [NDBG, 128, DC * T], F32, kind="ExternalOutput").ap() if debug else None
    dbgb_d = nc.dram_tensor("dbgb", [NDBG, 128, 16384], BF16, kind="ExternalOutput").ap() if debug else None
    hbuf = dint("hbuf", [128, DC * T])
    send_l = [dint(f"send_l{s}", [128, 5 * T], BF16) for s in range(NSTEP)]
    recv_l = [dint(f"recv_l{s}", [4 * 128, 5 * T], BF16) for s in range(NSTEP)]
    send_t = [dint(f"send_t{s}", [128, 256]) for s in range(NSTEP)]
    recv_t = [dint(f"recv_t{s}", [4 * 128, 256]) for s in range(NSTEP)]
    Kc = dint("Kc", [NH, 128, KTOT], BF16)
    Krc = dint("Krc", [64, KTOT], BF16)
    Vc = dint("Vc", [NH, 128, NKB, 128], BF16)

    with ExitStack() as es:
        P = Prog(nc, es)
        sb = lambda name, shape, dt: es.enter_context(nc.sbuf_tensor(name, shape, dt))
        H = sb("H", [128, DC, T], F32)
        U = sb("U", [128, DC, T], BF16)
        SC = sb("SC", [128, 16384], BF16)
        RING = [sb(f"ring{i}", [128, 8192], BF16) for i in range(3)]
        G = sb("G", [128, NG], F32)
        CST = sb("CST", [128, 192], F32)
        ONESF = sb("ONESF", [128, 128], F32)
        ONESB = sb("ONESB", [128, 128], BF16)
        TRI = sb("TRI", [128, 4, T], BF16)
        XK = sb("XK", [128, 4, 256], BF16)
        XV = sb("XV", [128, 2, 512], BF16)
        COSD = sb("COSD", [128, T], F32)
        SIND = sb("SIND", [128, T], F32)
        TMPA = sb("TMPA", [128, 6 * T], F32)
        PT = sb("PT", [128, 256], F32)
        INVC = sb("INVC", [128, 64], F32)
        GS = sb("GS", [128, 4], F32)
        PS = [es.enter_context(nc.psum_tensor(f"ps{i}", [128, T], F32)) for i in range(8)]
        ident = CST[:, 0:128]
        perm = CST[0:64, 128:192]
        TMP = [TMPA[:, i * T:(i + 1) * T] for i in range(6)]
        tk = lambda i: ("tmp", i)
        Hf = H[:].rearrange("p a b -> p (a b)")
        E = Hf[:, 0:16 * 528].rearrange("p (c t) -> p c t", t=528)
        AOUT = Hf[:, 0:4096].bitcast(BF16).rearrange("p (c t) -> p c t", t=T)
        BOUT = Hf[:, 4096:8192].bitcast(BF16).rearrange("p (c t) -> p c t", t=T)
        ZQ = H[:, 17:25, :]
        hk = lambda c: ("H", c)
        HALL = [hk(i) for i in range(DC)]

        def ekeys(c):
            lo, hi = c * 2112, (c + 1) * 2112 - 1
            return [hk(i) for i in range(lo // 2048, hi // 2048 + 1)]
        EALL = [hk(i) for i in range(17)]
        SCf = SC[:].bitcast(F32)
        ACT_ = SC[:].rearrange("p (c t) -> p c t", t=T)
        sck = lambda i: ("sc", i)
        SCALL = [sck(i) for i in range(32)]
        CQ = SC[:, 0:4096].rearrange("p (c t) -> p c t", t=T)
        AMX = SC[:, 4096:12288].rearrange("p (c t) -> p c t", t=T)
        SENDT = SC[:, 4096:4096 + 2560].rearrange("p (c t) -> p c t", t=T)
        PTILE = [SC[:, 12288 + i * T: 12288 + (i + 1) * T] for i in range(3)]
        QN = [SC[:, 13824 + i * T: 13824 + (i + 1) * T] for i in range(2)]
        QR = [SC[:, 14848 + i * T: 14848 + (i + 1) * T] for i in range(2)]
        GT = SCf[:, 0:1024].rearrange("p (r n) -> p r n", n=256)
        XQT = SC[:, 0:2048].rearrange("p (c t) -> p c t", t=T)
        XOT = SC[:, 2048:4096].rearrange("p (c t) -> p c t", t=T)
        XST = [SCf[:, i * 4096:(i + 1) * 4096] for i in range(2)]
        xstk = lambda i: [sck(j) for j in range(16 * i, 16 * i + 16)]

        st = {"bank": 0, "set": 0, "ring": 0, "misc": 0}
        reserved = set()
        ring_sem = [P.new_sem(f"ring{i}") for i in range(3)]
        d_miscs = [P.new_sem(f"d_misc{i}") for i in range(10)]
        d_x = [P.new_sem("d_x0"), P.new_sem("d_x1")]
        d_o = [P.new_sem("d_o0"), P.new_sem("d_o1")]
        d_kst = [P.new_sem(f"d_kst{i}") for i in range(7)]
        d_ccl = P.new_sem("d_ccl")
        d_cct = P.new_sem("d_cct")

        def misc_dma(ins, reads=(), writes=(), eng="sp"):
            i = st["misc"]
            st["misc"] = (i + 1) % len(d_miscs)
            P.dma(eng, ins, d_miscs[i], reads=reads, writes=writes, guard=True)

        def bank():
            while True:
                b = st["bank"]
                st["bank"] = (b + 1) % 8
                if b not in reserved:
                    return b

        def bank_set():
            for _ in range(2):
                s_ = st["set"]
                st["set"] ^= 1
                bs = [4 * s_ + j for j in range(4)]
                if not (set(bs) & reserved):
                    return bs
            raise RuntimeError("no free bank set")

        pk = lambda b: ("ps", b)

        def ring_load(inss_fn, reads=()):
            i = st["ring"]
            st["ring"] = (i + 1) % 3
            for ins in inss_fn(RING[i]):
                P.dma("pool", ins, ring_sem[i], reads=reads, writes=[("ring", i)])
            return i

        def w_slab(w_ap, k0, nk, c0, ncols):
            src = w_ap[k0 * 128:(k0 + nk) * 128, c0:c0 + ncols].rearrange("(kc p) n -> p kc n", p=128)
            i = ring_load(lambda slot: [I("dma_start", out=slot[:, 0:nk * ncols].rearrange("p (k n) -> p k n", n=ncols), in_=src)])
            return i, RING[i][:, 0:nk * ncols].rearrange("p (k n) -> p k n", n=ncols)

        def stream_linear(w_ap, k0w, rhs, col0, ncols, evac, ntok=T):
            nk = len(rhs)
            for cb in range(0, ncols, 512):
                nb = min(512, ncols - cb)
                nj = nb // 128
                banks = bank_set()
                reserved.update(banks[0:nj])
                for s0 in range(0, nk, 16):
                    ns = min(16, nk - s0)
                    si, sv = w_slab(w_ap, k0w + s0, ns, col0 + cb, nb)
                    for j in range(nj):
                        b = banks[j]
                        inss = [I("matmul", PS[b][:, 0:ntok], lhsT=sv[:, kk, j * 128:(j + 1) * 128], rhs=rhs[s0 + kk][0],
                                  start=(s0 + kk == 0), stop=(s0 + kk == nk - 1)) for kk in range(ns)]
                        P.pe_group(inss, reads=[("ring", si)] + [rhs[s0 + kk][1] for kk in range(ns)], writes=[pk(b)])
                for j in range(nj):
                    keep = evac(cb // 128 + j, PS[banks[j]], pk(banks[j]))
                    if not keep:
                        reserved.discard(banks[j])

        def rstd_from(ps_b, nfeat, t_out, rows=128, n=T):
            P.op("act", I("activation", out=TMP[t_out][0:rows, 0:n], in_=PS[ps_b][0:rows, 0:n], func=AF.Sqrt, scale=1.0 / nfeat, bias=EPS),
                 reads=[pk(ps_b)], writes=[tk(t_out)])
            P.op("dve", I("reciprocal", out=TMP[t_out][0:rows, 0:n], in_=TMP[t_out][0:rows, 0:n]), reads=[tk(t_out)], writes=[tk(t_out)])

        def norm_full(gcol):
            b = bank()
            reserved.add(b)
            for ci in range(DC):
                t = ci % 2
                P.op("act", I("activation", out=TMP[t], in_=H[:, ci, :], func=AF.Square), reads=[hk(ci)], writes=[tk(t)])
                P.pe_group([I("matmul", PS[b][:], lhsT=ONESF[:], rhs=TMP[t], start=(ci == 0), stop=(ci == DC - 1))],
                           reads=[tk(t), "onesf"], writes=[pk(b)])
            rstd_from(b, D, 2)
            reserved.discard(b)
            for ci in range(DC):
                P.op("dve", I("scalar_tensor_tensor", out=U[:, ci, :], in0=H[:, ci, :], scalar=G[:, gcol + ci:gcol + ci + 1], in1=TMP[2],
                              op0=ALU.mult, op1=ALU.mult), reads=[hk(ci), tk(2), "G"], writes=[("U", ci)])

        Urhs = [(U[:, ci, :], ("U", ci)) for ci in range(DC)]

        def ffn(w_gu, w_d):
            nch = dff // 128
            parts = [(i, min(i + 32, nch)) for i in range(0, nch, 32)]
            for (c0, c1) in parts:
                for cb in range(c0 * 128, c1 * 128, 512):
                    nb = min(512, c1 * 128 - cb)

                    def evac_g(j, ps, pkey):
                        P.op("act", I("activation", out=TMP[j], in_=ps[:], func=AF.Silu), reads=[pkey], writes=[tk(j)])

                    def evac_u(j, ps, pkey, cb=cb, c0=c0):
                        a = cb // 128 + j - c0
                        P.op("dve", I("tensor_tensor", out=ACT_[:, a, :], in0=TMP[j], in1=ps[:], op=ALU.mult), reads=[pkey, tk(j)], writes=[sck(a)])
                    stream_linear(w_gu, 0, Urhs, cb, nb, evac_g)
                    stream_linear(w_gu, 0, Urhs, dff + cb, nb, evac_u)
                arhs_ = [(ACT_[:, a, :], sck(a)) for a in range(c1 - c0)]

                def evac_d(m, ps, pkey):
                    P.op("dve", I("scalar_tensor_tensor", out=H[:, m, :], in0=ps[:], scalar=0.5, in1=H[:, m, :], op0=ALU.mult, op1=ALU.add),
                         reads=[pkey, hk(m)], writes=[hk(m)])
                stream_linear(w_d, c0, arhs_, 0, D, evac_d)

        def partition_norm_stats(ps_b, nrows, t_sq, t_r, n=T):
            P.op("act", I("activation", out=TMP[t_sq][0:nrows, 0:n], in_=PS[ps_b][0:nrows, 0:n], func=AF.Square), reads=[pk(ps_b)], writes=[tk(t_sq)])
            b2 = bank()
            P.pe_group([I("matmul", PS[b2][0:nrows, 0:n], lhsT=ONESF[0:nrows, 0:nrows], rhs=TMP[t_sq][0:nrows, 0:n], start=True, stop=True)],
                       reads=[tk(t_sq), "onesf"], writes=[pk(b2)])
            rstd_from(b2, nrows, t_r, rows=nrows, n=n)

        def rope_norm(ps_b, gcol, gscol, out_ap, out_key, gtile, gkey):
            P.op("act", I("copy", out=TMP[0][0:64, :], in_=PS[ps_b][0:64, :]), reads=[pk(ps_b)], writes=[tk(0)])
            b_sw = bank()
            P.pe_group([I("matmul", PS[b_sw][0:64, :], lhsT=perm, rhs=TMP[0][0:64, :], start=True, stop=True)], reads=[tk(0), "cst"], writes=[pk(b_sw)])
            partition_norm_stats(ps_b, 64, 1, 2)
            P.op("dve", I("scalar_tensor_tensor", out=TMP[3][0:64, :], in0=TMP[0][0:64, :], scalar=gtile[0:64, gcol:gcol + 1], in1=COSD[0:64, :],
                          op0=ALU.mult, op1=ALU.mult), reads=[tk(0), "rope", gkey], writes=[tk(3)])
            P.op("dve", I("scalar_tensor_tensor", out=TMP[4][0:64, :], in0=PS[b_sw][0:64, :], scalar=gtile[0:64, gscol:gscol + 1], in1=SIND[0:64, :],
                          op0=ALU.mult, op1=ALU.mult), reads=[pk(b_sw), "rope", gkey], writes=[tk(4)])
            P.op("dve", I("tensor_tensor", out=TMP[3][0:64, :], in0=TMP[3][0:64, :], in1=TMP[4][0:64, :], op=ALU.add), reads=[tk(3), tk(4)], writes=[tk(3)])
            P.op("dve", I("tensor_tensor", out=out_ap, in0=TMP[3][0:64, :], in1=TMP[2][0:64, :], op=ALU.mult), reads=[tk(3), tk(2)], writes=[out_key])

        def dump(slot, keys):
            if debug:
                misc_dma(I("dma_start", out=dbg_d[slot], in_=Hf), reads=keys, writes=[("dbg", slot)])

        def dumpb(slot, keys):
            if debug:
                misc_dma(I("dma_start", out=dbgb_d[slot], in_=SC[:]), reads=keys, writes=[("dbgb", slot)])

        def softmax_pv(blocks, qk_fn, v_fn, out_ap, out_key, extra_reads):
            bo, bd = bank(), bank()
            reserved.add(bo); reserved.add(bd)
            nb_ = len(blocks)
            pend = None
            for i in range(nb_ + 1):
                if i < nb_:
                    blk = blocks[i]
                    if "loader" in blk:
                        blk["reads"] = [("ring", blk["loader"]())]
                    bs = bank()
                    pt = i % 3
                    P.pe_group(qk_fn(i, bs), reads=blk["reads"] + extra_reads, writes=[pk(bs)])
                    if blk["bias"] is not None:
                        P.op("act", I("activation", out=PTILE[pt], in_=PS[bs][:], func=AF.Exp, bias=blk["bias"], scale=1.0), reads=[pk(bs), "G"], writes=[sck(24 + pt)])
                    else:
                        P.op("act", I("activation", out=PTILE[pt], in_=PS[bs][:], func=AF.Exp), reads=[pk(bs)], writes=[sck(24 + pt)])
                    if blk["tri"] is not None:
                        P.op("dve", I("tensor_tensor", out=PTILE[pt], in0=PTILE[pt], in1=TRI[:, blk["tri"], :], op=ALU.mult),
                             reads=[sck(24 + pt), "tri"], writes=[sck(24 + pt)])
                if pend is not None:
                    pi, ppt = pend
                    P.pe_group([I("matmul", PS[bo][:], lhsT=v_fn(pi), rhs=PTILE[ppt], start=(pi == 0), stop=(pi == nb_ - 1)),
                                I("matmul", PS[bd][:], lhsT=ONESB[:], rhs=PTILE[ppt], start=(pi == 0), stop=(pi == nb_ - 1))],
                               reads=blocks[pi]["reads"] + [sck(24 + ppt), "onesb"], writes=[pk(bo), pk(bd)])
                if i < nb_:
                    pend = (i, pt)
            P.op("dve", I("reciprocal", out=TMP[5], in_=PS[bd][:]), reads=[pk(bd)], writes=[tk(5)])
            P.op("dve", I("tensor_tensor", out=out_ap, in0=PS[bo][:], in1=TMP[5], op=ALU.mult), reads=[pk(bo), tk(5)], writes=[out_key])
            reserved.discard(bo); reserved.discard(bd)

        misc_dma(I("dma_start", out=G[:], in_=g_d[:, :]), writes=["G"])
        misc_dma(I("dma_start", out=CST[:], in_=cst_d[:, :]), writes=["cst"])
        misc_dma(I("dma_start", out=TRI[:].rearrange("p a b -> p (a b)"), in_=tri_d[:, :]), writes=["tri"])
        misc_dma(I("dma_start", out=INVC[:], in_=invc_d[:, :]), writes=["invc"])
        P.op("dve", I("memset", ONESF[:], 1.0), writes=["onesf"])
        P.op("dve", I("memset", ONESB[:], 1.0), writes=["onesb"])
        P.op("dve", I("memset", PT[:], 0.0), writes=["PT"])
        sc_q = 192.0 ** -0.5
        P.op("dve", I("tensor_scalar", out=GS[:, 0:1], in0=G[:, G_QN:G_QN + 1], scalar1=sc_q, scalar2=None, op0=ALU.mult), reads=["G"], writes=["GS"])
        P.op("dve", I("tensor_scalar", out=GS[:, 1:3], in0=G[:, G_QR:G_QR + 2], scalar1=sc_q, scalar2=None, op0=ALU.mult), reads=["G"], writes=["GS"])
        P.op("dve", I("tensor_scalar", out=GS[:, 3:4], in0=G[:, G_XQ:G_XQ + 1], scalar1=128.0 ** -0.5, scalar2=None, op0=ALU.mult), reads=["G"], writes=["GS"])

        MNT = SC[:, 0:8192].rearrange("p (c t) -> p c t", t=256)
        MST = SCf[:, 4096:8192].rearrange("p (a f) -> p a f", f=2048)
        mstk = [sck(j) for j in range(16, 32)]
        bss = bank()
        reserved.add(bss)
        for pas in range(2):
            for fg in range(2):
                P.dma("sp", I("dma_start", out=MST, in_=mem_d[:, fg * 2048:(fg + 1) * 2048].rearrange("(a p) f -> p a f", p=128)), d_x[0], writes=mstk)
                for cl in range(16):
                    ci = fg * 16 + cl
                    b = bank()
                    P.pe_group([I("transpose", PS[b][:, a * 128:(a + 1) * 128], MST[:, a, cl * 128:(cl + 1) * 128], ident) for a in range(2)],
                               reads=mstk + ["cst"], writes=[pk(b)])
                    if pas == 0:
                        t = ci % 2
                        P.op("act", I("activation", out=TMP[t][:, 0:256], in_=PS[b][:, 0:256], func=AF.Square), reads=[pk(b)], writes=[tk(t)])
                        P.pe_group([I("matmul", PS[bss][:, 0:256], lhsT=ONESF[:], rhs=TMP[t][:, 0:256], start=(ci == 0), stop=(ci == DC - 1))],
                                   reads=[tk(t), "onesf"], writes=[pk(bss)])
                    else:
                        P.op("dve", I("scalar_tensor_tensor", out=MNT[:, ci, :], in0=PS[b][:, 0:256], scalar=G[:, G_MEM + ci:G_MEM + ci + 1],
                                      in1=TMP[2][:, 0:256], op0=ALU.mult, op1=ALU.mult), reads=[pk(b), tk(2), "G"], writes=[sck(ci // 2)])
            if pas == 0:
                rstd_from(bss, D, 2, n=256)
                reserved.discard(bss)
        mrhs = [(MNT[:, ci, :], sck(ci // 2)) for ci in range(DC)]

        def evac_xk(j, ps, pkey):
            partition_norm_stats(pkey[1], 128, 0, 1, n=256)
            P.op("dve", I("scalar_tensor_tensor", out=XK[:, j, :], in0=ps[:, 0:256], scalar=G[:, G_XK:G_XK + 1], in1=TMP[1][:, 0:256],
                          op0=ALU.mult, op1=ALU.mult), reads=[pkey, tk(1), "G"], writes=["XK"])
        stream_linear(w_xkv, 0, mrhs, 0, 512, evac_xk, ntok=256)
        banks = bank_set()
        for s0 in (0, 16):
            si, sv = w_slab(w_xkv, s0, 16, 512, 512)
            for a in range(2):
                P.pe_group([I("matmul", PS[banks[a]][:], lhsT=mrhs[s0 + kk][0][:, a * 128:(a + 1) * 128], rhs=sv[:, kk, :],
                              start=(s0 + kk == 0), stop=(s0 + kk == DC - 1)) for kk in range(16)],
                           reads=[("ring", si)] + [mrhs[s0 + kk][1] for kk in range(16)], writes=[pk(banks[a])])
        for a in range(2):
            P.op("act", I("copy", out=XV[:, a, :], in_=PS[banks[a]][:]), reads=[pk(banks[a])], writes=["XV"])

        rg = [[0, 1, 2, 3], [4, 5, 6, 7]]
        for s in range(NSTEP):
            P.new_epoch()
            for tc in range(4):
                xs = XST[tc % 2]
                P.dma("sp", I("dma_start", out=xs, in_=x_d[s * T + tc * 128: s * T + (tc + 1) * 128, :]), d_x[tc % 2], writes=xstk(tc % 2))
                for cg in range(8):
                    b = bank()
                    P.pe_group([I("transpose", PS[b][:, i * 128:(i + 1) * 128], xs[:, (4 * cg + i) * 128:(4 * cg + i + 1) * 128], ident) for i in range(4)],
                               reads=xstk(tc % 2) + ["cst"], writes=[pk(b)])
                    dst = H[:, 4 * cg:4 * cg + 4, tc * 128:(tc + 1) * 128]
                    srcv = PS[b][:].rearrange("p (i t) -> p i t", t=128)
                    if cg % 2 == 0:
                        P.op("act", I("copy", out=dst, in_=srcv), reads=[pk(b)], writes=[hk(4 * cg + i) for i in range(4)])
                    else:
                        P.op("dve", I("tensor_copy", out=dst, in_=srcv), reads=[pk(b)], writes=[hk(4 * cg + i) for i in range(4)])
            POSI = TMPA[:, 5 * T:6 * T].bitcast(I32)
            KI = TMPA[:, 4 * T:5 * T].bitcast(I32)
            misc_dma(I("dma_start", out=POSI[0:64, :], in_=pos_d[s].partition_broadcast(64)), writes=[tk(5)])
            P.op("dve", I("tensor_copy", out=TMP[0][0:64, :], in_=POSI[0:64, :]), reads=[tk(5)], writes=[tk(0)])
            P.op("dve", I("tensor_scalar", out=TMP[0][0:64, :], in0=TMP[0][0:64, :], scalar1=G[0:64, G_INV:G_INV + 1], scalar2=None, op0=ALU.mult),
                 reads=[tk(0), "G"], writes=[tk(0)])
            t1, t2 = TMP[1][0:64, :], TMP[2][0:64, :]
            for which, dst in ((0, SIND), (1, COSD)):
                shift = 0.0 if which == 0 else float(np.pi / 2)
                P.op("dve", I("tensor_scalar", out=t1, in0=TMP[0][0:64, :], scalar1=shift, scalar2=None, op0=ALU.add), reads=[tk(0)], writes=[tk(1)])
                P.op("dve", I("tensor_scalar", out=KI[0:64, :], in0=t1, scalar1=1.0 / TWO_PI, scalar2=None, op0=ALU.mult), reads=[tk(1)], writes=[tk(4)])
                P.op("dve", I("tensor_copy", out=t2, in_=KI[0:64, :]), reads=[tk(4)], writes=[tk(2)])
                P.op("dve", I("scalar_tensor_tensor", out=t1, in0=t2, scalar=-C1, in1=t1, op0=ALU.mult, op1=ALU.add), reads=[tk(1), tk(2)], writes=[tk(1)])
                P.op("dve", I("scalar_tensor_tensor", out=t1, in0=t2, scalar=-C2, in1=t1, op0=ALU.mult, op1=ALU.add), reads=[tk(1), tk(2)], writes=[tk(1)])
                P.op("dve", I("tensor_scalar", out=t2, in0=t1, scalar1=float(np.pi), scalar2=-TWO_PI, op0=ALU.is_gt, op1=ALU.mult), reads=[tk(1)], writes=[tk(2)])
                P.op("dve", I("tensor_tensor", out=t1, in0=t1, in1=t2, op=ALU.add), reads=[tk(1), tk(2)], writes=[tk(1)])
                P.op("dve", I("tensor_scalar", out=t2, in0=t1, scalar1=float(-np.pi), scalar2=TWO_PI, op0=ALU.is_lt, op1=ALU.mult), reads=[tk(1)], writes=[tk(2)])
                P.op("dve", I("tensor_tensor", out=t1, in0=t1, in1=t2, op=ALU.add), reads=[tk(1), tk(2)], writes=[tk(1)])
                P.op("dve", I("tensor_scalar", out=t1, in0=t1, scalar1=-PI_SAFE, scalar2=PI_SAFE, op0=ALU.max, op1=ALU.min), reads=[tk(1)], writes=[tk(1)])
                P.op("act", I("activation", out=dst[0:64, :], in_=t1, func=AF.Sin), reads=[tk(1)], writes=["rope"])
            P.op("dve", I("tensor_scalar", out=SIND[0:64, :], in0=SIND[0:64, :], scalar1=G[0:64, G_SGN:G_SGN + 1], scalar2=None, op0=ALU.mult),
                 reads=["rope", "G"], writes=["rope"])

            norm_full(G_F1)
            ffn(w_gu1, w_d1)
            if s == 0:
                dump(0, HALL)
            norm_full(G_MIX)
            misc_dma(I("dma_start", out=hbuf[:, :], in_=Hf), reads=HALL, writes=["hbuf"])

            def evac_pool(j, ps, pkey):
                P.op("act", I("copy", out=E[:, j, 16:528], in_=ps[:]), reads=[pkey], writes=ekeys(j))
            stream_linear(w_in, 0, Urhs, IN_POOL, 2048, evac_pool)
            misc_dma(I("dma_start", out=send_t[s][:, :].rearrange("p (c t) -> p c t", t=16), in_=E[:, :, 512:528]), reads=EALL, writes=[("send_t", s)])
            bkv = bank()
            reserved.add(bkv)

            def evac_kv(j, ps, pkey):
                P.op("act", I("copy", out=TMP[j], in_=ps[:]), reads=[pkey], writes=[tk(j)])
                t = 4 + j % 2
                P.op("act", I("activation", out=TMP[t], in_=ps[:], func=AF.Square), reads=[pkey], writes=[tk(t)])
                P.pe_group([I("matmul", PS[bkv][:], lhsT=ONESF[:], rhs=TMP[t], start=(j == 0), stop=(j == 3))], reads=[tk(t), "onesf"], writes=[pk(bkv)])
            stream_linear(w_in, 0, Urhs, IN_KV, 512, evac_kv)
            rstd_from(bkv, 512, 4)
            reserved.discard(bkv)
            for j in range(4):
                P.op("dve", I("scalar_tensor_tensor", out=SENDT[:, j, :], in0=TMP[j], scalar=G[:, G_KVL + j:G_KVL + j + 1], in1=TMP[4], op0=ALU.mult, op1=ALU.mult),
                     reads=[tk(j), tk(4), "G"], writes=[sck(8 + j)])
            bkr = bank()
            reserved.add(bkr)
            for s0 in (0, 16):
                si, sv = w_slab(w_in, s0, 16, IN_KR, 64)
                P.pe_group([I("matmul", PS[bkr][0:64, :], lhsT=sv[:, kk, :], rhs=Urhs[s0 + kk][0], start=(s0 + kk == 0), stop=(s0 + kk == DC - 1)) for kk in range(16)],
                           reads=[("ring", si)] + [Urhs[s0 + kk][1] for kk in range(16)], writes=[pk(bkr)])
            rope_norm(bkr, G_KR, G_KRS, SENDT[0:64, 4, :], sck(12), G, "G")
            reserved.discard(bkr)
            misc_dma(I("dma_start", out=send_l[s][:, 0:4 * T], in_=SC[:, 4096:4096 + 4 * T]), reads=[sck(8 + j) for j in range(4)], writes=[("send_l", s)])
            misc_dma(I("dma_start", out=send_l[s][0:64, 4 * T:5 * T], in_=SC[0:64, 4096 + 4 * T:4096 + 5 * T]), reads=[sck(12)], writes=[("send_l", s)])

            P.dma("pool", I("collective_compute", "AllGather", ALU.bypass, replica_groups=rg, ins=[send_l[s][:, :]], outs=[recv_l[s][:, :]]),
                  d_ccl, reads=[("send_l", s)], writes=[("recv_l", s)], inc=1)
            P.dma("pool", I("collective_compute", "AllGather", ALU.bypass, replica_groups=rg, ins=[send_t[s][:, :]], outs=[recv_t[s][:, :]]),
                  d_cct, reads=[("send_t", s)], writes=[("recv_t", s)], inc=1)

            gtk = [sck(i) for i in range(4)]
            misc_dma(I("dma_start", out=GT, in_=recv_t[s][:, :].rearrange("(r p) n -> p r n", p=128)), reads=[("recv_t", s)], writes=gtk)
            halo = E[:, :, 0:16]
            P.op("dve", I("tensor_scalar", out=halo, in0=PT[:].rearrange("p (c t) -> p c t", t=16), scalar1=G[:, G_SEL + 4:G_SEL + 5], scalar2=None, op0=ALU.mult),
                 reads=["PT", "G"], writes=EALL)
            for r in range(4):
                P.op("dve", I("scalar_tensor_tensor", out=halo, in0=GT[:, r, :].rearrange("p (c t) -> p c t", t=16), scalar=G[:, G_SEL + r:G_SEL + r + 1], in1=halo,
                              op0=ALU.mult, op1=ALU.add), reads=gtk + ["G"] + EALL, writes=EALL)
            P.op("dve", I("tensor_copy", out=PT[:], in_=GT[:, 3, :]), reads=gtk, writes=["PT"])
            BA = TMPA[:, 0:528]
            BB = TMPA[:, 1024:1552]
            bufs = [(BA, [tk(0), tk(1)]), (BB, [tk(2), tk(3)])]
            for ci in range(16):
                g = ci // 4
                w = 2 << g
                src = E[:, ci, :]
                ek = ekeys(ci)
                cur, curk = src, ek
                d = 1
                bi = 0
                while d < w:
                    dst, dstk = bufs[bi]
                    lo = 2 * d - 1
                    P.op("dve", I("tensor_tensor", out=dst[:, lo:528], in0=cur[:, lo:528], in1=cur[:, lo - d:528 - d], op=ALU.add), reads=curk, writes=dstk)
                    cur, curk = dst, dstk
                    bi ^= 1
                    d *= 2
                P.op("dve", I("scalar_tensor_tensor", out=AMX[:, ci, :], in0=cur[:, 16:528], scalar=1.0 / w, in1=src[:, 16:528], op0=ALU.mult, op1=ALU.subtract),
                     reads=curk + ek, writes=[sck(8 + ci)])
                if s == 0:
                    P.op("dve", I("tensor_tensor", out=TMP[4][:, 0:16], in0=cur[:, 16:32], in1=INVC[:, g * 16:(g + 1) * 16], op=ALU.mult), reads=curk + ["invc"], writes=[tk(4)])
                    P.op("dve", I("tensor_tensor", out=AMX[:, ci, 0:16], in0=TMP[4][:, 0:16], in1=src[:, 16:32], op=ALU.subtract), reads=[tk(4)] + ek, writes=[sck(8 + ci)])
            for g in range(4):
                banks = bank_set()
                si, sv = w_slab(w_pool, g * 4, 4, 0, 512)
                for j in range(4):
                    P.pe_group([I("matmul", PS[banks[j]][:], lhsT=sv[:, i, j * 128:(j + 1) * 128], rhs=AMX[:, 4 * g + i, :], start=(i == 0), stop=(i == 3)) for i in range(4)],
                               reads=[("ring", si)] + [sck(8 + 4 * g + i) for i in range(4)], writes=[pk(banks[j])])
                for j in range(4):
                    co = 4 * g + j
                    P.op("dve", I("tensor_scalar", out=AOUT[:, co, :], in0=PS[banks[j]][:], scalar1=G[:, G_PS + co:G_PS + co + 1], scalar2=None, op0=ALU.mult),
                         reads=[pk(banks[j]), "G"] + EALL, writes=[hk(co // 2)])

            for half in range(2):
                for gi in range(5):
                    pos0 = s * 2048 + gi * 512 if gi < 4 else SEQ
                    ck = ("Kc", s) if gi < 4 else "KcOwn"
                    if gi < 4:
                        src = recv_l[s][gi * 128:(gi + 1) * 128, 0:4 * T]
                        rk = ("recv_l", s)
                        srckr = recv_l[s][gi * 128:gi * 128 + 64, 4 * T:5 * T]
                    else:
                        src = send_l[s][:, 0:4 * T]
                        rk = ("send_l", s)
                        srckr = send_l[s][0:64, 4 * T:5 * T]
                    li = ring_load(lambda slot: [I("dma_start", out=slot[:, 0:4 * T], in_=src)], reads=[rk])
                    lat = RING[li][:, 0:4 * T].rearrange("p (k t) -> p k t", t=T)
                    wsi, wsv = w_slab(w_ukv, 0, 4, half * 2048, 2048)
                    if half == 0:
                        misc_dma(I("dma_start", out=Krc[:, pos0:pos0 + T], in_=srckr), reads=[rk], writes=[ck])
                    for hh in range(8):
                        h = half * 8 + hh
                        b = bank()
                        P.pe_group([I("matmul", PS[b][:], lhsT=wsv[:, kk, hh * 256:hh * 256 + 128], rhs=lat[:, kk, :], start=(kk == 0), stop=(kk == 3)) for kk in range(4)],
                                   reads=[("ring", wsi), ("ring", li)], writes=[pk(b)])
                        reserved.add(b)
                        partition_norm_stats(b, 128, 0, 1)
                        reserved.discard(b)
                        ki = hh % 5
                        kst = PTILE[ki] if ki < 3 else QN[ki - 3]
                        kstk = sck(24 + ki)
                        P.op("dve", I("scalar_tensor_tensor", out=kst, in0=PS[b][:], scalar=G[:, G_KN:G_KN + 1], in1=TMP[1], op0=ALU.mult, op1=ALU.mult),
                             reads=[pk(b), tk(1), "G"], writes=[kstk])
                        P.dma("sp", I("dma_start", out=Kc[h, :, pos0:pos0 + T], in_=kst), d_kst[ki], reads=[kstk], writes=[ck])
                    for tc in range(4):
                        for cg in range(2):
                            b = bank()
                            P.pe_group([I("matmul", PS[b][:], lhsT=lat[:, kk, tc * 128:(tc + 1) * 128],
                                          rhs=wsv[:, kk, :].rearrange("p (h c) -> p h c", c=256)[:, cg * 4:(cg + 1) * 4, 128:256], start=(kk == 0), stop=(kk == 3))
                                        for kk in range(4)], reads=[("ring", wsi), ("ring", li)], writes=[pk(b)])
                            vi = (tc * 2 + cg) % 2
                            vst = QR[vi]
                            vstk = sck(29 + vi)
                            P.op("act", I("copy", out=vst, in_=PS[b][:]), reads=[pk(b)], writes=[vstk])
                            h0 = half * 8 + cg * 4
                            kb = pos0 // 128 + tc
                            P.dma("sp", I("dma_start", out=Vc[h0:h0 + 4, :, kb, :].rearrange("h p d -> p h d"), in_=vst.rearrange("p (h d) -> p h d", d=128)),
                                  d_kst[5 + vi], reads=[vstk], writes=[ck])

            bq = bank()
            reserved.add(bq)

            def evac_q(j, ps, pkey):
                P.op("act", I("copy", out=ZQ[:, j, :], in_=ps[:]), reads=[pkey], writes=[hk(17 + j)])
                t = j % 2
                P.op("act", I("activation", out=TMP[t], in_=ps[:], func=AF.Square), reads=[pkey], writes=[tk(t)])
                P.pe_group([I("matmul", PS[bq][:], lhsT=ONESF[:], rhs=TMP[t], start=(j == 0), stop=(j == 7))], reads=[tk(t), "onesf"], writes=[pk(bq)])
            stream_linear(w_in, 0, Urhs, IN_Q, 1024, evac_q)
            rstd_from(bq, 1024, 2)
            reserved.discard(bq)
            for j in range(8):
                P.op("dve", I("scalar_tensor_tensor", out=CQ[:, j, :], in0=ZQ[:, j, :], scalar=G[:, G_QL + j:G_QL + j + 1], in1=TMP[2], op0=ALU.mult, op1=ALU.mult),
                     reads=[hk(17 + j), tk(2), "G"], writes=[sck(j)])
            cqr = [(CQ[:, j, :], sck(j)) for j in range(8)]

            for h in range(NH):
                qi = h % 2
                si, sv = w_slab(w_uq, 0, 8, h * 192, 192)
                bn = bank()
                P.pe_group([I("matmul", PS[bn][:], lhsT=sv[:, kk, 0:128], rhs=cqr[kk][0], start=(kk == 0), stop=(kk == 7)) for kk in range(8)],
                           reads=[("ring", si)] + [cqr[kk][1] for kk in range(8)], writes=[pk(bn)])
                reserved.add(bn)
                br = bank()
                P.pe_group([I("matmul", PS[br][0:64, :], lhsT=sv[:, kk, 128:192], rhs=cqr[kk][0], start=(kk == 0), stop=(kk == 7)) for kk in range(8)],
                           reads=[("ring", si)] + [cqr[kk][1] for kk in range(8)], writes=[pk(br)])
                reserved.add(br)
                partition_norm_stats(bn, 128, 0, 1)
                P.op("dve", I("scalar_tensor_tensor", out=QN[qi], in0=PS[bn][:], scalar=GS[:, 0:1], in1=TMP[1], op0=ALU.mult, op1=ALU.mult),
                     reads=[pk(bn), tk(1), "GS"], writes=[sck(27 + qi)])
                reserved.discard(bn)
                rope_norm(br, 1, 2, QR[qi][0:64, :], sck(29 + qi), GS, "GS")
                reserved.discard(br)
                chunks = [(ci * 2048, 16, ("Kc", ci), ci == s) for ci in range(s + 1)] + [(SEQ, 4, "KcOwn", None)]
                blocks = []
                for (k0, nkb, ckey, cur) in chunks:
                    nk = nkb * 128
                    holder = {}

                    def loader(holder=holder, k0=k0, nk=nk, nkb=nkb, ckey=ckey, h=h):
                        if "li" not in holder:
                            holder["li"] = ring_load(lambda slot: [
                                I("dma_start", out=slot[:, 0:nk], in_=Kc[h, :, k0:k0 + nk]),
                                I("dma_start", out=slot[:, 2048:2048 + nk].rearrange("p (k d) -> p k d", d=128), in_=Vc[h, :, k0 // 128:k0 // 128 + nkb, :]),
                                I("dma_start", out=slot[0:64, 4096:4096 + nk], in_=Krc[:, k0:k0 + nk]),
                            ], reads=[ckey])
                        return holder["li"]
                    for kb in range(nkb):
                        bias = None
                        tri = None
                        if cur is None:
                            tri = kb
                        elif cur:
                            gi = kb // 4
                            bias = G[:, G_BIAS + gi:G_BIAS + gi + 1]
                        blocks.append(dict(loader=loader, bias=bias, tri=tri, kb=kb))

                def qk_fn(i, bs, blocks=blocks, qi=qi):
                    slot, kb = RING[blocks[i]["loader"]()], blocks[i]["kb"]
                    return [I("matmul", PS[bs][:], lhsT=slot[:, kb * 128:(kb + 1) * 128], rhs=QN[qi], start=True, stop=False),
                            I("matmul", PS[bs][:], lhsT=slot[0:64, 4096 + kb * 128:4096 + (kb + 1) * 128], rhs=QR[qi][0:64, :], start=False, stop=True)]

                def v_fn(i, blocks=blocks):
                    slot, kb = RING[blocks[i]["loader"]()], blocks[i]["kb"]
                    return slot[:, 2048 + kb * 128:2048 + (kb + 1) * 128]
                softmax_pv(blocks, qk_fn, v_fn, BOUT[:, h, :], hk(8 + h // 2), [sck(27 + qi), sck(29 + qi)])
            if s == 0 and debug:
                dump(4, HALL)

            arhs = [(AOUT[:, k, :], hk(k // 2)) for k in range(16)]
            brhs = [(BOUT[:, k, :], hk(8 + k // 2)) for k in range(16)]
            for mb in range(8):
                Bps = {}

                def evac_gp(j, ps, pkey):
                    P.op("act", I("activation", out=TMP[j], in_=ps[:], func=AF.Sigmoid), reads=[pkey], writes=[tk(j)])

                def evac_a(j, ps, pkey):
                    P.op("dve", I("tensor_tensor", out=TMP[j], in0=TMP[j], in1=ps[:], op=ALU.mult), reads=[pkey, tk(j)], writes=[tk(j)])

                def evac_b(j, ps, pkey, Bps=Bps):
                    Bps[j] = (ps, pkey)
                    return True

                def evac_gm(j, ps, pkey, Bps=Bps, mb=mb):
                    t = 4 + j % 2
                    P.op("act", I("activation", out=TMP[t], in_=ps[:], func=AF.Sigmoid), reads=[pkey], writes=[tk(t)])
                    P.op("dve", I("tensor_tensor", out=TMP[t], in0=TMP[t], in1=Bps[j][0][:], op=ALU.mult), reads=[Bps[j][1], tk(t)], writes=[tk(t)])
                    P.op("dve", I("tensor_tensor", out=ACT_[:, 4 * mb + j, :], in0=TMP[j], in1=TMP[t], op=ALU.add), reads=[tk(j), tk(t)], writes=[sck(4 * mb + j)])
                    reserved.discard(Bps[j][1][1])
                stream_linear(w_in, 0, Urhs, IN_GP + mb * 512, 512, evac_gp)
                stream_linear(w_pa, 0, arhs, mb * 512, 512, evac_a)
                stream_linear(w_pb, 0, brhs, mb * 512, 512, evac_b)
                stream_linear(w_in, 0, Urhs, IN_GM + mb * 512, 512, evac_gm)
            if s == 0:
                dumpb(0, SCALL)

            for mb in range(8):
                misc_dma(I("dma_start", out=Hf[:, mb * 2048:(mb + 1) * 2048], in_=hbuf[:, mb * 2048:(mb + 1) * 2048]), reads=["hbuf"], writes=[hk(4 * mb + i) for i in range(4)])
            mrg = [(ACT_[:, k, :], sck(k)) for k in range(DC)]

            def evac_o(m, ps, pkey):
                P.op("dve", I("tensor_tensor", out=H[:, m, :], in0=ps[:], in1=H[:, m, :], op=ALU.add), reads=[pkey, hk(m)], writes=[hk(m)])
            stream_linear(w_out, 0, mrg, 0, D, evac_o)
            if s == 0:
                dump(1, HALL)

            norm_full(G_X)

            def evac_xq(j, ps, pkey):
                partition_norm_stats(pkey[1], 128, 0, 1)
                P.op("dve", I("scalar_tensor_tensor", out=XQT[:, j, :], in0=ps[:], scalar=GS[:, 3:4], in1=TMP[1], op0=ALU.mult, op1=ALU.mult),
                     reads=[pkey, tk(1), "GS"], writes=[sck(j)])
            stream_linear(w_xq, 0, Urhs, 0, 512, evac_xq)
            for hd in range(4):
                blocks = [dict(reads=["XK", "XV"], bias=None, tri=None, mc=mc) for mc in range(2)]

                def qk_fn(i, bs, hd=hd):
                    return [I("matmul", PS[bs][:], lhsT=XK[:, hd, i * 128:(i + 1) * 128], rhs=XQT[:, hd, :], start=True, stop=True)]

                def v_fn(i, hd=hd):
                    return XV[:, i, hd * 128:(hd + 1) * 128]
                softmax_pv(blocks, qk_fn, v_fn, XOT[:, hd, :], sck(4 + hd), [sck(hd)])
            xor = [(XOT[:, k, :], sck(4 + k)) for k in range(4)]
            stream_linear(w_xo, 0, xor, 0, D, evac_o)
            if s == 0:
                dump(2, HALL)

            norm_full(G_F2)
            ffn(w_gu2, w_d2)
            if s == 0:
                dump(3, HALL)

            for tc in range(4):
                ost = XST[tc % 2]
                for cg in range(8):
                    b = bank()
                    P.pe_group([I("transpose", PS[b][:, i * 128:(i + 1) * 128], H[:, 4 * cg + i, tc * 128:(tc + 1) * 128], ident) for i in range(4)],
                               reads=[hk(4 * cg + i) for i in range(4)] + ["cst"], writes=[pk(b)])
                    uk = [sck(16 * (tc % 2) + 2 * cg), sck(16 * (tc % 2) + 2 * cg + 1)]
                    if cg % 2 == 0:
                        P.op("act", I("copy", out=ost[:, cg * 512:(cg + 1) * 512], in_=PS[b][:]), reads=[pk(b)], writes=uk)
                    else:
                        P.op("dve", I("tensor_copy", out=ost[:, cg * 512:(cg + 1) * 512], in_=PS[b][:]), reads=[pk(b)], writes=uk)
                P.dma("sp", I("dma_start", out=out_d[s * T + tc * 128: s * T + (tc + 1) * 128, :], in_=ost), d_o[tc % 2], reads=xstk(tc % 2), writes=["out"])

        P.wait_all("sp", ["out"] + ([("dbg", i) for i in range(NDBG)] + [("dbgb", i) for i in range(NDBG)] if debug else []))
        P.run_block()
        print("instr counts", P.ninstr, "nsem", P.nsem, flush=True)
    return nc


def _fm(v):
    v = np.asarray(v, np.float32).reshape(-1, 128)
    return np.ascontiguousarray(v.T)


def make_in_maps(inputs, NSTEP=4, dff=DFF):
    SEQ = NSTEP * 2048
    x = np.asarray(inputs["x"]); mem = np.asarray(inputs["mem"]); positions = np.asarray(inputs["positions"])
    sq = lambda k: np.ascontiguousarray(np.asarray(inputs[k])[0])
    weights = {k: sq(k) for k in ["ffn1_w_gu", "ffn1_w_down", "ffn2_w_gu", "ffn2_w_down", "w_in", "w_uq", "w_ukv", "w_branch_pool", "w_branch_mla",
                                  "w_out", "w_xq", "w_xkv", "w_xo"]}
    weights["w_pool"] = np.ascontiguousarray(np.asarray(inputs["w_pool"])[0].reshape(2048, 512))
    if dff != DFF:
        for f in ("ffn1", "ffn2"):
            wg = weights[f + "_w_gu"]
            weights[f + "_w_gu"] = np.ascontiguousarray(np.concatenate([wg[:, 0:dff], wg[:, DFF:DFF + dff]], axis=1))
            weights[f + "_w_down"] = np.ascontiguousarray(weights[f + "_w_down"][0:dff])
    gbase = np.zeros((128, NG), np.float32)
    gbase[:, G_F1:G_F1 + 32] = _fm(sq("ffn1_norm")); gbase[:, G_MIX:G_MIX + 32] = _fm(sq("mix_norm"))
    gbase[:, G_X:G_X + 32] = _fm(sq("x_norm")); gbase[:, G_F2:G_F2 + 32] = _fm(sq("ffn2_norm"))
    gbase[:, G_MEM:G_MEM + 32] = _fm(sq("mem_norm")); gbase[:, G_PS:G_PS + 16] = _fm(sq("pool_scale"))
    gbase[:, G_QL:G_QL + 8] = _fm(sq("q_latent_norm")); gbase[:, G_KVL:G_KVL + 4] = _fm(sq("kv_latent_norm"))
    gbase[:, G_QN] = sq("q_nope_norm"); gbase[:, G_KN] = sq("k_nope_norm"); gbase[:, G_XQ] = sq("xq_norm"); gbase[:, G_XK] = sq("xk_norm")
    qr = sq("q_rope_norm"); kr = sq("k_rope_norm")
    gbase[0:64, G_QR] = qr; gbase[0:64, G_QRS] = np.concatenate([qr[32:], qr[:32]])
    gbase[0:64, G_KR] = kr; gbase[0:64, G_KRS] = np.concatenate([kr[32:], kr[:32]])
    half = 32
    inv = (1.0 / (np.float32(10000.0) ** (np.arange(half, dtype=np.float32) * np.float32(2.0 / 64)))).astype(np.float32)
    gbase[0:64, G_INV] = np.concatenate([inv, inv])
    gbase[0:32, G_SGN] = -1.0; gbase[32:64, G_SGN] = 1.0
    cst = np.zeros((128, 192), np.float32)
    cst[:, 0:128] = np.eye(128, dtype=np.float32)
    for m in range(64):
        cst[(m + 32) % 64, 128 + m] = 1.0
    p_ = np.arange(128)[:, None]
    q_ = np.arange(T)[None, :]
    tri = np.concatenate([(j * 128 + p_ <= q_).astype(np.float32) for j in range(4)], axis=1).astype(ml_dtypes.bfloat16)
    in_maps = []
    for core in range(8):
        b, c = core // 4, core % 4
        g = gbase.copy()
        for r in range(4):
            g[:, G_BIAS + r] = 0.0 if r < c else NEG
            g[:, G_SEL + r] = 1.0 if (c >= 1 and r == c - 1) else 0.0
        g[:, G_SEL + 4] = 1.0 if c == 0 else 0.0
        invc = np.zeros((128, 64), np.float32)
        for gg in range(4):
            w = 2 << gg
            t = np.arange(16)
            invc[:, gg * 16:(gg + 1) * 16] = (1.0 / np.minimum(t + 1, w)) if c == 0 else (1.0 / w)
        rows = np.concatenate([np.arange(s * 2048 + c * 512, s * 2048 + c * 512 + 512) for s in range(NSTEP)])
        m = {"x": np.ascontiguousarray(x[b, rows, :]), "pos": np.ascontiguousarray(positions[b, rows].reshape(NSTEP, T).astype(np.int32)),
             "mem": np.ascontiguousarray(mem[b]), "gpack": g, "cst": cst, "tri": tri, "invc": invc}
        m.update(weights)
        in_maps.append(m)
    return in_maps


def assemble(results, NSTEP=4):
    SEQ = NSTEP * 2048
    out = np.zeros((2, SEQ, D), np.float32)
    for core in range(8):
        b, c = core // 4, core % 4
        o = results[core]["out"]
        for s in range(NSTEP):
            out[b, s * 2048 + c * 512: s * 2048 + c * 512 + 512, :] = o[s * T:(s + 1) * T]
    return out


_NC_CACHE = {}


def kernel(**inputs):
    NSTEP = 4
    if NSTEP not in _NC_CACHE:
        _NC_CACHE[NSTEP] = build(NSTEP)
    nc = _NC_CACHE[NSTEP]
    in_maps = make_in_maps(inputs, NSTEP)
    res = run_bass_kernel_spmd(nc, in_maps, core_ids=list(range(8)))
    return assemble(res.results, NSTEP)
```

```python
import numpy as np
import ml_dtypes
import concourse.bass as bass
import concourse.mybir as mybir
from concourse.bass_utils import run_bass_kernel_spmd
from contextlib import ExitStack

F32 = mybir.dt.float32
BF16 = mybir.dt.bfloat16
I32 = mybir.dt.int32
AF = mybir.ActivationFunctionType
ALU = mybir.AluOpType

ENGS = ("pe", "act", "dve", "pool", "sp")

D = 4096
DC = 32
DFF = 11008
T = 512
NH = 16
EPS = 1e-6
IN_POOL, IN_Q, IN_KV, IN_KR, IN_GP, IN_GM = 0, 2048, 3072, 3584, 3648, 3648 + 4096
G_F1, G_MIX, G_X, G_F2, G_MEM, G_PS, G_QL, G_KVL = 0, 32, 64, 96, 128, 160, 176, 184
G_QN, G_KN, G_XQ, G_XK, G_QR, G_QRS, G_KR, G_KRS, G_INV, G_SGN, G_BIAS, G_SEL = 188, 189, 190, 191, 192, 193, 194, 195, 196, 197, 198, 202
NG = 208
NEG = -30000.0
TWO_PI = 6.283185307179586
C1 = 6.28125
C2 = TWO_PI - C1
PI_SAFE = 3.1415925


class Sem:
    __slots__ = ("h", "count", "name")

    def __init__(self, h, name):
        self.h = h
        self.count = 0
        self.name = name


class Prog:
    def __init__(self, nc, es):
        self.nc = nc
        self.es = es
        self.ops = {e: [] for e in ENGS}
        self.prog_sem = {}
        self.waited = {e: {} for e in ENGS}
        self.W = {}
        self.R = {}
        self.nsem = 0
        self.ninstr = {e: 0 for e in ENGS}
        for e in ENGS:
            self.prog_sem[e] = self.new_sem("p_" + e)

    def new_sem(self, name):
        self.nsem += 1
        h = self.es.enter_context(self.nc.semaphore(f"{name}_{self.nsem}"))
        return Sem(h, name)

    def new_epoch(self):
        for e in ENGS:
            self.prog_sem[e] = self.new_sem("p_" + e)

    def _deps(self, eng, reads, writes, skip_sem=None, extra=None):
        need = {}
        if extra:
            for s, v in extra:
                if need.get(s, 0) < v:
                    need[s] = v
        for k in reads:
            w = self.W.get(k)
            if w:
                for s, v in w.items():
                    if need.get(s, 0) < v:
                        need[s] = v
        for k in writes:
            for d in (self.W.get(k), self.R.get(k)):
                if d:
                    for s, v in d.items():
                        if need.get(s, 0) < v:
                            need[s] = v
        wd = self.waited[eng]
        out = []
        for s, v in need.items():
            if s is skip_sem:
                continue
            if wd.get(s, 0) < v:
                wd[s] = v
                out.append((s, v))
        return out

    def _record(self, reads, writes, sem, val):
        for k in reads:
            d = self.R.get(k)
            if d is None:
                d = self.R[k] = {}
            if d.get(sem, 0) < val:
                d[sem] = val
        for k in writes:
            d = self.W.get(k)
            if d is None:
                d = self.W[k] = {}
            if d.get(sem, 0) < val:
                d[sem] = val

    def op(self, eng, ins, reads=(), writes=()):
        ps = self.prog_sem[eng]
        waits = self._deps(eng, reads, writes, skip_sem=ps if eng == "pe" else None)
        ps.count += 1
        self._record(reads, writes, ps, ps.count)
        self.ops[eng].append((waits, ins, ps, 1))
        self.ninstr[eng] += 1

    def pe_group(self, inss, reads=(), writes=()):
        ps = self.prog_sem["pe"]
        waits = self._deps("pe", reads, writes, skip_sem=ps)
        ps.count += 1
        self._record(reads, writes, ps, ps.count)
        lst = self.ops["pe"]
        n = len(inss)
        for i, ins in enumerate(inss):
            lst.append((waits if i == 0 else (), ins, ps if i == n - 1 else None, 1))
        self.ninstr["pe"] += n

    def dma(self, eng, ins, dsem, reads=(), writes=(), inc=16, guard=False):
        extra = [(dsem, dsem.count)] if (guard and dsem.count > 0) else None
        waits = self._deps(eng, reads, writes, extra=extra)
        dsem.count += inc
        self._record(reads, writes, dsem, dsem.count)
        self.ops[eng].append((waits, ins, dsem, inc))
        self.ninstr[eng] += 1

    def wait_all(self, eng, keys):
        waits = self._deps(eng, keys, ())
        self.ops[eng].append((waits, None, None, 0))

    def replay(self, eng, e):
        for waits, ins, sem, inc in self.ops[eng]:
            for s, v in waits:
                e.wait_ge(s.h, v)
            if ins is None:
                continue
            meth, args, kw = ins
            r = getattr(e, meth)(*args, **kw)
            if sem is not None:
                r.then_inc(sem.h, inc)

    def run_block(self):
        nc = self.nc
        with nc.Block() as block:
            @block.tensor
            def _(e):
                self.replay("pe", e)

            @block.scalar
            def _(e):
                self.replay("act", e)

            @block.vector
            def _(e):
                self.replay("dve", e)

            @block.gpsimd
            def _(e):
                self.replay("pool", e)

            @block.sync
            def _(e):
                self.replay("sp", e)


def I(meth, *args, **kw):
    return (meth, args, kw)


def build(NSTEP=4, debug=False, dff=DFF):
    SEQ = NSTEP * 2048
    KTOT = SEQ + 512
    NKB = KTOT // 128
    nc = bass.Bass("TRN2", target_bir_lowering=False)
    din = lambda name, shape, dt=F32: nc.dram_tensor(name, shape, dt, kind="ExternalInput").ap()
    dint = lambda name, shape, dt=F32: nc.dram_tensor(name, shape, dt).ap()
    x_d = din("x", [NSTEP * T, D])
    pos_d = din("pos", [NSTEP, T], I32)
    mem_d = din("mem", [256, D])
    g_d = din("gpack", [128, NG])
    cst_d = din("cst", [128, 192])
    tri_d = din("tri", [128, 4 * T], BF16)
    invc_d = din("invc", [128, 64])
    w_gu1 = din("ffn1_w_gu", [D, 2 * dff]); w_d1 = din("ffn1_w_down", [dff, D])
    w_gu2 = din("ffn2_w_gu", [D, 2 * dff]); w_d2 = din("ffn2_w_down", [dff, D])
    w_in = din("w_in", [D, 11840]); w_pool = din("w_pool", [2048, 512])
    w_uq = din("w_uq", [1024, 3072]); w_ukv = din("w_ukv", [512, 4096])
    w_pa = din("w_branch_pool", [2048, D]); w_pb = din("w_branch_mla", [2048, D])
    w_out = din("w_out", [D, D]); w_xq = din("w_xq", [D, 512]); w_xkv = din("w_xkv", [D, 1024]); w_xo = din("w_xo", [512, D])
    out_d = nc.dram_tensor("out", [NSTEP * T, D], F32, kind="ExternalOutput").ap()
    NDBG = 6
    dbg_d = nc.dram_tensor("dbg", [NDBG, 128, DC * T], F32, kind="ExternalOutput").ap() if debug else None
    dbgb_d = nc.dram_tensor("dbgb", [NDBG, 128, 16384], BF16, kind="ExternalOutput").ap() if debug else None
    hbuf = dint("hbuf", [128, DC * T])
    send_l = [dint(f"send_l{s}", [128, 5 * T], BF16) for s in range(NSTEP)]
    recv_l = [dint(f"recv_l{s}", [4 * 128, 5 * T], BF16) for s in range(NSTEP)]
    send_t = [dint(f"send_t{s}", [128, 256]) for s in range(NSTEP)]
    recv_t = [dint(f"recv_t{s}", [4 * 128, 256]) for s in range(NSTEP)]
    Kc = dint("Kc", [NH, 128, KTOT], BF16)
    Krc = dint("Krc", [64, KTOT], BF16)
    Vc = dint("Vc", [NH, 128, NKB, 128], BF16)

    with ExitStack() as es:
        P = Prog(nc, es)
        sb = lambda name, shape, dt: es.enter_context(nc.sbuf_tensor(name, shape, dt))
        H = sb("H", [128, DC, T], F32)
        U = sb("U", [128, DC, T], BF16)
        SC = sb("SC", [128, 16384], BF16)
        RING = [sb(f"ring{i}", [128, 8192], BF16) for i in range(3)]
        G = sb("G", [128, NG], F32)
        CST = sb("CST", [128, 192], F32)
        ONESF = sb("ONESF", [128, 128], F32)
        ONESB = sb("ONESB", [128, 128], BF16)
        TRI = sb("TRI", [128, 4, T], BF16)
        XK = sb("XK", [128, 4, 256], BF16)
        XV = sb("XV", [128, 2, 512], BF16)
        COSD = sb("COSD", [128, T], F32)
        SIND = sb("SIND", [128, T], F32)
        TMPA = sb("TMPA", [128, 6 * T], F32)
        PT = sb("PT", [128, 256], F32)
        INVC = sb("INVC", [128, 64], F32)
        GS = sb("GS", [128, 4], F32)
        PS = [es.enter_context(nc.psum_tensor(f"ps{i}", [128, T], F32)) for i in range(8)]
        ident = CST[:, 0:128]
        perm = CST[0:64, 128:192]
        TMP = [TMPA[:, i * T:(i + 1) * T] for i in range(6)]
        TMPb = [TMPA[:, i * T:(i + 1) * T].bitcast(BF16)[:, 0:T] for i in range(6)]
        tk = lambda i: ("tmp", i)
        Hf = H[:].rearrange("p a b -> p (a b)")
        E = Hf[:, 0:16 * 528].rearrange("p (c t) -> p c t", t=528)
        AOUT = Hf[:, 0:4096].bitcast(BF16).rearrange("p (c t) -> p c t", t=T)
        BOUT = Hf[:, 4096:8192].bitcast(BF16).rearrange("p (c t) -> p c t", t=T)
        ZQ = H[:, 17:25, :]
        hk = lambda c: ("H", c)
        HALL = [hk(i) for i in range(DC)]

        def ekeys(c):
            lo, hi = c * 2112, (c + 1) * 2112 - 1
            return [hk(i) for i in range(lo // 2048, hi // 2048 + 1)]
        EALL = [hk(i) for i in range(17)]
        SCf = SC[:].bitcast(F32)
        ACT_ = SC[:].rearrange("p (c t) -> p c t", t=T)
        sck = lambda i: ("sc", i)
        SCALL = [sck(i) for i in range(32)]
        CQ = SC[:, 0:4096].rearrange("p (c t) -> p c t", t=T)
        AMX = SC[:, 4096:12288].rearrange("p (c t) -> p c t", t=T)
        SENDT = SC[:, 4096:4096 + 2560].rearrange("p (c t) -> p c t", t=T)
        PTILE = [SC[:, 12288 + i * T: 12288 + (i + 1) * T] for i in range(3)]
        QN = [SC[:, 13824 + i * T: 13824 + (i + 1) * T] for i in range(2)]
        QR = [SC[:, 14848 + i * T: 14848 + (i + 1) * T] for i in range(2)]
        GT = SCf[:, 0:1024].rearrange("p (r n) -> p r n", n=256)
        XQT = SC[:, 0:2048].rearrange("p (c t) -> p c t", t=T)
        XOT = SC[:, 2048:4096].rearrange("p (c t) -> p c t", t=T)
        XST = [SCf[:, i * 4096:(i + 1) * 4096] for i in range(2)]
        xstk = lambda i: [sck(j) for j in range(16 * i, 16 * i + 16)]

        st = {"bank": 0, "set": 0, "ring": 0, "misc": 0}
        reserved = set()
        ring_sem = [P.new_sem(f"ring{i}") for i in range(3)]
        d_miscs = [P.new_sem(f"d_misc{i}") for i in range(10)]
        d_x = [P.new_sem("d_x0"), P.new_sem("d_x1")]
        d_o = [P.new_sem("d_o0"), P.new_sem("d_o1")]
        d_kst = [P.new_sem(f"d_kst{i}") for i in range(7)]
        d_ccl = P.new_sem("d_ccl")
        d_cct = P.new_sem("d_cct")

        def misc_dma(ins, reads=(), writes=(), eng="sp"):
            i = st["misc"]
            st["misc"] = (i + 1) % len(d_miscs)
            P.dma(eng, ins, d_miscs[i], reads=reads, writes=writes, guard=True)

        def bank():
            while True:
                b = st["bank"]
                st["bank"] = (b + 1) % 8
                if b not in reserved:
                    return b

        def bank_set():
            for _ in range(2):
                s_ = st["set"]
                st["set"] ^= 1
                bs = [4 * s_ + j for j in range(4)]
                if not (set(bs) & reserved):
                    return bs
            raise RuntimeError("no free bank set")

        pk = lambda b: ("ps", b)

        def ring_load(inss_fn, reads=()):
            i = st["ring"]
            st["ring"] = (i + 1) % 3
            for ins in inss_fn(RING[i]):
                P.dma("pool", ins, ring_sem[i], reads=reads, writes=[("ring", i)])
            return i

        def w_slab(w_ap, k0, nk, c0, ncols):
            src = w_ap[k0 * 128:(k0 + nk) * 128, c0:c0 + ncols].rearrange("(kc p) n -> p kc n", p=128)
            i = ring_load(lambda slot: [I("dma_start", out=slot[:, 0:nk * ncols].rearrange("p (k n) -> p k n", n=ncols), in_=src)])
            return i, RING[i][:, 0:nk * ncols].rearrange("p (k n) -> p k n", n=ncols)

        def stream_linear(w_ap, k0w, rhs, col0, ncols, evac, ntok=T):
            nk = len(rhs)
            for cb in range(0, ncols, 512):
                nb = min(512, ncols - cb)
                nj = nb // 128
                banks = bank_set()
                reserved.update(banks[0:nj])
                for s0 in range(0, nk, 16):
                    ns = min(16, nk - s0)
                    si, sv = w_slab(w_ap, k0w + s0, ns, col0 + cb, nb)
                    for j in range(nj):
                        b = banks[j]
                        inss = [I("matmul", PS[b][:, 0:ntok], lhsT=sv[:, kk, j * 128:(j + 1) * 128], rhs=rhs[s0 + kk][0],
                                  start=(s0 + kk == 0), stop=(s0 + kk == nk - 1)) for kk in range(ns)]
                        P.pe_group(inss, reads=[("ring", si)] + [rhs[s0 + kk][1] for kk in range(ns)], writes=[pk(b)])
                for j in range(nj):
                    keep = evac(cb // 128 + j, PS[banks[j]], pk(banks[j]))
                    if not keep:
                        reserved.discard(banks[j])

        def rstd_from(ps_b, nfeat, t_out, rows=128, n=T):
            P.op("act", I("activation", out=TMP[t_out][0:rows, 0:n], in_=PS[ps_b][0:rows, 0:n], func=AF.Sqrt, scale=1.0 / nfeat, bias=EPS),
                 reads=[pk(ps_b)], writes=[tk(t_out)])
            P.op("dve", I("reciprocal", out=TMP[t_out][0:rows, 0:n], in_=TMP[t_out][0:rows, 0:n]), reads=[tk(t_out)], writes=[tk(t_out)])

        def norm_full(gcol):
            b = bank()
            reserved.add(b)
            for ci in range(DC):
                t = ci % 2
                P.op("act", I("activation", out=TMPb[t], in_=H[:, ci, :], func=AF.Square), reads=[hk(ci)], writes=[tk(t)])
                P.pe_group([I("matmul", PS[b][:], lhsT=ONESB[:], rhs=TMPb[t], start=(ci == 0), stop=(ci == DC - 1))],
                           reads=[tk(t), "onesb"], writes=[pk(b)])
            rstd_from(b, D, 2)
            reserved.discard(b)
            for ci in range(DC):
                P.op("dve", I("scalar_tensor_tensor", out=U[:, ci, :], in0=H[:, ci, :], scalar=G[:, gcol + ci:gcol + ci + 1], in1=TMP[2],
                              op0=ALU.mult, op1=ALU.mult), reads=[hk(ci), tk(2), "G"], writes=[("U", ci)])

        Urhs = [(U[:, ci, :], ("U", ci)) for ci in range(DC)]

        def ffn(w_gu, w_d):
            nch = dff // 128
            parts = [(i, min(i + 32, nch)) for i in range(0, nch, 32)]
            for (c0, c1) in parts:
                for cb in range(c0 * 128, c1 * 128, 512):
                    nb = min(512, c1 * 128 - cb)

                    def evac_g(j, ps, pkey):
                        P.op("act", I("activation", out=TMP[j], in_=ps[:], func=AF.Silu), reads=[pkey], writes=[tk(j)])

                    def evac_u(j, ps, pkey, cb=cb, c0=c0):
                        a = cb // 128 + j - c0
                        P.op("dve", I("tensor_tensor", out=ACT_[:, a, :], in0=TMP[j], in1=ps[:], op=ALU.mult), reads=[pkey, tk(j)], writes=[sck(a)])
                    stream_linear(w_gu, 0, Urhs, cb, nb, evac_g)
                    stream_linear(w_gu, 0, Urhs, dff + cb, nb, evac_u)
                arhs_ = [(ACT_[:, a, :], sck(a)) for a in range(c1 - c0)]

                def evac_d(m, ps, pkey):
                    P.op("dve", I("scalar_tensor_tensor", out=H[:, m, :], in0=ps[:], scalar=0.5, in1=H[:, m, :], op0=ALU.mult, op1=ALU.add),
                         reads=[pkey, hk(m)], writes=[hk(m)])
                stream_linear(w_d, c0, arhs_, 0, D, evac_d)

        def run(gen):
            for _ in gen:
                pass

        def pns_gen(ps_b, nrows, t_sq, t_r, n=T):
            P.op("act", I("activation", out=TMPb[t_sq][0:nrows, 0:n], in_=PS[ps_b][0:nrows, 0:n], func=AF.Square), reads=[pk(ps_b)], writes=[tk(t_sq)])
            yield
            b2 = bank()
            reserved.add(b2)
            P.pe_group([I("matmul", PS[b2][0:nrows, 0:n], lhsT=ONESB[0:nrows, 0:nrows], rhs=TMPb[t_sq][0:nrows, 0:n], start=True, stop=True)],
                       reads=[tk(t_sq), "onesb"], writes=[pk(b2)])
            yield
            P.op("act", I("activation", out=TMP[t_r][0:nrows, 0:n], in_=PS[b2][0:nrows, 0:n], func=AF.Sqrt, scale=1.0 / nrows, bias=EPS),
                 reads=[pk(b2)], writes=[tk(t_r)])
            reserved.discard(b2)
            yield
            P.op("dve", I("reciprocal", out=TMP[t_r][0:nrows, 0:n], in_=TMP[t_r][0:nrows, 0:n]), reads=[tk(t_r)], writes=[tk(t_r)])
            yield

        def partition_norm_stats(ps_b, nrows, t_sq, t_r, n=T):
            run(pns_gen(ps_b, nrows, t_sq, t_r, n))

        def rope_gen(ps_b, gcol, gscol, out_ap, out_key, gtile, gkey):
            P.op("act", I("copy", out=TMP[0][0:64, :], in_=PS[ps_b][0:64, :]), reads=[pk(ps_b)], writes=[tk(0)])
            yield
            b_sw = bank()
            reserved.add(b_sw)
            P.pe_group([I("matmul", PS[b_sw][0:64, :], lhsT=perm, rhs=TMP[0][0:64, :], start=True, stop=True)], reads=[tk(0), "cst"], writes=[pk(b_sw)])
            yield
            yield from pns_gen(ps_b, 64, 1, 2)
            P.op("dve", I("scalar_tensor_tensor", out=TMP[3][0:64, :], in0=TMP[0][0:64, :], scalar=gtile[0:64, gcol:gcol + 1], in1=COSD[0:64, :],
                          op0=ALU.mult, op1=ALU.mult), reads=[tk(0), "rope", gkey], writes=[tk(3)])
            yield
            P.op("dve", I("scalar_tensor_tensor", out=TMP[4][0:64, :], in0=PS[b_sw][0:64, :], scalar=gtile[0:64, gscol:gscol + 1], in1=SIND[0:64, :],
                          op0=ALU.mult, op1=ALU.mult), reads=[pk(b_sw), "rope", gkey], writes=[tk(4)])
            reserved.discard(b_sw)
            yield
            P.op("dve", I("tensor_tensor", out=TMP[3][0:64, :], in0=TMP[3][0:64, :], in1=TMP[4][0:64, :], op=ALU.add), reads=[tk(3), tk(4)], writes=[tk(3)])
            yield
            P.op("dve", I("tensor_tensor", out=out_ap, in0=TMP[3][0:64, :], in1=TMP[2][0:64, :], op=ALU.mult), reads=[tk(3), tk(2)], writes=[out_key])
            yield

        def rope_norm(ps_b, gcol, gscol, out_ap, out_key, gtile, gkey):
            run(rope_gen(ps_b, gcol, gscol, out_ap, out_key, gtile, gkey))

        def dump(slot, keys):
            if debug:
                misc_dma(I("dma_start", out=dbg_d[slot], in_=Hf), reads=keys, writes=[("dbg", slot)])

        def dumpb(slot, keys):
            if debug:
                misc_dma(I("dma_start", out=dbgb_d[slot], in_=SC[:]), reads=keys, writes=[("dbgb", slot)])

        def softmax_pv(blocks, qk_fn, v_fn, out_ap, out_key, extra_reads, bg=None):
            bo, bd = bank(), bank()
            reserved.add(bo); reserved.add(bd)
            nb_ = len(blocks)
            pend = []
            for i in range(nb_ + 2):
                if i < nb_:
                    blk = blocks[i]
                    if "loader" in blk:
                        blk["reads"] = [("ring", blk["loader"]())]
                    bs = bank()
                    pt = i % 3
                    P.pe_group(qk_fn(i, bs), reads=blk["reads"] + extra_reads, writes=[pk(bs)])
                    if blk["bias"] is not None:
                        P.op("act", I("activation", out=PTILE[pt], in_=PS[bs][:], func=AF.Exp, bias=blk["bias"], scale=1.0), reads=[pk(bs), "G"], writes=[sck(24 + pt)])
                    else:
                        P.op("act", I("activation", out=PTILE[pt], in_=PS[bs][:], func=AF.Exp), reads=[pk(bs)], writes=[sck(24 + pt)])
                    if blk["tri"] is not None:
                        P.op("dve", I("tensor_tensor", out=PTILE[pt], in0=PTILE[pt], in1=TRI[:, blk["tri"], :], op=ALU.mult),
                             reads=[sck(24 + pt), "tri"], writes=[sck(24 + pt)])
                if len(pend) >= 2 or (i >= nb_ and pend):
                    pi, ppt = pend.pop(0)
                    P.pe_group([I("matmul", PS[bo][:], lhsT=v_fn(pi), rhs=PTILE[ppt], start=(pi == 0), stop=(pi == nb_ - 1)),
                                I("matmul", PS[bd][:], lhsT=ONESB[:], rhs=PTILE[ppt], start=(pi == 0), stop=(pi == nb_ - 1))],
                               reads=blocks[pi]["reads"] + [sck(24 + ppt), "onesb"], writes=[pk(bo), pk(bd)])
                if i < nb_:
                    pend.append((i, pt))
                if bg is not None:
                    next(bg, None)
            assert not pend
            P.op("dve", I("reciprocal", out=TMP[5], in_=PS[bd][:]), reads=[pk(bd)], writes=[tk(5)])
            P.op("dve", I("tensor_tensor", out=out_ap, in0=PS[bo][:], in1=TMP[5], op=ALU.mult), reads=[pk(bo), tk(5)], writes=[out_key])
            reserved.discard(bo); reserved.discard(bd)

        misc_dma(I("dma_start", out=G[:], in_=g_d[:, :]), writes=["G"])
        misc_dma(I("dma_start", out=CST[:], in_=cst_d[:, :]), writes=["cst"])
        misc_dma(I("dma_start", out=TRI[:].rearrange("p a b -> p (a b)"), in_=tri_d[:, :]), writes=["tri"])
        misc_dma(I("dma_start", out=INVC[:], in_=invc_d[:, :]), writes=["invc"])
        P.op("dve", I("memset", ONESF[:], 1.0), writes=["onesf"])
        P.op("dve", I("memset", ONESB[:], 1.0), writes=["onesb"])
        P.op("dve", I("memset", PT[:], 0.0), writes=["PT"])
        sc_q = 192.0 ** -0.5
        P.op("dve", I("tensor_scalar", out=GS[:, 0:1], in0=G[:, G_QN:G_QN + 1], scalar1=sc_q, scalar2=None, op0=ALU.mult), reads=["G"], writes=["GS"])
        P.op("dve", I("tensor_scalar", out=GS[:, 1:3], in0=G[:, G_QR:G_QR + 2], scalar1=sc_q, scalar2=None, op0=ALU.mult), reads=["G"], writes=["GS"])
        P.op("dve", I("tensor_scalar", out=GS[:, 3:4], in0=G[:, G_XQ:G_XQ + 1], scalar1=128.0 ** -0.5, scalar2=None, op0=ALU.mult), reads=["G"], writes=["GS"])

        MNT = SC[:, 0:8192].rearrange("p (c t) -> p c t", t=256)
        MST = SCf[:, 4096:8192].rearrange("p (a f) -> p a f", f=2048)
        mstk = [sck(j) for j in range(16, 32)]
        bss = bank()
        reserved.add(bss)
        for pas in range(2):
            for fg in range(2):
                P.dma("sp", I("dma_start", out=MST, in_=mem_d[:, fg * 2048:(fg + 1) * 2048].rearrange("(a p) f -> p a f", p=128)), d_x[0], writes=mstk)
                for cl in range(16):
                    ci = fg * 16 + cl
                    b = bank()
                    P.pe_group([I("transpose", PS[b][:, a * 128:(a + 1) * 128], MST[:, a, cl * 128:(cl + 1) * 128], ident) for a in range(2)],
                               reads=mstk + ["cst"], writes=[pk(b)])
                    if pas == 0:
                        t = ci % 2
                        P.op("act", I("activation", out=TMPb[t][:, 0:256], in_=PS[b][:, 0:256], func=AF.Square), reads=[pk(b)], writes=[tk(t)])
                        P.pe_group([I("matmul", PS[bss][:, 0:256], lhsT=ONESB[:], rhs=TMPb[t][:, 0:256], start=(ci == 0), stop=(ci == DC - 1))],
                                   reads=[tk(t), "onesb"], writes=[pk(bss)])
                    else:
                        P.op("dve", I("scalar_tensor_tensor", out=MNT[:, ci, :], in0=PS[b][:, 0:256], scalar=G[:, G_MEM + ci:G_MEM + ci + 1],
                                      in1=TMP[2][:, 0:256], op0=ALU.mult, op1=ALU.mult), reads=[pk(b), tk(2), "G"], writes=[sck(ci // 2)])
            if pas == 0:
                rstd_from(bss, D, 2, n=256)
                reserved.discard(bss)
        mrhs = [(MNT[:, ci, :], sck(ci // 2)) for ci in range(DC)]

        def evac_xk(j, ps, pkey):
            partition_norm_stats(pkey[1], 128, 0, 1, n=256)
            P.op("dve", I("scalar_tensor_tensor", out=XK[:, j, :], in0=ps[:, 0:256], scalar=G[:, G_XK:G_XK + 1], in1=TMP[1][:, 0:256],
                          op0=ALU.mult, op1=ALU.mult), reads=[pkey, tk(1), "G"], writes=["XK"])
        stream_linear(w_xkv, 0, mrhs, 0, 512, evac_xk, ntok=256)
        banks = bank_set()
        for s0 in (0, 16):
            si, sv = w_slab(w_xkv, s0, 16, 512, 512)
            for a in range(2):
                P.pe_group([I("matmul", PS[banks[a]][:], lhsT=mrhs[s0 + kk][0][:, a * 128:(a + 1) * 128], rhs=sv[:, kk, :],
                              start=(s0 + kk == 0), stop=(s0 + kk == DC - 1)) for kk in range(16)],
                           reads=[("ring", si)] + [mrhs[s0 + kk][1] for kk in range(16)], writes=[pk(banks[a])])
        for a in range(2):
            P.op("act", I("copy", out=XV[:, a, :], in_=PS[banks[a]][:]), reads=[pk(banks[a])], writes=["XV"])

        rg = [[0, 1, 2, 3], [4, 5, 6, 7]]
        for s in range(NSTEP):
            P.new_epoch()
            for tc in range(4):
                xs = XST[tc % 2]
                P.dma("sp", I("dma_start", out=xs, in_=x_d[s * T + tc * 128: s * T + (tc + 1) * 128, :]), d_x[tc % 2], writes=xstk(tc % 2))
                for cg in range(8):
                    b = bank()
                    P.pe_group([I("transpose", PS[b][:, i * 128:(i + 1) * 128], xs[:, (4 * cg + i) * 128:(4 * cg + i + 1) * 128], ident) for i in range(4)],
                               reads=xstk(tc % 2) + ["cst"], writes=[pk(b)])
                    dst = H[:, 4 * cg:4 * cg + 4, tc * 128:(tc + 1) * 128]
                    srcv = PS[b][:].rearrange("p (i t) -> p i t", t=128)
                    if cg % 2 == 0:
                        P.op("act", I("copy", out=dst, in_=srcv), reads=[pk(b)], writes=[hk(4 * cg + i) for i in range(4)])
                    else:
                        P.op("dve", I("tensor_copy", out=dst, in_=srcv), reads=[pk(b)], writes=[hk(4 * cg + i) for i in range(4)])
            POSI = TMPA[:, 5 * T:6 * T].bitcast(I32)
            KI = TMPA[:, 4 * T:5 * T].bitcast(I32)
            misc_dma(I("dma_start", out=POSI[0:64, :], in_=pos_d[s].partition_broadcast(64)), writes=[tk(5)])
            P.op("dve", I("tensor_copy", out=TMP[0][0:64, :], in_=POSI[0:64, :]), reads=[tk(5)], writes=[tk(0)])
            P.op("dve", I("tensor_scalar", out=TMP[0][0:64, :], in0=TMP[0][0:64, :], scalar1=G[0:64, G_INV:G_INV + 1], scalar2=None, op0=ALU.mult),
                 reads=[tk(0), "G"], writes=[tk(0)])
            t1, t2 = TMP[1][0:64, :], TMP[2][0:64, :]
            for which, dst in ((0, SIND), (1, COSD)):
                shift = 0.0 if which == 0 else float(np.pi / 2)
                P.op("dve", I("tensor_scalar", out=t1, in0=TMP[0][0:64, :], scalar1=shift, scalar2=None, op0=ALU.add), reads=[tk(0)], writes=[tk(1)])
                P.op("dve", I("tensor_scalar", out=KI[0:64, :], in0=t1, scalar1=1.0 / TWO_PI, scalar2=None, op0=ALU.mult), reads=[tk(1)], writes=[tk(4)])
                P.op("dve", I("tensor_copy", out=t2, in_=KI[0:64, :]), reads=[tk(4)], writes=[tk(2)])
                P.op("dve", I("scalar_tensor_tensor", out=t1, in0=t2, scalar=-C1, in1=t1, op0=ALU.mult, op1=ALU.add), reads=[tk(1), tk(2)], writes=[tk(1)])
                P.op("dve", I("scalar_tensor_tensor", out=t1, in0=t2, scalar=-C2, in1=t1, op0=ALU.mult, op1=ALU.add), reads=[tk(1), tk(2)], writes=[tk(1)])
                P.op("dve", I("tensor_scalar", out=t2, in0=t1, scalar1=float(np.pi), scalar2=-TWO_PI, op0=ALU.is_gt, op1=ALU.mult), reads=[tk(1)], writes=[tk(2)])
                P.op("dve", I("tensor_tensor", out=t1, in0=t1, in1=t2, op=ALU.add), reads=[tk(1), tk(2)], writes=[tk(1)])
                P.op("dve", I("tensor_scalar", out=t2, in0=t1, scalar1=float(-np.pi), scalar2=TWO_PI, op0=ALU.is_lt, op1=ALU.mult), reads=[tk(1)], writes=[tk(2)])
                P.op("dve", I("tensor_tensor", out=t1, in0=t1, in1=t2, op=ALU.add), reads=[tk(1), tk(2)], writes=[tk(1)])
                P.op("dve", I("tensor_scalar", out=t1, in0=t1, scalar1=-PI_SAFE, scalar2=PI_SAFE, op0=ALU.max, op1=ALU.min), reads=[tk(1)], writes=[tk(1)])
                P.op("act", I("activation", out=dst[0:64, :], in_=t1, func=AF.Sin), reads=[tk(1)], writes=["rope"])
            P.op("dve", I("tensor_scalar", out=SIND[0:64, :], in0=SIND[0:64, :], scalar1=G[0:64, G_SGN:G_SGN + 1], scalar2=None, op0=ALU.mult),
                 reads=["rope", "G"], writes=["rope"])

            norm_full(G_F1)
            ffn(w_gu1, w_d1)
            if s == 0:
                dump(0, HALL)
            norm_full(G_MIX)
            misc_dma(I("dma_start", out=hbuf[:, :], in_=Hf), reads=HALL, writes=["hbuf"])

            def evac_pool(j, ps, pkey):
                P.op("act", I("copy", out=E[:, j, 16:528], in_=ps[:]), reads=[pkey], writes=ekeys(j))
            stream_linear(w_in, 0, Urhs, IN_POOL, 2048, evac_pool)
            misc_dma(I("dma_start", out=send_t[s][:, :].rearrange("p (c t) -> p c t", t=16), in_=E[:, :, 512:528]), reads=EALL, writes=[("send_t", s)])
            bkv = bank()
            reserved.add(bkv)

            def evac_kv(j, ps, pkey):
                P.op("act", I("copy", out=TMP[j], in_=ps[:]), reads=[pkey], writes=[tk(j)])
                t = 4 + j % 2
                P.op("act", I("activation", out=TMPb[t], in_=ps[:], func=AF.Square), reads=[pkey], writes=[tk(t)])
                P.pe_group([I("matmul", PS[bkv][:], lhsT=ONESB[:], rhs=TMPb[t], start=(j == 0), stop=(j == 3))], reads=[tk(t), "onesb"], writes=[pk(bkv)])
            stream_linear(w_in, 0, Urhs, IN_KV, 512, evac_kv)
            rstd_from(bkv, 512, 4)
            reserved.discard(bkv)
            for j in range(4):
                P.op("dve", I("scalar_tensor_tensor", out=SENDT[:, j, :], in0=TMP[j], scalar=G[:, G_KVL + j:G_KVL + j + 1], in1=TMP[4], op0=ALU.mult, op1=ALU.mult),
                     reads=[tk(j), tk(4), "G"], writes=[sck(8 + j)])
            bkr = bank()
            reserved.add(bkr)
            for s0 in (0, 16):
                si, sv = w_slab(w_in, s0, 16, IN_KR, 64)
                P.pe_group([I("matmul", PS[bkr][0:64, :], lhsT=sv[:, kk, :], rhs=Urhs[s0 + kk][0], start=(s0 + kk == 0), stop=(s0 + kk == DC - 1)) for kk in range(16)],
                           reads=[("ring", si)] + [Urhs[s0 + kk][1] for kk in range(16)], writes=[pk(bkr)])
            rope_norm(bkr, G_KR, G_KRS, SENDT[0:64, 4, :], sck(12), G, "G")
            reserved.discard(bkr)
            misc_dma(I("dma_start", out=send_l[s][:, 0:4 * T], in_=SC[:, 4096:4096 + 4 * T]), reads=[sck(8 + j) for j in range(4)], writes=[("send_l", s)])
            misc_dma(I("dma_start", out=send_l[s][0:64, 4 * T:5 * T], in_=SC[0:64, 4096 + 4 * T:4096 + 5 * T]), reads=[sck(12)], writes=[("send_l", s)])

            P.dma("pool", I("collective_compute", "AllGather", ALU.bypass, replica_groups=rg, ins=[send_l[s][:, :]], outs=[recv_l[s][:, :]]),
                  d_ccl, reads=[("send_l", s)], writes=[("recv_l", s)], inc=1)
            P.dma("pool", I("collective_compute", "AllGather", ALU.bypass, replica_groups=rg, ins=[send_t[s][:, :]], outs=[recv_t[s][:, :]]),
                  d_cct, reads=[("send_t", s)], writes=[("recv_t", s)], inc=1)

            gtk = [sck(i) for i in range(4)]
            misc_dma(I("dma_start", out=GT, in_=recv_t[s][:, :].rearrange("(r p) n -> p r n", p=128)), reads=[("recv_t", s)], writes=gtk)
            halo = E[:, :, 0:16]
            P.op("dve", I("tensor_scalar", out=halo, in0=PT[:].rearrange("p (c t) -> p c t", t=16), scalar1=G[:, G_SEL + 4:G_SEL + 5], scalar2=None, op0=ALU.mult),
                 reads=["PT", "G"], writes=EALL)
            for r in range(4):
                P.op("dve", I("scalar_tensor_tensor", out=halo, in0=GT[:, r, :].rearrange("p (c t) -> p c t", t=16), scalar=G[:, G_SEL + r:G_SEL + r + 1], in1=halo,
                              op0=ALU.mult, op1=ALU.add), reads=gtk + ["G"] + EALL, writes=EALL)
            P.op("dve", I("tensor_copy", out=PT[:], in_=GT[:, 3, :]), reads=gtk, writes=["PT"])
            BA = TMPA[:, 0:528]
            BB = TMPA[:, 1024:1552]
            bufs = [(BA, [tk(0), tk(1)]), (BB, [tk(2), tk(3)])]
            for ci in range(16):
                g = ci // 4
                w = 2 << g
                src = E[:, ci, :]
                ek = ekeys(ci)
                cur, curk = src, ek
                d = 1
                bi = 0
                while d < w:
                    dst, dstk = bufs[bi]
                    lo = 2 * d - 1
                    P.op("dve", I("tensor_tensor", out=dst[:, lo:528], in0=cur[:, lo:528], in1=cur[:, lo - d:528 - d], op=ALU.add), reads=curk, writes=dstk)
                    cur, curk = dst, dstk
                    bi ^= 1
                    d *= 2
                P.op("dve", I("scalar_tensor_tensor", out=AMX[:, ci, :], in0=cur[:, 16:528], scalar=1.0 / w, in1=src[:, 16:528], op0=ALU.mult, op1=ALU.subtract),
                     reads=curk + ek, writes=[sck(8 + ci)])
                if s == 0:
                    P.op("dve", I("tensor_tensor", out=TMP[4][:, 0:16], in0=cur[:, 16:32], in1=INVC[:, g * 16:(g + 1) * 16], op=ALU.mult), reads=curk + ["invc"], writes=[tk(4)])
                    P.op("dve", I("tensor_tensor", out=AMX[:, ci, 0:16], in0=TMP[4][:, 0:16], in1=src[:, 16:32], op=ALU.subtract), reads=[tk(4)] + ek, writes=[sck(8 + ci)])
            for g in range(4):
                banks = bank_set()
                si, sv = w_slab(w_pool, g * 4, 4, 0, 512)
                for j in range(4):
                    P.pe_group([I("matmul", PS[banks[j]][:], lhsT=sv[:, i, j * 128:(j + 1) * 128], rhs=AMX[:, 4 * g + i, :], start=(i == 0), stop=(i == 3)) for i in range(4)],
                               reads=[("ring", si)] + [sck(8 + 4 * g + i) for i in range(4)], writes=[pk(banks[j])])
                for j in range(4):
                    co = 4 * g + j
                    P.op("dve", I("tensor_scalar", out=AOUT[:, co, :], in0=PS[banks[j]][:], scalar1=G[:, G_PS + co:G_PS + co + 1], scalar2=None, op0=ALU.mult),
                         reads=[pk(banks[j]), "G"] + EALL, writes=[hk(co // 2)])

            for half in range(2):
                for gi in range(5):
                    pos0 = s * 2048 + gi * 512 if gi < 4 else SEQ
                    ck = ("Kc", s) if gi < 4 else "KcOwn"
                    if gi < 4:
                        src = recv_l[s][gi * 128:(gi + 1) * 128, 0:4 * T]
                        rk = ("recv_l", s)
                        srckr = recv_l[s][gi * 128:gi * 128 + 64, 4 * T:5 * T]
                    else:
                        src = send_l[s][:, 0:4 * T]
                        rk = ("send_l", s)
                        srckr = send_l[s][0:64, 4 * T:5 * T]
                    li = ring_load(lambda slot: [I("dma_start", out=slot[:, 0:4 * T], in_=src)], reads=[rk])
                    lat = RING[li][:, 0:4 * T].rearrange("p (k t) -> p k t", t=T)
                    wsi, wsv = w_slab(w_ukv, 0, 4, half * 2048, 2048)
                    if half == 0:
                        misc_dma(I("dma_start", out=Krc[:, pos0:pos0 + T], in_=srckr), reads=[rk], writes=[ck])
                    for hh in range(8):
                        h = half * 8 + hh
                        b = bank()
                        P.pe_group([I("matmul", PS[b][:], lhsT=wsv[:, kk, hh * 256:hh * 256 + 128], rhs=lat[:, kk, :], start=(kk == 0), stop=(kk == 3)) for kk in range(4)],
                                   reads=[("ring", wsi), ("ring", li)], writes=[pk(b)])
                        reserved.add(b)
                        tq, tr_ = (0, 1) if hh % 2 == 0 else (2, 3)
                        partition_norm_stats(b, 128, tq, tr_)
                        reserved.discard(b)
                        ki = hh % 5
                        kst = PTILE[ki] if ki < 3 else QN[ki - 3]
                        kstk = sck(24 + ki)
                        P.op("dve", I("scalar_tensor_tensor", out=kst, in0=PS[b][:], scalar=G[:, G_KN:G_KN + 1], in1=TMP[tr_], op0=ALU.mult, op1=ALU.mult),
                             reads=[pk(b), tk(tr_), "G"], writes=[kstk])
                        P.dma("sp", I("dma_start", out=Kc[h, :, pos0:pos0 + T], in_=kst), d_kst[ki], reads=[kstk], writes=[ck])
                    for tc in range(4):
                        for cg in range(2):
                            b = bank()
                            P.pe_group([I("matmul", PS[b][:], lhsT=lat[:, kk, tc * 128:(tc + 1) * 128],
                                          rhs=wsv[:, kk, :].rearrange("p (h c) -> p h c", c=256)[:, cg * 4:(cg + 1) * 4, 128:256], start=(kk == 0), stop=(kk == 3))
                                        for kk in range(4)], reads=[("ring", wsi), ("ring", li)], writes=[pk(b)])
                            vi = (tc * 2 + cg) % 2
                            vst = QR[vi]
                            vstk = sck(29 + vi)
                            P.op("act", I("copy", out=vst, in_=PS[b][:]), reads=[pk(b)], writes=[vstk])
                            h0 = half * 8 + cg * 4
                            kb = pos0 // 128 + tc
                            P.dma("sp", I("dma_start", out=Vc[h0:h0 + 4, :, kb, :].rearrange("h p d -> p h d"), in_=vst.rearrange("p (h d) -> p h d", d=128)),
                                  d_kst[5 + vi], reads=[vstk], writes=[ck])

            bq = bank()
            reserved.add(bq)

            def evac_q(j, ps, pkey):
                P.op("act", I("copy", out=ZQ[:, j, :], in_=ps[:]), reads=[pkey], writes=[hk(17 + j)])
                t = j % 2
                P.op("act", I("activation", out=TMPb[t], in_=ps[:], func=AF.Square), reads=[pkey], writes=[tk(t)])
                P.pe_group([I("matmul", PS[bq][:], lhsT=ONESB[:], rhs=TMPb[t], start=(j == 0), stop=(j == 7))], reads=[tk(t), "onesb"], writes=[pk(bq)])
            stream_linear(w_in, 0, Urhs, IN_Q, 1024, evac_q)
            rstd_from(bq, 1024, 2)
            reserved.discard(bq)
            for j in range(8):
                P.op("dve", I("scalar_tensor_tensor", out=CQ[:, j, :], in0=ZQ[:, j, :], scalar=G[:, G_QL + j:G_QL + j + 1], in1=TMP[2], op0=ALU.mult, op1=ALU.mult),
                     reads=[hk(17 + j), tk(2), "G"], writes=[sck(j)])
            cqr = [(CQ[:, j, :], sck(j)) for j in range(8)]

            def prep_q(h):
                qi = h % 2
                si, sv = w_slab(w_uq, 0, 8, h * 192, 192)
                bn = bank()
                reserved.add(bn)
                P.pe_group([I("matmul", PS[bn][:], lhsT=sv[:, kk, 0:128], rhs=cqr[kk][0], start=(kk == 0), stop=(kk == 7)) for kk in range(8)],
                           reads=[("ring", si)] + [cqr[kk][1] for kk in range(8)], writes=[pk(bn)])
                br = bank()
                reserved.add(br)
                P.pe_group([I("matmul", PS[br][0:64, :], lhsT=sv[:, kk, 128:192], rhs=cqr[kk][0], start=(kk == 0), stop=(kk == 7)) for kk in range(8)],
                           reads=[("ring", si)] + [cqr[kk][1] for kk in range(8)], writes=[pk(br)])
                yield
                yield from pns_gen(bn, 128, 0, 1)
                P.op("dve", I("scalar_tensor_tensor", out=QN[qi], in0=PS[bn][:], scalar=GS[:, 0:1], in1=TMP[1], op0=ALU.mult, op1=ALU.mult),
                     reads=[pk(bn), tk(1), "GS"], writes=[sck(27 + qi)])
                reserved.discard(bn)
                yield
                yield from rope_gen(br, 1, 2, QR[qi][0:64, :], sck(29 + qi), GS, "GS")
                reserved.discard(br)

            run(prep_q(0))
            for h in range(NH):
                qi = h % 2
                bgq = prep_q(h + 1) if h + 1 < NH else None
                chunks = [(ci * 2048, 16, ("Kc", ci), ci == s) for ci in range(s + 1)] + [(SEQ, 4, "KcOwn", None)]
                blocks = []
                for (k0, nkb, ckey, cur) in chunks:
                    nk = nkb * 128
                    holder = {}

                    def loader(holder=holder, k0=k0, nk=nk, nkb=nkb, ckey=ckey, h=h):
                        if "li" not in holder:
                            holder["li"] = ring_load(lambda slot: [
                                I("dma_start", out=slot[:, 0:nk], in_=Kc[h, :, k0:k0 + nk]),
                                I("dma_start", out=slot[:, 2048:2048 + nk].rearrange("p (k d) -> p k d", d=128), in_=Vc[h, :, k0 // 128:k0 // 128 + nkb, :]),
                                I("dma_start", out=slot[0:64, 4096:4096 + nk], in_=Krc[:, k0:k0 + nk]),
                            ], reads=[ckey])
                        return holder["li"]
                    for kb in range(nkb):
                        bias = None
                        tri = None
                        if cur is None:
                            tri = kb
                        elif cur:
                            gi = kb // 4
                            bias = G[:, G_BIAS + gi:G_BIAS + gi + 1]
                        blocks.append(dict(loader=loader, bias=bias, tri=tri, kb=kb))

                def qk_fn(i, bs, blocks=blocks, qi=qi):
                    slot, kb = RING[blocks[i]["loader"]()], blocks[i]["kb"]
                    return [I("matmul", PS[bs][:], lhsT=slot[:, kb * 128:(kb + 1) * 128], rhs=QN[qi], start=True, stop=False),
                            I("matmul", PS[bs][:], lhsT=slot[0:64, 4096 + kb * 128:4096 + (kb + 1) * 128], rhs=QR[qi][0:64, :], start=False, stop=True)]

                def v_fn(i, blocks=blocks):
                    slot, kb = RING[blocks[i]["loader"]()], blocks[i]["kb"]
                    return slot[:, 2048 + kb * 128:2048 + (kb + 1) * 128]
                softmax_pv(blocks, qk_fn, v_fn, BOUT[:, h, :], hk(8 + h // 2), [sck(27 + qi), sck(29 + qi)], bg=bgq)
                if bgq is not None:
                    run(bgq)
            if s == 0 and debug:
                dump(4, HALL)

            arhs = [(AOUT[:, k, :], hk(k // 2)) for k in range(16)]
            brhs = [(BOUT[:, k, :], hk(8 + k // 2)) for k in range(16)]
            for mb in range(8):
                Bps = {}

                def evac_gp(j, ps, pkey):
                    P.op("act", I("activation", out=TMP[j], in_=ps[:], func=AF.Sigmoid), reads=[pkey], writes=[tk(j)])

                def evac_a(j, ps, pkey):
                    P.op("dve", I("tensor_tensor", out=TMP[j], in0=TMP[j], in1=ps[:], op=ALU.mult), reads=[pkey, tk(j)], writes=[tk(j)])

                def evac_b(j, ps, pkey, Bps=Bps):
                    Bps[j] = (ps, pkey)
                    return True

                def evac_gm(j, ps, pkey, Bps=Bps, mb=mb):
                    t = 4 + j % 2
                    P.op("act", I("activation", out=TMP[t], in_=ps[:], func=AF.Sigmoid), reads=[pkey], writes=[tk(t)])
                    P.op("dve", I("tensor_tensor", out=TMP[t], in0=TMP[t], in1=Bps[j][0][:], op=ALU.mult), reads=[Bps[j][1], tk(t)], writes=[tk(t)])
                    P.op("dve", I("tensor_tensor", out=ACT_[:, 4 * mb + j, :], in0=TMP[j], in1=TMP[t], op=ALU.add), reads=[tk(j), tk(t)], writes=[sck(4 * mb + j)])
                    reserved.discard(Bps[j][1][1])
                stream_linear(w_in, 0, Urhs, IN_GP + mb * 512, 512, evac_gp)
                stream_linear(w_pa, 0, arhs, mb * 512, 512, evac_a)
                stream_linear(w_pb, 0, brhs, mb * 512, 512, evac_b)
                stream_linear(w_in, 0, Urhs, IN_GM + mb * 512, 512, evac_gm)
            if s == 0:
                dumpb(0, SCALL)

            for mb in range(8):
                misc_dma(I("dma_start", out=Hf[:, mb * 2048:(mb + 1) * 2048], in_=hbuf[:, mb * 2048:(mb + 1) * 2048]), reads=["hbuf"], writes=[hk(4 * mb + i) for i in range(4)])
            mrg = [(ACT_[:, k, :], sck(k)) for k in range(DC)]

            def evac_o(m, ps, pkey):
                P.op("dve", I("tensor_tensor", out=H[:, m, :], in0=ps[:], in1=H[:, m, :], op=ALU.add), reads=[pkey, hk(m)], writes=[hk(m)])
            stream_linear(w_out, 0, mrg, 0, D, evac_o)
            if s == 0:
                dump(1, HALL)

            norm_full(G_X)

            def evac_xq(j, ps, pkey):
                partition_norm_stats(pkey[1], 128, 0, 1)
                P.op("dve", I("scalar_tensor_tensor", out=XQT[:, j, :], in0=ps[:], scalar=GS[:, 3:4], in1=TMP[1], op0=ALU.mult, op1=ALU.mult),
                     reads=[pkey, tk(1), "GS"], writes=[sck(j)])
            stream_linear(w_xq, 0, Urhs, 0, 512, evac_xq)
            for hd in range(4):
                blocks = [dict(reads=["XK", "XV"], bias=None, tri=None, mc=mc) for mc in range(2)]

                def qk_fn(i, bs, hd=hd):
                    return [I("matmul", PS[bs][:], lhsT=XK[:, hd, i * 128:(i + 1) * 128], rhs=XQT[:, hd, :], start=True, stop=True)]

                def v_fn(i, hd=hd):
                    return XV[:, i, hd * 128:(hd + 1) * 128]
                softmax_pv(blocks, qk_fn, v_fn, XOT[:, hd, :], sck(4 + hd), [sck(hd)])
            xor = [(XOT[:, k, :], sck(4 + k)) for k in range(4)]
            stream_linear(w_xo, 0, xor, 0, D, evac_o)
            if s == 0:
                dump(2, HALL)

            norm_full(G_F2)
            ffn(w_gu2, w_d2)
            if s == 0:
                dump(3, HALL)

            for tc in range(4):
                ost = XST[tc % 2]
                for cg in range(8):
                    b = bank()
                    P.pe_group([I("transpose", PS[b][:, i * 128:(i + 1) * 128], H[:, 4 * cg + i, tc * 128:(tc + 1) * 128], ident) for i in range(4)],
                               reads=[hk(4 * cg + i) for i in range(4)] + ["cst"], writes=[pk(b)])
                    uk = [sck(16 * (tc % 2) + 2 * cg), sck(16 * (tc % 2) + 2 * cg + 1)]
                    if cg % 2 == 0:
                        P.op("act", I("copy", out=ost[:, cg * 512:(cg + 1) * 512], in_=PS[b][:]), reads=[pk(b)], writes=uk)
                    else:
                        P.op("dve", I("tensor_copy", out=ost[:, cg * 512:(cg + 1) * 512], in_=PS[b][:]), reads=[pk(b)], writes=uk)
                P.dma("sp", I("dma_start", out=out_d[s * T + tc * 128: s * T + (tc + 1) * 128, :], in_=ost), d_o[tc % 2], reads=xstk(tc % 2), writes=["out"])

        P.wait_all("sp", ["out"] + ([("dbg", i) for i in range(NDBG)] + [("dbgb", i) for i in range(NDBG)] if debug else []))
        P.run_block()
        print("instr counts", P.ninstr, "nsem", P.nsem, flush=True)
    return nc


def _fm(v):
    v = np.asarray(v, np.float32).reshape(-1, 128)
    return np.ascontiguousarray(v.T)


def make_in_maps(inputs, NSTEP=4, dff=DFF):
    SEQ = NSTEP * 2048
    x = np.asarray(inputs["x"]); mem = np.asarray(inputs["mem"]); positions = np.asarray(inputs["positions"])
    sq = lambda k: np.ascontiguousarray(np.asarray(inputs[k])[0])
    weights = {k: sq(k) for k in ["ffn1_w_gu", "ffn1_w_down", "ffn2_w_gu", "ffn2_w_down", "w_in", "w_uq", "w_ukv", "w_branch_pool", "w_branch_mla",
                                  "w_out", "w_xq", "w_xkv", "w_xo"]}
    weights["w_pool"] = np.ascontiguousarray(np.asarray(inputs["w_pool"])[0].reshape(2048, 512))
    if dff != DFF:
        for f in ("ffn1", "ffn2"):
            wg = weights[f + "_w_gu"]
            weights[f + "_w_gu"] = np.ascontiguousarray(np.concatenate([wg[:, 0:dff], wg[:, DFF:DFF + dff]], axis=1))
            weights[f + "_w_down"] = np.ascontiguousarray(weights[f + "_w_down"][0:dff])
    gbase = np.zeros((128, NG), np.float32)
    gbase[:, G_F1:G_F1 + 32] = _fm(sq("ffn1_norm")); gbase[:, G_MIX:G_MIX + 32] = _fm(sq("mix_norm"))
    gbase[:, G_X:G_X + 32] = _fm(sq("x_norm")); gbase[:, G_F2:G_F2 + 32] = _fm(sq("ffn2_norm"))
    gbase[:, G_MEM:G_MEM + 32] = _fm(sq("mem_norm")); gbase[:, G_PS:G_PS + 16] = _fm(sq("pool_scale"))
    gbase[:, G_QL:G_QL + 8] = _fm(sq("q_latent_norm")); gbase[:, G_KVL:G_KVL + 4] = _fm(sq("kv_latent_norm"))
    gbase[:, G_QN] = sq("q_nope_norm"); gbase[:, G_KN] = sq("k_nope_norm"); gbase[:, G_XQ] = sq("xq_norm"); gbase[:, G_XK] = sq("xk_norm")
    qr = sq("q_rope_norm"); kr = sq("k_rope_norm")
    gbase[0:64, G_QR] = qr; gbase[0:64, G_QRS] = np.concatenate([qr[32:], qr[:32]])
    gbase[0:64, G_KR] = kr; gbase[0:64, G_KRS] = np.concatenate([kr[32:], kr[:32]])
    half = 32
    inv = (1.0 / (np.float32(10000.0) ** (np.arange(half, dtype=np.float32) * np.float32(2.0 / 64)))).astype(np.float32)
    gbase[0:64, G_INV] = np.concatenate([inv, inv])
    gbase[0:32, G_SGN] = -1.0; gbase[32:64, G_SGN] = 1.0
    cst = np.zeros((128, 192), np.float32)
    cst[:, 0:128] = np.eye(128, dtype=np.float32)
    for m in range(64):
        cst[(m + 32) % 64, 128 + m] = 1.0
    p_ = np.arange(128)[:, None]
    q_ = np.arange(T)[None, :]
    tri = np.concatenate([(j * 128 + p_ <= q_).astype(np.float32) for j in range(4)], axis=1).astype(ml_dtypes.bfloat16)
    in_maps = []
    for core in range(8):
        b, c = core // 4, core % 4
        g = gbase.copy()
        for r in range(4):
            g[:, G_BIAS + r] = 0.0 if r < c else NEG
            g[:, G_SEL + r] = 1.0 if (c >= 1 and r == c - 1) else 0.0
        g[:, G_SEL + 4] = 1.0 if c == 0 else 0.0
        invc = np.zeros((128, 64), np.float32)
        for gg in range(4):
            w = 2 << gg
            t = np.arange(16)
            invc[:, gg * 16:(gg + 1) * 16] = (1.0 / np.minimum(t + 1, w)) if c == 0 else (1.0 / w)
        rows = np.concatenate([np.arange(s * 2048 + c * 512, s * 2048 + c * 512 + 512) for s in range(NSTEP)])
        m = {"x": np.ascontiguousarray(x[b, rows, :]), "pos": np.ascontiguousarray(positions[b, rows].reshape(NSTEP, T).astype(np.int32)),
             "mem": np.ascontiguousarray(mem[b]), "gpack": g, "cst": cst, "tri": tri, "invc": invc}
        m.update(weights)
        in_maps.append(m)
    return in_maps


def assemble(results, NSTEP=4):
    SEQ = NSTEP * 2048
    out = np.zeros((2, SEQ, D), np.float32)
    for core in range(8):
        b, c = core // 4, core % 4
        o = results[core]["out"]
        for s in range(NSTEP):
            out[b, s * 2048 + c * 512: s * 2048 + c * 512 + 512, :] = o[s * T:(s + 1) * T]
    return out


_NC_CACHE = {}


def kernel(**inputs):
    NSTEP = 4
    if NSTEP not in _NC_CACHE:
        _NC_CACHE[NSTEP] = build(NSTEP)
    nc = _NC_CACHE[NSTEP]
    in_maps = make_in_maps(inputs, NSTEP)
    res = run_bass_kernel_spmd(nc, in_maps, core_ids=list(range(8)))
    return assemble(res.results, NSTEP)
```

```python
import numpy as np
import ml_dtypes
import concourse.bass as bass
import concourse.mybir as mybir
from concourse.bass_utils import run_bass_kernel_spmd
from contextlib import ExitStack

F32 = mybir.dt.float32
BF16 = mybir.dt.bfloat16
I32 = mybir.dt.int32
AF = mybir.ActivationFunctionType
ALU = mybir.AluOpType

ENGS = ("pe", "act", "dve", "pool", "sp")

D = 4096
DC = 32
DFF = 11008
T = 512
NH = 16
EPS = 1e-6
IN_POOL, IN_Q, IN_KV, IN_KR, IN_GP, IN_GM = 0, 2048, 3072, 3584, 3648, 3648 + 4096
G_F1, G_MIX, G_X, G_F2, G_MEM, G_PS, G_QL, G_KVL = 0, 32, 64, 96, 128, 160, 176, 184
G_QN, G_KN, G_XQ, G_XK, G_QR, G_QRS, G_KR, G_KRS, G_INV, G_SGN, G_BIAS, G_SEL = 188, 189, 190, 191, 192, 193, 194, 195, 196, 197, 198, 202
NG = 208
NEG = -30000.0
TWO_PI = 6.283185307179586
C1 = 6.28125
C2 = TWO_PI - C1
PI_SAFE = 3.1415925


class Sem:
    __slots__ = ("h", "count", "name")

    def __init__(self, h, name):
        self.h = h
        self.count = 0
        self.name = name


class Prog:
    def __init__(self, nc, es):
        self.nc = nc
        self.es = es
        self.ops = {e: [] for e in ENGS}
        self.prog_sem = {}
        self.waited = {e: {} for e in ENGS}
        self.W = {}
        self.R = {}
        self.nsem = 0
        self.ninstr = {e: 0 for e in ENGS}
        for e in ENGS:
            self.prog_sem[e] = self.new_sem("p_" + e)

    def new_sem(self, name):
        self.nsem += 1
        h = self.es.enter_context(self.nc.semaphore(f"{name}_{self.nsem}"))
        return Sem(h, name)

    def new_epoch(self):
        for e in ENGS:
            self.prog_sem[e] = self.new_sem("p_" + e)

    def _deps(self, eng, reads, writes, skip_sem=None, extra=None):
        need = {}
        if extra:
            for s, v in extra:
                if need.get(s, 0) < v:
                    need[s] = v
        for k in reads:
            w = self.W.get(k)
            if w:
                for s, v in w.items():
                    if need.get(s, 0) < v:
                        need[s] = v
        for k in writes:
            for d in (self.W.get(k), self.R.get(k)):
                if d:
                    for s, v in d.items():
                        if need.get(s, 0) < v:
                            need[s] = v
        wd = self.waited[eng]
        out = []
        for s, v in need.items():
            if s is skip_sem:
                continue
            if wd.get(s, 0) < v:
                wd[s] = v
                out.append((s, v))
        return out

    def _record(self, reads, writes, sem, val):
        for k in reads:
            d = self.R.get(k)
            if d is None:
                d = self.R[k] = {}
            if d.get(sem, 0) < val:
                d[sem] = val
        for k in writes:
            d = self.W.get(k)
            if d is None:
                d = self.W[k] = {}
            if d.get(sem, 0) < val:
                d[sem] = val

    def op(self, eng, ins, reads=(), writes=()):
        ps = self.prog_sem[eng]
        waits = self._deps(eng, reads, writes, skip_sem=ps if eng == "pe" else None)
        ps.count += 1
        self._record(reads, writes, ps, ps.count)
        self.ops[eng].append((waits, ins, ps, 1))
        self.ninstr[eng] += 1

    def pe_group(self, inss, reads=(), writes=()):
        ps = self.prog_sem["pe"]
        waits = self._deps("pe", reads, writes, skip_sem=ps)
        ps.count += 1
        self._record(reads, writes, ps, ps.count)
        lst = self.ops["pe"]
        n = len(inss)
        for i, ins in enumerate(inss):
            lst.append((waits if i == 0 else (), ins, ps if i == n - 1 else None, 1))
        self.ninstr["pe"] += n

    def dma(self, eng, ins, dsem, reads=(), writes=(), inc=16, guard=False):
        extra = [(dsem, dsem.count)] if (guard and dsem.count > 0) else None
        waits = self._deps(eng, reads, writes, extra=extra)
        dsem.count += inc
        self._record(reads, writes, dsem, dsem.count)
        self.ops[eng].append((waits, ins, dsem, inc))
        self.ninstr[eng] += 1

    def wait_all(self, eng, keys):
        waits = self._deps(eng, keys, ())
        self.ops[eng].append((waits, None, None, 0))

    def replay(self, eng, e):
        for waits, ins, sem, inc in self.ops[eng]:
            for s, v in waits:
                e.wait_ge(s.h, v)
            if ins is None:
                continue
            meth, args, kw = ins
            r = getattr(e, meth)(*args, **kw)
            if sem is not None:
                r.then_inc(sem.h, inc)

    def run_block(self):
        nc = self.nc
        with nc.Block() as block:
            @block.tensor
            def _(e):
                self.replay("pe", e)

            @block.scalar
            def _(e):
                self.replay("act", e)

            @block.vector
            def _(e):
                self.replay("dve", e)

            @block.gpsimd
            def _(e):
                self.replay("pool", e)

            @block.sync
            def _(e):
                self.replay("sp", e)


def I(meth, *args, **kw):
    return (meth, args, kw)


def build(NSTEP=4, debug=False, dff=DFF):
    SEQ = NSTEP * 2048
    KTOT = SEQ + 512
    NKB = KTOT // 128
    nc = bass.Bass("TRN2", target_bir_lowering=False)
    din = lambda name, shape, dt=F32: nc.dram_tensor(name, shape, dt, kind="ExternalInput").ap()
    dint = lambda name, shape, dt=F32: nc.dram_tensor(name, shape, dt).ap()
    x_d = din("x", [NSTEP * T, D])
    pos_d = din("pos", [NSTEP, T], I32)
    mem_d = din("mem", [256, D])
    g_d = din("gpack", [128, NG])
    cst_d = din("cst", [128, 192])
    tri_d = din("tri", [128, 4 * T], BF16)
    invc_d = din("invc", [128, 64])
    w_gu1 = din("ffn1_w_gu", [D, 2 * dff]); w_d1 = din("ffn1_w_down", [dff, D])
    w_gu2 = din("ffn2_w_gu", [D, 2 * dff]); w_d2 = din("ffn2_w_down", [dff, D])
    w_in = din("w_in", [D, 11840]); w_pool = din("w_pool", [2048, 512])
    w_uq = din("w_uq", [1024, 3072]); w_ukv = din("w_ukv", [512, 4096])
    w_pa = din("w_branch_pool", [2048, D]); w_pb = din("w_branch_mla", [2048, D])
    w_out = din("w_out", [D, D]); w_xq = din("w_xq", [D, 512]); w_xkv = din("w_xkv", [D, 1024]); w_xo = din("w_xo", [512, D])
    out_d = nc.dram_tensor("out", [NSTEP * T, D], F32, kind="ExternalOutput").ap()
    NDBG = 6
    dbg_d = nc.dram_tensor("dbg", [NDBG, 128, DC * T], F32, kind="ExternalOutput").ap() if debug else None
    dbgb_d = nc.dram_tensor("dbgb", [NDBG, 128, 16384], BF16, kind="ExternalOutput").ap() if debug else None
    hbuf = dint("hbuf", [128, DC * T])
    send_l = [dint(f"send_l{s}", [128, 5 * T], BF16) for s in range(NSTEP)]
    recv_l = [dint(f"recv_l{s}", [4 * 128, 5 * T], BF16) for s in range(NSTEP)]
    send_t = [dint(f"send_t{s}", [128, 256]) for s in range(NSTEP)]
    recv_t = [dint(f"recv_t{s}", [4 * 128, 256]) for s in range(NSTEP)]
    Kc = dint("Kc", [NH, 128, KTOT], BF16)
    Krc = dint("Krc", [64, KTOT], BF16)
    Vc = dint("Vc", [NH, 128, NKB, 128], BF16)

    with ExitStack() as es:
        P = Prog(nc, es)
        sb = lambda name, shape, dt: es.enter_context(nc.sbuf_tensor(name, shape, dt))
        H = sb("H", [128, DC, T], F32)
        U = sb("U", [128, DC, T], BF16)
        SC = sb("SC", [128, 16384], BF16)
        RING = [sb(f"ring{i}", [128, 8192], BF16) for i in range(3)]
        G = sb("G", [128, NG], F32)
        CST = sb("CST", [128, 192], F32)
        ONESF = sb("ONESF", [128, 128], F32)
        ONESB = sb("ONESB", [128, 128], BF16)
        TRI = sb("TRI", [128, 4, T], BF16)
        XK = sb("XK", [128, 4, 256], BF16)
        XV = sb("XV", [128, 2, 512], BF16)
        COSD = sb("COSD", [128, T], F32)
        SIND = sb("SIND", [128, T], F32)
        TMPA = sb("TMPA", [128, 6 * T], F32)
        PT = sb("PT", [128, 256], F32)
        INVC = sb("INVC", [128, 64], F32)
        GS = sb("GS", [128, 4], F32)
        PS = [es.enter_context(nc.psum_tensor(f"ps{i}", [128, T], F32)) for i in range(8)]
        ident = CST[:, 0:128]
        perm = CST[0:64, 128:192]
        TMP = [TMPA[:, i * T:(i + 1) * T] for i in range(6)]
        TMPb = [TMPA[:, i * T:(i + 1) * T].bitcast(BF16)[:, 0:T] for i in range(6)]
        tk = lambda i: ("tmp", i)
        Hf = H[:].rearrange("p a b -> p (a b)")
        E = Hf[:, 0:16 * 528].rearrange("p (c t) -> p c t", t=528)
        AOUT = Hf[:, 0:4096].bitcast(BF16).rearrange("p (c t) -> p c t", t=T)
        BOUT = Hf[:, 4096:8192].bitcast(BF16).rearrange("p (c t) -> p c t", t=T)
        ZQ = H[:, 17:25, :]
        hk = lambda c: ("H", c)
        HALL = [hk(i) for i in range(DC)]

        def ekeys(c):
            lo, hi = c * 2112, (c + 1) * 2112 - 1
            return [hk(i) for i in range(lo // 2048, hi // 2048 + 1)]
        EALL = [hk(i) for i in range(17)]
        SCf = SC[:].bitcast(F32)
        ACT_ = SC[:].rearrange("p (c t) -> p c t", t=T)
        sck = lambda i: ("sc", i)
        SCALL = [sck(i) for i in range(32)]
        CQ = SC[:, 0:4096].rearrange("p (c t) -> p c t", t=T)
        AMX = SC[:, 4096:12288].rearrange("p (c t) -> p c t", t=T)
        SENDT = SC[:, 4096:4096 + 2560].rearrange("p (c t) -> p c t", t=T)
        PTILE = [SC[:, 12288 + i * T: 12288 + (i + 1) * T] for i in range(3)]
        QN = [SC[:, 13824 + i * T: 13824 + (i + 1) * T] for i in range(2)]
        QR = [SC[:, 14848 + i * T: 14848 + (i + 1) * T] for i in range(2)]
        GT = SCf[:, 6144:7168].rearrange("p (r n) -> p r n", n=256)
        XQT = SC[:, 0:2048].rearrange("p (c t) -> p c t", t=T)
        XOT = SC[:, 2048:4096].rearrange("p (c t) -> p c t", t=T)
        XST = [SCf[:, i * 4096:(i + 1) * 4096] for i in range(2)]
        xstk = lambda i: [sck(j) for j in range(16 * i, 16 * i + 16)]

        st = {"bank": 0, "set": 0, "ring": 0, "misc": 0}
        reserved = set()
        ring_sem = [P.new_sem(f"ring{i}") for i in range(3)]
        d_miscs = [P.new_sem(f"d_misc{i}") for i in range(10)]
        d_x = [P.new_sem("d_x0"), P.new_sem("d_x1")]
        d_o = [P.new_sem("d_o0"), P.new_sem("d_o1")]
        d_kst = [P.new_sem(f"d_kst{i}") for i in range(7)]
        d_ccl = P.new_sem("d_ccl")
        d_cct = P.new_sem("d_cct")

        def misc_dma(ins, reads=(), writes=(), eng="sp"):
            i = st["misc"]
            st["misc"] = (i + 1) % len(d_miscs)
            P.dma(eng, ins, d_miscs[i], reads=reads, writes=writes, guard=True)

        def bank():
            while True:
                b = st["bank"]
                st["bank"] = (b + 1) % 8
                if b not in reserved:
                    return b

        def bank_set():
            for _ in range(2):
                s_ = st["set"]
                st["set"] ^= 1
                bs = [4 * s_ + j for j in range(4)]
                if not (set(bs) & reserved):
                    return bs
            raise RuntimeError("no free bank set")

        pk = lambda b: ("ps", b)

        def ring_load(inss_fn, reads=()):
            i = st["ring"]
            st["ring"] = (i + 1) % 3
            for ins in inss_fn(RING[i]):
                P.dma("pool", ins, ring_sem[i], reads=reads, writes=[("ring", i)])
            return i

        def w_slab(w_ap, k0, nk, c0, ncols):
            src = w_ap[k0 * 128:(k0 + nk) * 128, c0:c0 + ncols].rearrange("(kc p) n -> p kc n", p=128)
            i = ring_load(lambda slot: [I("dma_start", out=slot[:, 0:nk * ncols].rearrange("p (k n) -> p k n", n=ncols), in_=src)])
            return i, RING[i][:, 0:nk * ncols].rearrange("p (k n) -> p k n", n=ncols)

        def stream_linear(w_ap, k0w, rhs, col0, ncols, evac, ntok=T):
            nk = len(rhs)
            for cb in range(0, ncols, 512):
                nb = min(512, ncols - cb)
                nj = nb // 128
                banks = bank_set()
                reserved.update(banks[0:nj])
                for s0 in range(0, nk, 16):
                    ns = min(16, nk - s0)
                    si, sv = w_slab(w_ap, k0w + s0, ns, col0 + cb, nb)
                    for j in range(nj):
                        b = banks[j]
                        inss = [I("matmul", PS[b][:, 0:ntok], lhsT=sv[:, kk, j * 128:(j + 1) * 128], rhs=rhs[s0 + kk][0],
                                  start=(s0 + kk == 0), stop=(s0 + kk == nk - 1)) for kk in range(ns)]
                        P.pe_group(inss, reads=[("ring", si)] + [rhs[s0 + kk][1] for kk in range(ns)], writes=[pk(b)])
                for j in range(nj):
                    keep = evac(cb // 128 + j, PS[banks[j]], pk(banks[j]))
                    if not keep:
                        reserved.discard(banks[j])

        def rstd_from(ps_b, nfeat, t_out, rows=128, n=T):
            P.op("act", I("activation", out=TMP[t_out][0:rows, 0:n], in_=PS[ps_b][0:rows, 0:n], func=AF.Sqrt, scale=1.0 / nfeat, bias=EPS),
                 reads=[pk(ps_b)], writes=[tk(t_out)])
            P.op("dve", I("reciprocal", out=TMP[t_out][0:rows, 0:n], in_=TMP[t_out][0:rows, 0:n]), reads=[tk(t_out)], writes=[tk(t_out)])

        def norm_full(gcol):
            b = bank()
            reserved.add(b)
            for ci in range(DC):
                t = ci % 2
                P.op("act", I("activation", out=TMPb[t], in_=H[:, ci, :], func=AF.Square), reads=[hk(ci)], writes=[tk(t)])
                P.pe_group([I("matmul", PS[b][:], lhsT=ONESB[:], rhs=TMPb[t], start=(ci == 0), stop=(ci == DC - 1))],
                           reads=[tk(t), "onesb"], writes=[pk(b)])
            rstd_from(b, D, 2)
            reserved.discard(b)
            for ci in range(DC):
                P.op("dve", I("scalar_tensor_tensor", out=U[:, ci, :], in0=H[:, ci, :], scalar=G[:, gcol + ci:gcol + ci + 1], in1=TMP[2],
                              op0=ALU.mult, op1=ALU.mult), reads=[hk(ci), tk(2), "G"], writes=[("U", ci)])

        Urhs = [(U[:, ci, :], ("U", ci)) for ci in range(DC)]

        def ffn(w_gu, w_d):
            nch = dff // 128
            parts = [(i, min(i + 32, nch)) for i in range(0, nch, 32)]
            for (c0, c1) in parts:
                for cb in range(c0 * 128, c1 * 128, 512):
                    nb = min(512, c1 * 128 - cb)

                    def evac_g(j, ps, pkey):
                        P.op("act", I("activation", out=TMP[j], in_=ps[:], func=AF.Silu), reads=[pkey], writes=[tk(j)])

                    def evac_u(j, ps, pkey, cb=cb, c0=c0):
                        a = cb // 128 + j - c0
                        P.op("dve", I("tensor_tensor", out=ACT_[:, a, :], in0=TMP[j], in1=ps[:], op=ALU.mult), reads=[pkey, tk(j)], writes=[sck(a)])
                    stream_linear(w_gu, 0, Urhs, cb, nb, evac_g)
                    stream_linear(w_gu, 0, Urhs, dff + cb, nb, evac_u)
                arhs_ = [(ACT_[:, a, :], sck(a)) for a in range(c1 - c0)]

                def evac_d(m, ps, pkey):
                    P.op("dve", I("scalar_tensor_tensor", out=H[:, m, :], in0=ps[:], scalar=0.5, in1=H[:, m, :], op0=ALU.mult, op1=ALU.add),
                         reads=[pkey, hk(m)], writes=[hk(m)])
                stream_linear(w_d, c0, arhs_, 0, D, evac_d)

        def run(gen):
            for _ in gen:
                pass

        def pns_gen(ps_b, nrows, t_sq, t_r, n=T):
            P.op("act", I("activation", out=TMPb[t_sq][0:nrows, 0:n], in_=PS[ps_b][0:nrows, 0:n], func=AF.Square), reads=[pk(ps_b)], writes=[tk(t_sq)])
            yield
            b2 = bank()
            reserved.add(b2)
            P.pe_group([I("matmul", PS[b2][0:nrows, 0:n], lhsT=ONESB[0:nrows, 0:nrows], rhs=TMPb[t_sq][0:nrows, 0:n], start=True, stop=True)],
                       reads=[tk(t_sq), "onesb"], writes=[pk(b2)])
            yield
            P.op("act", I("activation", out=TMP[t_r][0:nrows, 0:n], in_=PS[b2][0:nrows, 0:n], func=AF.Sqrt, scale=1.0 / nrows, bias=EPS),
                 reads=[pk(b2)], writes=[tk(t_r)])
            reserved.discard(b2)
            yield
            P.op("dve", I("reciprocal", out=TMP[t_r][0:nrows, 0:n], in_=TMP[t_r][0:nrows, 0:n]), reads=[tk(t_r)], writes=[tk(t_r)])
            yield

        def partition_norm_stats(ps_b, nrows, t_sq, t_r, n=T):
            run(pns_gen(ps_b, nrows, t_sq, t_r, n))

        def rope_gen(ps_b, gcol, gscol, out_ap, out_key, gtile, gkey):
            P.op("act", I("copy", out=TMP[0][0:64, :], in_=PS[ps_b][0:64, :]), reads=[pk(ps_b)], writes=[tk(0)])
            yield
            b_sw = bank()
            reserved.add(b_sw)
            P.pe_group([I("matmul", PS[b_sw][0:64, :], lhsT=perm, rhs=TMP[0][0:64, :], start=True, stop=True)], reads=[tk(0), "cst"], writes=[pk(b_sw)])
            yield
            yield from pns_gen(ps_b, 64, 1, 2)
            P.op("dve", I("scalar_tensor_tensor", out=TMP[3][0:64, :], in0=TMP[0][0:64, :], scalar=gtile[0:64, gcol:gcol + 1], in1=COSD[0:64, :],
                          op0=ALU.mult, op1=ALU.mult), reads=[tk(0), "rope", gkey], writes=[tk(3)])
            yield
            P.op("dve", I("scalar_tensor_tensor", out=TMP[4][0:64, :], in0=PS[b_sw][0:64, :], scalar=gtile[0:64, gscol:gscol + 1], in1=SIND[0:64, :],
                          op0=ALU.mult, op1=ALU.mult), reads=[pk(b_sw), "rope", gkey], writes=[tk(4)])
            reserved.discard(b_sw)
            yield
            P.op("dve", I("tensor_tensor", out=TMP[3][0:64, :], in0=TMP[3][0:64, :], in1=TMP[4][0:64, :], op=ALU.add), reads=[tk(3), tk(4)], writes=[tk(3)])
            yield
            P.op("dve", I("tensor_tensor", out=out_ap, in0=TMP[3][0:64, :], in1=TMP[2][0:64, :], op=ALU.mult), reads=[tk(3), tk(2)], writes=[out_key])
            yield

        def rope_norm(ps_b, gcol, gscol, out_ap, out_key, gtile, gkey):
            run(rope_gen(ps_b, gcol, gscol, out_ap, out_key, gtile, gkey))

        def dump(slot, keys):
            if debug:
                misc_dma(I("dma_start", out=dbg_d[slot], in_=Hf), reads=keys, writes=[("dbg", slot)])

        def dumpb(slot, keys):
            if debug:
                misc_dma(I("dma_start", out=dbgb_d[slot], in_=SC[:]), reads=keys, writes=[("dbgb", slot)])

        def softmax_pv(blocks, qk_fn, v_fn, out_ap, out_key, extra_reads, bg=None):
            bo, bd = bank(), bank()
            reserved.add(bo); reserved.add(bd)
            nb_ = len(blocks)
            pend = []
            for i in range(nb_ + 2):
                if i < nb_:
                    blk = blocks[i]
                    if "loader" in blk:
                        blk["reads"] = [("ring", blk["loader"]())]
                    bs = bank()
                    pt = i % 3
                    P.pe_group(qk_fn(i, bs), reads=blk["reads"] + extra_reads, writes=[pk(bs)])
                    if blk["bias"] is not None:
                        P.op("act", I("activation", out=PTILE[pt], in_=PS[bs][:], func=AF.Exp, bias=blk["bias"], scale=1.0), reads=[pk(bs), "G"], writes=[sck(24 + pt)])
                    else:
                        P.op("act", I("activation", out=PTILE[pt], in_=PS[bs][:], func=AF.Exp), reads=[pk(bs)], writes=[sck(24 + pt)])
                    if blk["tri"] is not None:
                        P.op("dve", I("tensor_tensor", out=PTILE[pt], in0=PTILE[pt], in1=TRI[:, blk["tri"], :], op=ALU.mult),
                             reads=[sck(24 + pt), "tri"], writes=[sck(24 + pt)])
                if len(pend) >= 2 or (i >= nb_ and pend):
                    pi, ppt = pend.pop(0)
                    P.pe_group([I("matmul", PS[bo][:], lhsT=v_fn(pi), rhs=PTILE[ppt], start=(pi == 0), stop=(pi == nb_ - 1)),
                                I("matmul", PS[bd][:], lhsT=ONESB[:], rhs=PTILE[ppt], start=(pi == 0), stop=(pi == nb_ - 1))],
                               reads=blocks[pi]["reads"] + [sck(24 + ppt), "onesb"], writes=[pk(bo), pk(bd)])
                if i < nb_:
                    pend.append((i, pt))
                if bg is not None:
                    next(bg, None)
            assert not pend
            P.op("dve", I("reciprocal", out=TMP[5], in_=PS[bd][:]), reads=[pk(bd)], writes=[tk(5)])
            P.op("dve", I("tensor_tensor", out=out_ap, in0=PS[bo][:], in1=TMP[5], op=ALU.mult), reads=[pk(bo), tk(5)], writes=[out_key])
            reserved.discard(bo); reserved.discard(bd)

        misc_dma(I("dma_start", out=G[:], in_=g_d[:, :]), writes=["G"])
        misc_dma(I("dma_start", out=CST[:], in_=cst_d[:, :]), writes=["cst"])
        misc_dma(I("dma_start", out=TRI[:].rearrange("p a b -> p (a b)"), in_=tri_d[:, :]), writes=["tri"])
        misc_dma(I("dma_start", out=INVC[:], in_=invc_d[:, :]), writes=["invc"])
        P.op("dve", I("memset", ONESF[:], 1.0), writes=["onesf"])
        P.op("dve", I("memset", ONESB[:], 1.0), writes=["onesb"])
        P.op("dve", I("memset", PT[:], 0.0), writes=["PT"])
        sc_q = 192.0 ** -0.5
        P.op("dve", I("tensor_scalar", out=GS[:, 0:1], in0=G[:, G_QN:G_QN + 1], scalar1=sc_q, scalar2=None, op0=ALU.mult), reads=["G"], writes=["GS"])
        P.op("dve", I("tensor_scalar", out=GS[:, 1:3], in0=G[:, G_QR:G_QR + 2], scalar1=sc_q, scalar2=None, op0=ALU.mult), reads=["G"], writes=["GS"])
        P.op("dve", I("tensor_scalar", out=GS[:, 3:4], in0=G[:, G_XQ:G_XQ + 1], scalar1=128.0 ** -0.5, scalar2=None, op0=ALU.mult), reads=["G"], writes=["GS"])

        MNT = SC[:, 0:8192].rearrange("p (c t) -> p c t", t=256)
        MST = SCf[:, 4096:8192].rearrange("p (a f) -> p a f", f=2048)
        mstk = [sck(j) for j in range(16, 32)]
        bss = bank()
        reserved.add(bss)
        for pas in range(2):
            for fg in range(2):
                P.dma("sp", I("dma_start", out=MST, in_=mem_d[:, fg * 2048:(fg + 1) * 2048].rearrange("(a p) f -> p a f", p=128)), d_x[0], writes=mstk)
                for cl in range(16):
                    ci = fg * 16 + cl
                    b = bank()
                    P.pe_group([I("transpose", PS[b][:, a * 128:(a + 1) * 128], MST[:, a, cl * 128:(cl + 1) * 128], ident) for a in range(2)],
                               reads=mstk + ["cst"], writes=[pk(b)])
                    if pas == 0:
                        t = ci % 2
                        P.op("act", I("activation", out=TMPb[t][:, 0:256], in_=PS[b][:, 0:256], func=AF.Square), reads=[pk(b)], writes=[tk(t)])
                        P.pe_group([I("matmul", PS[bss][:, 0:256], lhsT=ONESB[:], rhs=TMPb[t][:, 0:256], start=(ci == 0), stop=(ci == DC - 1))],
                                   reads=[tk(t), "onesb"], writes=[pk(bss)])
                    else:
                        P.op("dve", I("scalar_tensor_tensor", out=MNT[:, ci, :], in0=PS[b][:, 0:256], scalar=G[:, G_MEM + ci:G_MEM + ci + 1],
                                      in1=TMP[2][:, 0:256], op0=ALU.mult, op1=ALU.mult), reads=[pk(b), tk(2), "G"], writes=[sck(ci // 2)])
            if pas == 0:
                rstd_from(bss, D, 2, n=256)
                reserved.discard(bss)
        mrhs = [(MNT[:, ci, :], sck(ci // 2)) for ci in range(DC)]

        def evac_xk(j, ps, pkey):
            partition_norm_stats(pkey[1], 128, 0, 1, n=256)
            P.op("dve", I("scalar_tensor_tensor", out=XK[:, j, :], in0=ps[:, 0:256], scalar=G[:, G_XK:G_XK + 1], in1=TMP[1][:, 0:256],
                          op0=ALU.mult, op1=ALU.mult), reads=[pkey, tk(1), "G"], writes=["XK"])
        stream_linear(w_xkv, 0, mrhs, 0, 512, evac_xk, ntok=256)
        banks = bank_set()
        for s0 in (0, 16):
            si, sv = w_slab(w_xkv, s0, 16, 512, 512)
            for a in range(2):
                P.pe_group([I("matmul", PS[banks[a]][:], lhsT=mrhs[s0 + kk][0][:, a * 128:(a + 1) * 128], rhs=sv[:, kk, :],
                              start=(s0 + kk == 0), stop=(s0 + kk == DC - 1)) for kk in range(16)],
                           reads=[("ring", si)] + [mrhs[s0 + kk][1] for kk in range(16)], writes=[pk(banks[a])])
        for a in range(2):
            P.op("act", I("copy", out=XV[:, a, :], in_=PS[banks[a]][:]), reads=[pk(banks[a])], writes=["XV"])

        rg = [[0, 1, 2, 3], [4, 5, 6, 7]]
        for s in range(NSTEP):
            P.new_epoch()
            for tc in range(4):
                xs = XST[tc % 2]
                P.dma("sp", I("dma_start", out=xs, in_=x_d[s * T + tc * 128: s * T + (tc + 1) * 128, :]), d_x[tc % 2], writes=xstk(tc % 2))
                for cg in range(8):
                    b = bank()
                    P.pe_group([I("transpose", PS[b][:, i * 128:(i + 1) * 128], xs[:, (4 * cg + i) * 128:(4 * cg + i + 1) * 128], ident) for i in range(4)],
                               reads=xstk(tc % 2) + ["cst"], writes=[pk(b)])
                    dst = H[:, 4 * cg:4 * cg + 4, tc * 128:(tc + 1) * 128]
                    srcv = PS[b][:].rearrange("p (i t) -> p i t", t=128)
                    if cg % 2 == 0:
                        P.op("act", I("copy", out=dst, in_=srcv), reads=[pk(b)], writes=[hk(4 * cg + i) for i in range(4)])
                    else:
                        P.op("dve", I("tensor_copy", out=dst, in_=srcv), reads=[pk(b)], writes=[hk(4 * cg + i) for i in range(4)])
            POSI = TMPA[:, 5 * T:6 * T].bitcast(I32)
            KI = TMPA[:, 4 * T:5 * T].bitcast(I32)
            misc_dma(I("dma_start", out=POSI[0:64, :], in_=pos_d[s].partition_broadcast(64)), writes=[tk(5)])
            P.op("dve", I("tensor_copy", out=TMP[0][0:64, :], in_=POSI[0:64, :]), reads=[tk(5)], writes=[tk(0)])
            P.op("dve", I("tensor_scalar", out=TMP[0][0:64, :], in0=TMP[0][0:64, :], scalar1=G[0:64, G_INV:G_INV + 1], scalar2=None, op0=ALU.mult),
                 reads=[tk(0), "G"], writes=[tk(0)])
            t1, t2 = TMP[1][0:64, :], TMP[2][0:64, :]
            for which, dst in ((0, SIND), (1, COSD)):
                shift = 0.0 if which == 0 else float(np.pi / 2)
                P.op("dve", I("tensor_scalar", out=t1, in0=TMP[0][0:64, :], scalar1=shift, scalar2=None, op0=ALU.add), reads=[tk(0)], writes=[tk(1)])
                P.op("dve", I("tensor_scalar", out=KI[0:64, :], in0=t1, scalar1=1.0 / TWO_PI, scalar2=None, op0=ALU.mult), reads=[tk(1)], writes=[tk(4)])
                P.op("dve", I("tensor_copy", out=t2, in_=KI[0:64, :]), reads=[tk(4)], writes=[tk(2)])
                P.op("dve", I("scalar_tensor_tensor", out=t1, in0=t2, scalar=-C1, in1=t1, op0=ALU.mult, op1=ALU.add), reads=[tk(1), tk(2)], writes=[tk(1)])
                P.op("dve", I("scalar_tensor_tensor", out=t1, in0=t2, scalar=-C2, in1=t1, op0=ALU.mult, op1=ALU.add), reads=[tk(1), tk(2)], writes=[tk(1)])
                P.op("dve", I("tensor_scalar", out=t2, in0=t1, scalar1=float(np.pi), scalar2=-TWO_PI, op0=ALU.is_gt, op1=ALU.mult), reads=[tk(1)], writes=[tk(2)])
                P.op("dve", I("tensor_tensor", out=t1, in0=t1, in1=t2, op=ALU.add), reads=[tk(1), tk(2)], writes=[tk(1)])
                P.op("dve", I("tensor_scalar", out=t2, in0=t1, scalar1=float(-np.pi), scalar2=TWO_PI, op0=ALU.is_lt, op1=ALU.mult), reads=[tk(1)], writes=[tk(2)])
                P.op("dve", I("tensor_tensor", out=t1, in0=t1, in1=t2, op=ALU.add), reads=[tk(1), tk(2)], writes=[tk(1)])
                P.op("dve", I("tensor_scalar", out=t1, in0=t1, scalar1=-PI_SAFE, scalar2=PI_SAFE, op0=ALU.max, op1=ALU.min), reads=[tk(1)], writes=[tk(1)])
                P.op("act", I("activation", out=dst[0:64, :], in_=t1, func=AF.Sin), reads=[tk(1)], writes=["rope"])
            P.op("dve", I("tensor_scalar", out=SIND[0:64, :], in0=SIND[0:64, :], scalar1=G[0:64, G_SGN:G_SGN + 1], scalar2=None, op0=ALU.mult),
                 reads=["rope", "G"], writes=["rope"])

            norm_full(G_F1)
            ffn(w_gu1, w_d1)
            if s == 0:
                dump(0, HALL)
            norm_full(G_MIX)
            misc_dma(I("dma_start", out=hbuf[:, :], in_=Hf), reads=HALL, writes=["hbuf"])

            def evac_pool(j, ps, pkey):
                P.op("act", I("copy", out=E[:, j, 16:528], in_=ps[:]), reads=[pkey], writes=ekeys(j))
            stream_linear(w_in, 0, Urhs, IN_POOL, 2048, evac_pool)
            misc_dma(I("dma_start", out=send_t[s][:, :].rearrange("p (c t) -> p c t", t=16), in_=E[:, :, 512:528]), reads=EALL, writes=[("send_t", s)])
            bkv = bank()
            reserved.add(bkv)

            def evac_kv(j, ps, pkey):
                P.op("act", I("copy", out=TMP[j], in_=ps[:]), reads=[pkey], writes=[tk(j)])
                t = 4 + j % 2
                P.op("act", I("activation", out=TMPb[t], in_=ps[:], func=AF.Square), reads=[pkey], writes=[tk(t)])
                P.pe_group([I("matmul", PS[bkv][:], lhsT=ONESB[:], rhs=TMPb[t], start=(j == 0), stop=(j == 3))], reads=[tk(t), "onesb"], writes=[pk(bkv)])
            stream_linear(w_in, 0, Urhs, IN_KV, 512, evac_kv)
            rstd_from(bkv, 512, 4)
            reserved.discard(bkv)
            for j in range(4):
                P.op("dve", I("scalar_tensor_tensor", out=SENDT[:, j, :], in0=TMP[j], scalar=G[:, G_KVL + j:G_KVL + j + 1], in1=TMP[4], op0=ALU.mult, op1=ALU.mult),
                     reads=[tk(j), tk(4), "G"], writes=[sck(8 + j)])
            bkr = bank()
            reserved.add(bkr)
            for s0 in (0, 16):
                si, sv = w_slab(w_in, s0, 16, IN_KR, 64)
                P.pe_group([I("matmul", PS[bkr][0:64, :], lhsT=sv[:, kk, :], rhs=Urhs[s0 + kk][0], start=(s0 + kk == 0), stop=(s0 + kk == DC - 1)) for kk in range(16)],
                           reads=[("ring", si)] + [Urhs[s0 + kk][1] for kk in range(16)], writes=[pk(bkr)])
            rope_norm(bkr, G_KR, G_KRS, SENDT[0:64, 4, :], sck(12), G, "G")
            reserved.discard(bkr)
            misc_dma(I("dma_start", out=send_l[s][:, 0:4 * T], in_=SC[:, 4096:4096 + 4 * T]), reads=[sck(8 + j) for j in range(4)], writes=[("send_l", s)])
            misc_dma(I("dma_start", out=send_l[s][0:64, 4 * T:5 * T], in_=SC[0:64, 4096 + 4 * T:4096 + 5 * T]), reads=[sck(12)], writes=[("send_l", s)])

            P.dma("pool", I("collective_compute", "AllGather", ALU.bypass, replica_groups=rg, ins=[send_l[s][:, :]], outs=[recv_l[s][:, :]]),
                  d_ccl, reads=[("send_l", s)], writes=[("recv_l", s)], inc=1)
            P.dma("pool", I("collective_compute", "AllGather", ALU.bypass, replica_groups=rg, ins=[send_t[s][:, :]], outs=[recv_t[s][:, :]]),
                  d_cct, reads=[("send_t", s)], writes=[("recv_t", s)], inc=1)

            bq = bank()
            reserved.add(bq)

            def evac_q(j, ps, pkey):
                P.op("act", I("copy", out=ZQ[:, j, :], in_=ps[:]), reads=[pkey], writes=[hk(17 + j)])
                t = j % 2
                P.op("act", I("activation", out=TMPb[t], in_=ps[:], func=AF.Square), reads=[pkey], writes=[tk(t)])
                P.pe_group([I("matmul", PS[bq][:], lhsT=ONESB[:], rhs=TMPb[t], start=(j == 0), stop=(j == 7))], reads=[tk(t), "onesb"], writes=[pk(bq)])
            stream_linear(w_in, 0, Urhs, IN_Q, 1024, evac_q)
            rstd_from(bq, 1024, 2)
            reserved.discard(bq)
            for j in range(8):
                P.op("dve", I("scalar_tensor_tensor", out=CQ[:, j, :], in0=ZQ[:, j, :], scalar=G[:, G_QL + j:G_QL + j + 1], in1=TMP[2], op0=ALU.mult, op1=ALU.mult),
                     reads=[hk(17 + j), tk(2), "G"], writes=[sck(j)])
            cqr = [(CQ[:, j, :], sck(j)) for j in range(8)]

            gtk = [sck(i) for i in range(24, 28)]
            misc_dma(I("dma_start", out=GT, in_=recv_t[s][:, :].rearrange("(r p) n -> p r n", p=128)), reads=[("recv_t", s)], writes=gtk)
            halo = E[:, :, 0:16]
            P.op("dve", I("tensor_scalar", out=halo, in0=PT[:].rearrange("p (c t) -> p c t", t=16), scalar1=G[:, G_SEL + 4:G_SEL + 5], scalar2=None, op0=ALU.mult),
                 reads=["PT", "G"], writes=EALL)
            for r in range(4):
                P.op("dve", I("scalar_tensor_tensor", out=halo, in0=GT[:, r, :].rearrange("p (c t) -> p c t", t=16), scalar=G[:, G_SEL + r:G_SEL + r + 1], in1=halo,
                              op0=ALU.mult, op1=ALU.add), reads=gtk + ["G"] + EALL, writes=EALL)
            P.op("dve", I("tensor_copy", out=PT[:], in_=GT[:, 3, :]), reads=gtk, writes=["PT"])
            BA = TMPA[:, 0:528]
            BB = TMPA[:, 1024:1552]
            bufs = [(BA, [tk(0), tk(1)]), (BB, [tk(2), tk(3)])]
            for ci in range(16):
                g = ci // 4
                w = 2 << g
                src = E[:, ci, :]
                ek = ekeys(ci)
                cur, curk = src, ek
                d = 1
                bi = 0
                while d < w:
                    dst, dstk = bufs[bi]
                    lo = 2 * d - 1
                    P.op("dve", I("tensor_tensor", out=dst[:, lo:528], in0=cur[:, lo:528], in1=cur[:, lo - d:528 - d], op=ALU.add), reads=curk, writes=dstk)
                    cur, curk = dst, dstk
                    bi ^= 1
                    d *= 2
                P.op("dve", I("scalar_tensor_tensor", out=AMX[:, ci, :], in0=cur[:, 16:528], scalar=1.0 / w, in1=src[:, 16:528], op0=ALU.mult, op1=ALU.subtract),
                     reads=curk + ek, writes=[sck(8 + ci)])
                if s == 0:
                    P.op("dve", I("tensor_tensor", out=TMP[4][:, 0:16], in0=cur[:, 16:32], in1=INVC[:, g * 16:(g + 1) * 16], op=ALU.mult), reads=curk + ["invc"], writes=[tk(4)])
                    P.op("dve", I("tensor_tensor", out=AMX[:, ci, 0:16], in0=TMP[4][:, 0:16], in1=src[:, 16:32], op=ALU.subtract), reads=[tk(4)] + ek, writes=[sck(8 + ci)])
            for g in range(4):
                banks = bank_set()
                si, sv = w_slab(w_pool, g * 4, 4, 0, 512)
                for j in range(4):
                    P.pe_group([I("matmul", PS[banks[j]][:], lhsT=sv[:, i, j * 128:(j + 1) * 128], rhs=AMX[:, 4 * g + i, :], start=(i == 0), stop=(i == 3)) for i in range(4)],
                               reads=[("ring", si)] + [sck(8 + 4 * g + i) for i in range(4)], writes=[pk(banks[j])])
                for j in range(4):
                    co = 4 * g + j
                    P.op("dve", I("tensor_scalar", out=AOUT[:, co, :], in0=PS[banks[j]][:], scalar1=G[:, G_PS + co:G_PS + co + 1], scalar2=None, op0=ALU.mult),
                         reads=[pk(banks[j]), "G"] + EALL, writes=[hk(co // 2)])

            for half in range(2):
                for gi in range(5):
                    pos0 = s * 2048 + gi * 512 if gi < 4 else SEQ
                    ck = ("Kc", s) if gi < 4 else "KcOwn"
                    if gi < 4:
                        src = recv_l[s][gi * 128:(gi + 1) * 128, 0:4 * T]
                        rk = ("recv_l", s)
                        srckr = recv_l[s][gi * 128:gi * 128 + 64, 4 * T:5 * T]
                    else:
                        src = send_l[s][:, 0:4 * T]
                        rk = ("send_l", s)
                        srckr = send_l[s][0:64, 4 * T:5 * T]
                    li = ring_load(lambda slot: [I("dma_start", out=slot[:, 0:4 * T], in_=src)], reads=[rk])
                    lat = RING[li][:, 0:4 * T].rearrange("p (k t) -> p k t", t=T)
                    wsi, wsv = w_slab(w_ukv, 0, 4, half * 2048, 2048)
                    if half == 0:
                        misc_dma(I("dma_start", out=Krc[:, pos0:pos0 + T], in_=srckr), reads=[rk], writes=[ck])
                    for hh in range(8):
                        h = half * 8 + hh
                        b = bank()
                        P.pe_group([I("matmul", PS[b][:], lhsT=wsv[:, kk, hh * 256:hh * 256 + 128], rhs=lat[:, kk, :], start=(kk == 0), stop=(kk == 3)) for kk in range(4)],
                                   reads=[("ring", wsi), ("ring", li)], writes=[pk(b)])
                        reserved.add(b)
                        tq, tr_ = (0, 1) if hh % 2 == 0 else (2, 3)
                        partition_norm_stats(b, 128, tq, tr_)
                        reserved.discard(b)
                        ki = hh % 5
                        kst = PTILE[ki] if ki < 3 else QN[ki - 3]
                        kstk = sck(24 + ki)
                        P.op("dve", I("scalar_tensor_tensor", out=kst, in0=PS[b][:], scalar=G[:, G_KN:G_KN + 1], in1=TMP[tr_], op0=ALU.mult, op1=ALU.mult),
                             reads=[pk(b), tk(tr_), "G"], writes=[kstk])
                        P.dma("sp", I("dma_start", out=Kc[h, :, pos0:pos0 + T], in_=kst), d_kst[ki], reads=[kstk], writes=[ck])
                    for tc in range(4):
                        for cg in range(2):
                            b = bank()
                            P.pe_group([I("matmul", PS[b][:], lhsT=lat[:, kk, tc * 128:(tc + 1) * 128],
                                          rhs=wsv[:, kk, :].rearrange("p (h c) -> p h c", c=256)[:, cg * 4:(cg + 1) * 4, 128:256], start=(kk == 0), stop=(kk == 3))
                                        for kk in range(4)], reads=[("ring", wsi), ("ring", li)], writes=[pk(b)])
                            vi = (tc * 2 + cg) % 2
                            vst = QR[vi]
                            vstk = sck(29 + vi)
                            P.op("act", I("copy", out=vst, in_=PS[b][:]), reads=[pk(b)], writes=[vstk])
                            h0 = half * 8 + cg * 4
                            kb = pos0 // 128 + tc
                            P.dma("sp", I("dma_start", out=Vc[h0:h0 + 4, :, kb, :].rearrange("h p d -> p h d"), in_=vst.rearrange("p (h d) -> p h d", d=128)),
                                  d_kst[5 + vi], reads=[vstk], writes=[ck])

            for i_ in range(2):
                P.op("dve", I("memset", QR[i_][64:128, :], 0.0), writes=[sck(29 + i_)])
            def prep_q(h):
                qi = h % 2
                si, sv = w_slab(w_uq, 0, 8, h * 192, 192)
                bn = bank()
                reserved.add(bn)
                P.pe_group([I("matmul", PS[bn][:], lhsT=sv[:, kk, 0:128], rhs=cqr[kk][0], start=(kk == 0), stop=(kk == 7)) for kk in range(8)],
                           reads=[("ring", si)] + [cqr[kk][1] for kk in range(8)], writes=[pk(bn)])
                br = bank()
                reserved.add(br)
                P.pe_group([I("matmul", PS[br][0:64, :], lhsT=sv[:, kk, 128:192], rhs=cqr[kk][0], start=(kk == 0), stop=(kk == 7)) for kk in range(8)],
                           reads=[("ring", si)] + [cqr[kk][1] for kk in range(8)], writes=[pk(br)])
                yield
                yield from pns_gen(bn, 128, 0, 1)
                P.op("dve", I("scalar_tensor_tensor", out=QN[qi], in0=PS[bn][:], scalar=GS[:, 0:1], in1=TMP[1], op0=ALU.mult, op1=ALU.mult),
                     reads=[pk(bn), tk(1), "GS"], writes=[sck(27 + qi)])
                reserved.discard(bn)
                yield
                yield from rope_gen(br, 1, 2, QR[qi][0:64, :], sck(29 + qi), GS, "GS")
                reserved.discard(br)

            run(prep_q(0))
            for h in range(NH):
                qi = h % 2
                bgq = prep_q(h + 1) if h + 1 < NH else None
                chunks = [(ci * 2048, 16, ("Kc", ci), ci == s) for ci in range(s + 1)] + [(SEQ, 4, "KcOwn", None)]
                blocks = []
                for (k0, nkb, ckey, cur) in chunks:
                    nk = nkb * 128
                    holder = {}

                    def loader(holder=holder, k0=k0, nk=nk, nkb=nkb, ckey=ckey, h=h):
                        if "li" not in holder:
                            holder["li"] = ring_load(lambda slot: [
                                I("dma_start", out=slot[:, 0:nk], in_=Kc[h, :, k0:k0 + nk]),
                                I("dma_start", out=slot[:, 2048:2048 + nk].rearrange("p (k d) -> p k d", d=128), in_=Vc[h, :, k0 // 128:k0 // 128 + nkb, :]),
                                I("dma_start", out=slot[0:64, 4096:4096 + nk], in_=Krc[:, k0:k0 + nk]),
                            ], reads=[ckey])
                        return holder["li"]
                    for kb in range(nkb):
                        bias = None
                        tri = None
                        if cur is None:
                            tri = kb
                        elif cur:
                            gi = kb // 4
                            bias = G[:, G_BIAS + gi:G_BIAS + gi + 1]
                        blocks.append(dict(loader=loader, bias=bias, tri=tri, kb=kb))

                def qk_fn(i, bs, blocks=blocks, qi=qi):
                    slot, kb = RING[blocks[i]["loader"]()], blocks[i]["kb"]
                    return [I("matmul", PS[bs][:], lhsT=slot[:, kb * 128:(kb + 1) * 128], rhs=QN[qi], start=True, stop=False),
                            I("matmul", PS[bs][:], lhsT=slot[:, 4096 + kb * 128:4096 + (kb + 1) * 128], rhs=QR[qi], start=False, stop=True)]

                def v_fn(i, blocks=blocks):
                    slot, kb = RING[blocks[i]["loader"]()], blocks[i]["kb"]
                    return slot[:, 2048 + kb * 128:2048 + (kb + 1) * 128]
                softmax_pv(blocks, qk_fn, v_fn, BOUT[:, h, :], hk(8 + h // 2), [sck(27 + qi), sck(29 + qi)], bg=bgq)
                if bgq is not None:
                    run(bgq)
            if s == 0 and debug:
                dump(4, HALL)

            arhs = [(AOUT[:, k, :], hk(k // 2)) for k in range(16)]
            brhs = [(BOUT[:, k, :], hk(8 + k // 2)) for k in range(16)]
            for mb in range(8):
                Bps = {}

                def evac_gp(j, ps, pkey):
                    P.op("act", I("activation", out=TMP[j], in_=ps[:], func=AF.Sigmoid), reads=[pkey], writes=[tk(j)])

                def evac_a(j, ps, pkey):
                    P.op("dve", I("tensor_tensor", out=TMP[j], in0=TMP[j], in1=ps[:], op=ALU.mult), reads=[pkey, tk(j)], writes=[tk(j)])

                def evac_b(j, ps, pkey, Bps=Bps):
                    Bps[j] = (ps, pkey)
                    return True

                def evac_gm(j, ps, pkey, Bps=Bps, mb=mb):
                    t = 4 + j % 2
                    P.op("act", I("activation", out=TMP[t], in_=ps[:], func=AF.Sigmoid), reads=[pkey], writes=[tk(t)])
                    P.op("dve", I("tensor_tensor", out=TMP[t], in0=TMP[t], in1=Bps[j][0][:], op=ALU.mult), reads=[Bps[j][1], tk(t)], writes=[tk(t)])
                    P.op("dve", I("tensor_tensor", out=ACT_[:, 4 * mb + j, :], in0=TMP[j], in1=TMP[t], op=ALU.add), reads=[tk(j), tk(t)], writes=[sck(4 * mb + j)])
                    reserved.discard(Bps[j][1][1])
                stream_linear(w_in, 0, Urhs, IN_GP + mb * 512, 512, evac_gp)
                stream_linear(w_pa, 0, arhs, mb * 512, 512, evac_a)
                stream_linear(w_pb, 0, brhs, mb * 512, 512, evac_b)
                stream_linear(w_in, 0, Urhs, IN_GM + mb * 512, 512, evac_gm)
            if s == 0:
                dumpb(0, SCALL)

            for mb in range(8):
                misc_dma(I("dma_start", out=Hf[:, mb * 2048:(mb + 1) * 2048], in_=hbuf[:, mb * 2048:(mb + 1) * 2048]), reads=["hbuf"], writes=[hk(4 * mb + i) for i in range(4)])
            mrg = [(ACT_[:, k, :], sck(k)) for k in range(DC)]

            def evac_o(m, ps, pkey):
                P.op("dve", I("tensor_tensor", out=H[:, m, :], in0=ps[:], in1=H[:, m, :], op=ALU.add), reads=[pkey, hk(m)], writes=[hk(m)])
            stream_linear(w_out, 0, mrg, 0, D, evac_o)
            if s == 0:
                dump(1, HALL)

            norm_full(G_X)

            def evac_xq(j, ps, pkey):
                partition_norm_stats(pkey[1], 128, 0, 1)
                P.op("dve", I("scalar_tensor_tensor", out=XQT[:, j, :], in0=ps[:], scalar=GS[:, 3:4], in1=TMP[1], op0=ALU.mult, op1=ALU.mult),
                     reads=[pkey, tk(1), "GS"], writes=[sck(j)])
            stream_linear(w_xq, 0, Urhs, 0, 512, evac_xq)
            for hd in range(4):
                blocks = [dict(reads=["XK", "XV"], bias=None, tri=None, mc=mc) for mc in range(2)]

                def qk_fn(i, bs, hd=hd):
                    return [I("matmul", PS[bs][:], lhsT=XK[:, hd, i * 128:(i + 1) * 128], rhs=XQT[:, hd, :], start=True, stop=True)]

                def v_fn(i, hd=hd):
                    return XV[:, i, hd * 128:(hd + 1) * 128]
                softmax_pv(blocks, qk_fn, v_fn, XOT[:, hd, :], sck(4 + hd), [sck(hd)])
            xor = [(XOT[:, k, :], sck(4 + k)) for k in range(4)]
            stream_linear(w_xo, 0, xor, 0, D, evac_o)
            if s == 0:
                dump(2, HALL)

            norm_full(G_F2)
            ffn(w_gu2, w_d2)
            if s == 0:
                dump(3, HALL)

            for tc in range(4):
                ost = XST[tc % 2]
                for cg in range(8):
                    b = bank()
                    P.pe_group([I("transpose", PS[b][:, i * 128:(i + 1) * 128], H[:, 4 * cg + i, tc * 128:(tc + 1) * 128], ident) for i in range(4)],
                               reads=[hk(4 * cg + i) for i in range(4)] + ["cst"], writes=[pk(b)])
                    uk = [sck(16 * (tc % 2) + 2 * cg), sck(16 * (tc % 2) + 2 * cg + 1)]
                    if cg % 2 == 0:
                        P.op("act", I("copy", out=ost[:, cg * 512:(cg + 1) * 512], in_=PS[b][:]), reads=[pk(b)], writes=uk)
                    else:
                        P.op("dve", I("tensor_copy", out=ost[:, cg * 512:(cg + 1) * 512], in_=PS[b][:]), reads=[pk(b)], writes=uk)
                P.dma("sp", I("dma_start", out=out_d[s * T + tc * 128: s * T + (tc + 1) * 128, :], in_=ost), d_o[tc % 2], reads=xstk(tc % 2), writes=["out"])

        P.wait_all("sp", ["out"] + ([("dbg", i) for i in range(NDBG)] + [("dbgb", i) for i in range(NDBG)] if debug else []))
        P.run_block()
        print("instr counts", P.ninstr, "nsem", P.nsem, flush=True)
    return nc


def _fm(v):
    v = np.asarray(v, np.float32).reshape(-1, 128)
    return np.ascontiguousarray(v.T)


def make_in_maps(inputs, NSTEP=4, dff=DFF):
    SEQ = NSTEP * 2048
    x = np.asarray(inputs["x"]); mem = np.asarray(inputs["mem"]); positions = np.asarray(inputs["positions"])
    sq = lambda k: np.ascontiguousarray(np.asarray(inputs[k])[0])
    weights = {k: sq(k) for k in ["ffn1_w_gu", "ffn1_w_down", "ffn2_w_gu", "ffn2_w_down", "w_in", "w_uq", "w_ukv", "w_branch_pool", "w_branch_mla",
                                  "w_out", "w_xq", "w_xkv", "w_xo"]}
    weights["w_pool"] = np.ascontiguousarray(np.asarray(inputs["w_pool"])[0].reshape(2048, 512))
    if dff != DFF:
        for f in ("ffn1", "ffn2"):
            wg = weights[f + "_w_gu"]
            weights[f + "_w_gu"] = np.ascontiguousarray(np.concatenate([wg[:, 0:dff], wg[:, DFF:DFF + dff]], axis=1))
            weights[f + "_w_down"] = np.ascontiguousarray(weights[f + "_w_down"][0:dff])
    gbase = np.zeros((128, NG), np.float32)
    gbase[:, G_F1:G_F1 + 32] = _fm(sq("ffn1_norm")); gbase[:, G_MIX:G_MIX + 32] = _fm(sq("mix_norm"))
    gbase[:, G_X:G_X + 32] = _fm(sq("x_norm")); gbase[:, G_F2:G_F2 + 32] = _fm(sq("ffn2_norm"))
    gbase[:, G_MEM:G_MEM + 32] = _fm(sq("mem_norm")); gbase[:, G_PS:G_PS + 16] = _fm(sq("pool_scale"))
    gbase[:, G_QL:G_QL + 8] = _fm(sq("q_latent_norm")); gbase[:, G_KVL:G_KVL + 4] = _fm(sq("kv_latent_norm"))
    gbase[:, G_QN] = sq("q_nope_norm"); gbase[:, G_KN] = sq("k_nope_norm"); gbase[:, G_XQ] = sq("xq_norm"); gbase[:, G_XK] = sq("xk_norm")
    qr = sq("q_rope_norm"); kr = sq("k_rope_norm")
    gbase[0:64, G_QR] = qr; gbase[0:64, G_QRS] = np.concatenate([qr[32:], qr[:32]])
    gbase[0:64, G_KR] = kr; gbase[0:64, G_KRS] = np.concatenate([kr[32:], kr[:32]])
    half = 32
    inv = (1.0 / (np.float32(10000.0) ** (np.arange(half, dtype=np.float32) * np.float32(2.0 / 64)))).astype(np.float32)
    gbase[0:64, G_INV] = np.concatenate([inv, inv])
    gbase[0:32, G_SGN] = -1.0; gbase[32:64, G_SGN] = 1.0
    cst = np.zeros((128, 192), np.float32)
    cst[:, 0:128] = np.eye(128, dtype=np.float32)
    for m in range(64):
        cst[(m + 32) % 64, 128 + m] = 1.0
    p_ = np.arange(128)[:, None]
    q_ = np.arange(T)[None, :]
    tri = np.concatenate([(j * 128 + p_ <= q_).astype(np.float32) for j in range(4)], axis=1).astype(ml_dtypes.bfloat16)
    in_maps = []
    for core in range(8):
        b, c = core // 4, core % 4
        g = gbase.copy()
        for r in range(4):
            g[:, G_BIAS + r] = 0.0 if r < c else NEG
            g[:, G_SEL + r] = 1.0 if (c >= 1 and r == c - 1) else 0.0
        g[:, G_SEL + 4] = 1.0 if c == 0 else 0.0
        invc = np.zeros((128, 64), np.float32)
        for gg in range(4):
            w = 2 << gg
            t = np.arange(16)
            invc[:, gg * 16:(gg + 1) * 16] = (1.0 / np.minimum(t + 1, w)) if c == 0 else (1.0 / w)
        rows = np.concatenate([np.arange(s * 2048 + c * 512, s * 2048 + c * 512 + 512) for s in range(NSTEP)])
        m = {"x": np.ascontiguousarray(x[b, rows, :]), "pos": np.ascontiguousarray(positions[b, rows].reshape(NSTEP, T).astype(np.int32)),
             "mem": np.ascontiguousarray(mem[b]), "gpack": g, "cst": cst, "tri": tri, "invc": invc}
        m.update(weights)
        in_maps.append(m)
    return in_maps


def assemble(results, NSTEP=4):
    SEQ = NSTEP * 2048
    out = np.zeros((2, SEQ, D), np.float32)
    for core in range(8):
        b, c = core // 4, core % 4
        o = results[core]["out"]
        for s in range(NSTEP):
            out[b, s * 2048 + c * 512: s * 2048 + c * 512 + 512, :] = o[s * T:(s + 1) * T]
    return out


_NC_CACHE = {}


def kernel(**inputs):
    NSTEP = 4
    if NSTEP not in _NC_CACHE:
        _NC_CACHE[NSTEP] = build(NSTEP)
    nc = _NC_CACHE[NSTEP]
    in_maps = make_in_maps(inputs, NSTEP)
    res = run_bass_kernel_spmd(nc, in_maps, core_ids=list(range(8)))
    return assemble(res.results, NSTEP)
```

```python
import numpy as np
import ml_dtypes
import concourse.bass as bass
import concourse.mybir as mybir
from concourse.bass_utils import run_bass_kernel_spmd
from contextlib import ExitStack

F32 = mybir.dt.float32
BF16 = mybir.dt.bfloat16
I32 = mybir.dt.int32
AF = mybir.ActivationFunctionType
ALU = mybir.AluOpType

ENGS = ("pe", "act", "dve", "pool", "sp")

D = 4096
DC = 32
DFF = 11008
T = 512
NH = 16
EPS = 1e-6
IN_POOL, IN_Q, IN_KV, IN_KR, IN_GP, IN_GM = 0, 2048, 3072, 3584, 3648, 3648 + 4096
G_F1, G_MIX, G_X, G_F2, G_MEM, G_PS, G_QL, G_KVL = 0, 32, 64, 96, 128, 160, 176, 184
G_QN, G_KN, G_XQ, G_XK, G_QR, G_QRS, G_KR, G_KRS, G_INV, G_SGN, G_BIAS, G_SEL = 188, 189, 190, 191, 192, 193, 194, 195, 196, 197, 198, 202
NG = 208
NEG = -30000.0
TWO_PI = 6.283185307179586
C1 = 6.28125
C2 = TWO_PI - C1
PI_SAFE = 3.1415925


class Sem:
    __slots__ = ("h", "count", "name")

    def __init__(self, h, name):
        self.h = h
        self.count = 0
        self.name = name


class Prog:
    def __init__(self, nc, es):
        self.nc = nc
        self.es = es
        self.ops = {e: [] for e in ENGS}
        self.prog_sem = {}
        self.waited = {e: {} for e in ENGS}
        self.W = {}
        self.R = {}
        self.nsem = 0
        self.ninstr = {e: 0 for e in ENGS}
        for e in ENGS:
            self.prog_sem[e] = self.new_sem("p_" + e)

    def new_sem(self, name):
        self.nsem += 1
        h = self.es.enter_context(self.nc.semaphore(f"{name}_{self.nsem}"))
        return Sem(h, name)

    def new_epoch(self):
        for e in ENGS:
            self.prog_sem[e] = self.new_sem("p_" + e)

    def _deps(self, eng, reads, writes, skip_sem=None, extra=None):
        need = {}
        if extra:
            for s, v in extra:
                if need.get(s, 0) < v:
                    need[s] = v
        for k in reads:
            w = self.W.get(k)
            if w:
                for s, v in w.items():
                    if need.get(s, 0) < v:
                        need[s] = v
        for k in writes:
            for d in (self.W.get(k), self.R.get(k)):
                if d:
                    for s, v in d.items():
                        if need.get(s, 0) < v:
                            need[s] = v
        wd = self.waited[eng]
        out = []
        for s, v in need.items():
            if s is skip_sem:
                continue
            if wd.get(s, 0) < v:
                wd[s] = v
                out.append((s, v))
        return out

    def _record(self, reads, writes, sem, val):
        for k in reads:
            d = self.R.get(k)
            if d is None:
                d = self.R[k] = {}
            if d.get(sem, 0) < val:
                d[sem] = val
        for k in writes:
            d = self.W.get(k)
            if d is None:
                d = self.W[k] = {}
            if d.get(sem, 0) < val:
                d[sem] = val

    def op(self, eng, ins, reads=(), writes=()):
        ps = self.prog_sem[eng]
        waits = self._deps(eng, reads, writes, skip_sem=ps if eng == "pe" else None)
        ps.count += 1
        self._record(reads, writes, ps, ps.count)
        self.ops[eng].append((waits, ins, ps, 1))
        self.ninstr[eng] += 1

    def pe_group(self, inss, reads=(), writes=()):
        ps = self.prog_sem["pe"]
        waits = self._deps("pe", reads, writes, skip_sem=ps)
        ps.count += 1
        self._record(reads, writes, ps, ps.count)
        lst = self.ops["pe"]
        n = len(inss)
        for i, ins in enumerate(inss):
            lst.append((waits if i == 0 else (), ins, ps if i == n - 1 else None, 1))
        self.ninstr["pe"] += n

    def dma(self, eng, ins, dsem, reads=(), writes=(), inc=16, guard=False):
        extra = [(dsem, dsem.count)] if (guard and dsem.count > 0) else None
        waits = self._deps(eng, reads, writes, extra=extra)
        dsem.count += inc
        self._record(reads, writes, dsem, dsem.count)
        self.ops[eng].append((waits, ins, dsem, inc))
        self.ninstr[eng] += 1

    def wait_all(self, eng, keys):
        waits = self._deps(eng, keys, ())
        self.ops[eng].append((waits, None, None, 0))

    def replay(self, eng, e):
        for waits, ins, sem, inc in self.ops[eng]:
            for s, v in waits:
                e.wait_ge(s.h, v)
            if ins is None:
                continue
            meth, args, kw = ins
            r = getattr(e, meth)(*args, **kw)
            if sem is not None:
                r.then_inc(sem.h, inc)

    def run_block(self):
        nc = self.nc
        with nc.Block() as block:
            @block.tensor
            def _(e):
                self.replay("pe", e)

            @block.scalar
            def _(e):
                self.replay("act", e)

            @block.vector
            def _(e):
                self.replay("dve", e)

            @block.gpsimd
            def _(e):
                self.replay("pool", e)

            @block.sync
            def _(e):
                self.replay("sp", e)


def I(meth, *args, **kw):
    return (meth, args, kw)


def build(NSTEP=4, debug=False, dff=DFF):
    SEQ = NSTEP * 2048
    KTOT = SEQ + 512
    NKB = KTOT // 128
    nc = bass.Bass("TRN2", target_bir_lowering=False)
    din = lambda name, shape, dt=F32: nc.dram_tensor(name, shape, dt, kind="ExternalInput").ap()
    dint = lambda name, shape, dt=F32: nc.dram_tensor(name, shape, dt).ap()
    x_d = din("x", [NSTEP * T, D])
    pos_d = din("pos", [NSTEP, T], I32)
    mem_d = din("mem", [256, D])
    g_d = din("gpack", [128, NG])
    cst_d = din("cst", [128, 192])
    tri_d = din("tri", [128, 4 * T], BF16)
    invc_d = din("invc", [128, 64])
    w_gu1 = din("ffn1_w_gu", [D, 2 * dff]); w_d1 = din("ffn1_w_down", [dff, D])
    w_gu2 = din("ffn2_w_gu", [D, 2 * dff]); w_d2 = din("ffn2_w_down", [dff, D])
    w_in = din("w_in", [D, 11840]); w_pool = din("w_pool", [2048, 512])
    w_uq = din("w_uq", [1024, 3072]); w_ukv = din("w_ukv", [512, 4096])
    w_pa = din("w_branch_pool", [2048, D]); w_pb = din("w_branch_mla", [2048, D])
    w_out = din("w_out", [D, D]); w_xq = din("w_xq", [D, 512]); w_xkv = din("w_xkv", [D, 1024]); w_xo = din("w_xo", [512, D])
    out_d = nc.dram_tensor("out", [NSTEP * T, D], F32, kind="ExternalOutput").ap()
    NDBG = 6
    dbg_d = nc.dram_tensor("dbg", [NDBG, 128, DC * T], F32, kind="ExternalOutput").ap() if debug else None
    dbgb_d = nc.dram_tensor("dbgb", [NDBG, 128, 16384], BF16, kind="ExternalOutput").ap() if debug else None
    hbuf = dint("hbuf", [128, DC * T])
    send_l = [dint(f"send_l{s}", [128, 5 * T], BF16) for s in range(NSTEP)]
    recv_l = [dint(f"recv_l{s}", [4 * 128, 5 * T], BF16) for s in range(NSTEP)]
    send_t = [dint(f"send_t{s}", [128, 256]) for s in range(NSTEP)]
    recv_t = [dint(f"recv_t{s}", [4 * 128, 256]) for s in range(NSTEP)]
    Kc = dint("Kc", [NH, 128, KTOT], BF16)
    Krc = dint("Krc", [64, KTOT], BF16)
    Vc = dint("Vc", [NH, 128, NKB, 128], BF16)

    with ExitStack() as es:
        P = Prog(nc, es)
        sb = lambda name, shape, dt: es.enter_context(nc.sbuf_tensor(name, shape, dt))
        H = sb("H", [128, DC, T], F32)
        U = sb("U", [128, DC, T], BF16)
        SC = sb("SC", [128, 16384], BF16)
        RING = [sb(f"ring{i}", [128, 8192], BF16) for i in range(3)]
        G = sb("G", [128, NG], F32)
        CST = sb("CST", [128, 192], F32)
        ONESF = sb("ONESF", [128, 128], F32)
        ONESB = sb("ONESB", [128, 128], BF16)
        TRI = sb("TRI", [128, 4, T], BF16)
        XK = sb("XK", [128, 4, 256], BF16)
        XV = sb("XV", [128, 2, 512], BF16)
        COSD = sb("COSD", [128, T], F32)
        SIND = sb("SIND", [128, T], F32)
        TMPA = sb("TMPA", [128, 6 * T], F32)
        PT = sb("PT", [128, 256], F32)
        INVC = sb("INVC", [128, 64], F32)
        GS = sb("GS", [128, 4], F32)
        PS = [es.enter_context(nc.psum_tensor(f"ps{i}", [128, T], F32)) for i in range(8)]
        ident = CST[:, 0:128]
        perm = CST[0:64, 128:192]
        TMP = [TMPA[:, i * T:(i + 1) * T] for i in range(6)]
        TMPb = [TMPA[:, i * T:(i + 1) * T].bitcast(BF16)[:, 0:T] for i in range(6)]
        tk = lambda i: ("tmp", i)
        Hf = H[:].rearrange("p a b -> p (a b)")
        E = Hf[:, 0:16 * 528].rearrange("p (c t) -> p c t", t=528)
        AOUT = Hf[:, 0:4096].bitcast(BF16).rearrange("p (c t) -> p c t", t=T)
        BOUT = Hf[:, 4096:8192].bitcast(BF16).rearrange("p (c t) -> p c t", t=T)
        ZQ = H[:, 17:25, :]
        hk = lambda c: ("H", c)
        HALL = [hk(i) for i in range(DC)]

        def ekeys(c):
            lo, hi = c * 2112, (c + 1) * 2112 - 1
            return [hk(i) for i in range(lo // 2048, hi // 2048 + 1)]
        EALL = [hk(i) for i in range(17)]
        SCf = SC[:].bitcast(F32)
        ACT_ = SC[:].rearrange("p (c t) -> p c t", t=T)
        sck = lambda i: ("sc", i)
        SCALL = [sck(i) for i in range(32)]
        CQ = SC[:, 0:4096].rearrange("p (c t) -> p c t", t=T)
        AMX = SC[:, 4096:12288].rearrange("p (c t) -> p c t", t=T)
        SENDT = SC[:, 4096:4096 + 2560].rearrange("p (c t) -> p c t", t=T)
        PTILE = [SC[:, 12288 + i * T: 12288 + (i + 1) * T] for i in range(3)]
        QN = [SC[:, 13824 + i * T: 13824 + (i + 1) * T] for i in range(2)]
        QR = [SC[:, 14848 + i * T: 14848 + (i + 1) * T] for i in range(2)]
        GT = SCf[:, 6144:7168].rearrange("p (r n) -> p r n", n=256)
        XQT = SC[:, 0:2048].rearrange("p (c t) -> p c t", t=T)
        XOT = SC[:, 2048:4096].rearrange("p (c t) -> p c t", t=T)
        XST = [SCf[:, i * 4096:(i + 1) * 4096] for i in range(2)]
        xstk = lambda i: [sck(j) for j in range(16 * i, 16 * i + 16)]

        st = {"bank": 0, "set": 0, "ring": 0, "misc": 0}
        reserved = set()
        ring_sem = [P.new_sem(f"ring{i}") for i in range(3)]
        d_miscs = [P.new_sem(f"d_misc{i}") for i in range(10)]
        d_x = [P.new_sem("d_x0"), P.new_sem("d_x1")]
        d_o = [P.new_sem("d_o0"), P.new_sem("d_o1")]
        d_kst = [P.new_sem(f"d_kst{i}") for i in range(7)]
        d_ccl = P.new_sem("d_ccl")
        d_cct = P.new_sem("d_cct")

        def misc_dma(ins, reads=(), writes=(), eng="sp"):
            i = st["misc"]
            st["misc"] = (i + 1) % len(d_miscs)
            P.dma(eng, ins, d_miscs[i], reads=reads, writes=writes, guard=True)

        def bank():
            while True:
                b = st["bank"]
                st["bank"] = (b + 1) % 8
                if b not in reserved:
                    return b

        def bank_set():
            for _ in range(2):
                s_ = st["set"]
                st["set"] ^= 1
                bs = [4 * s_ + j for j in range(4)]
                if not (set(bs) & reserved):
                    return bs
            raise RuntimeError("no free bank set")

        pk = lambda b: ("ps", b)

        def ring_load(inss_fn, reads=()):
            i = st["ring"]
            st["ring"] = (i + 1) % 3
            for ins in inss_fn(RING[i]):
                P.dma("pool", ins, ring_sem[i], reads=reads, writes=[("ring", i)])
            return i

        def w_slab(w_ap, k0, nk, c0, ncols):
            src = w_ap[k0 * 128:(k0 + nk) * 128, c0:c0 + ncols].rearrange("(kc p) n -> p kc n", p=128)
            i = ring_load(lambda slot: [I("dma_start", out=slot[:, 0:nk * ncols].rearrange("p (k n) -> p k n", n=ncols), in_=src)])
            return i, RING[i][:, 0:nk * ncols].rearrange("p (k n) -> p k n", n=ncols)

        def stream_linear(w_ap, k0w, rhs, col0, ncols, evac, ntok=T):
            nk = len(rhs)
            for cb in range(0, ncols, 512):
                nb = min(512, ncols - cb)
                nj = nb // 128
                banks = bank_set()
                reserved.update(banks[0:nj])
                for s0 in range(0, nk, 16):
                    ns = min(16, nk - s0)
                    si, sv = w_slab(w_ap, k0w + s0, ns, col0 + cb, nb)
                    for j in range(nj):
                        b = banks[j]
                        inss = [I("matmul", PS[b][:, 0:ntok], lhsT=sv[:, kk, j * 128:(j + 1) * 128], rhs=rhs[s0 + kk][0],
                                  start=(s0 + kk == 0), stop=(s0 + kk == nk - 1)) for kk in range(ns)]
                        P.pe_group(inss, reads=[("ring", si)] + [rhs[s0 + kk][1] for kk in range(ns)], writes=[pk(b)])
                for j in range(nj):
                    keep = evac(cb // 128 + j, PS[banks[j]], pk(banks[j]))
                    if not keep:
                        reserved.discard(banks[j])

        def rstd_from(ps_b, nfeat, t_out, rows=128, n=T):
            P.op("act", I("activation", out=TMP[t_out][0:rows, 0:n], in_=PS[ps_b][0:rows, 0:n], func=AF.Sqrt, scale=1.0 / nfeat, bias=EPS),
                 reads=[pk(ps_b)], writes=[tk(t_out)])
            P.op("dve", I("reciprocal", out=TMP[t_out][0:rows, 0:n], in_=TMP[t_out][0:rows, 0:n]), reads=[tk(t_out)], writes=[tk(t_out)])

        def norm_full(gcol):
            b = bank()
            reserved.add(b)
            for ci in range(DC):
                t = ci % 2
                P.op("act", I("activation", out=TMPb[t], in_=H[:, ci, :], func=AF.Square), reads=[hk(ci)], writes=[tk(t)])
                P.pe_group([I("matmul", PS[b][:], lhsT=ONESB[:], rhs=TMPb[t], start=(ci == 0), stop=(ci == DC - 1))],
                           reads=[tk(t), "onesb"], writes=[pk(b)])
            rstd_from(b, D, 2)
            reserved.discard(b)
            for ci in range(DC):
                P.op("dve", I("scalar_tensor_tensor", out=U[:, ci, :], in0=H[:, ci, :], scalar=G[:, gcol + ci:gcol + ci + 1], in1=TMP[2],
                              op0=ALU.mult, op1=ALU.mult), reads=[hk(ci), tk(2), "G"], writes=[("U", ci)])

        Urhs = [(U[:, ci, :], ("U", ci)) for ci in range(DC)]

        def ffn(w_gu, w_d):
            nch = dff // 128
            parts = [(i, min(i + 32, nch)) for i in range(0, nch, 32)]
            for (c0, c1) in parts:
                for cb in range(c0 * 128, c1 * 128, 512):
                    nb = min(512, c1 * 128 - cb)

                    def evac_g(j, ps, pkey):
                        P.op("act", I("activation", out=TMP[j], in_=ps[:], func=AF.Silu), reads=[pkey], writes=[tk(j)])

                    def evac_u(j, ps, pkey, cb=cb, c0=c0):
                        a = cb // 128 + j - c0
                        P.op("dve", I("tensor_tensor", out=ACT_[:, a, :], in0=TMP[j], in1=ps[:], op=ALU.mult), reads=[pkey, tk(j)], writes=[sck(a)])
                    stream_linear(w_gu, 0, Urhs, cb, nb, evac_g)
                    stream_linear(w_gu, 0, Urhs, dff + cb, nb, evac_u)
                arhs_ = [(ACT_[:, a, :], sck(a)) for a in range(c1 - c0)]

                def evac_d(m, ps, pkey):
                    P.op("dve", I("scalar_tensor_tensor", out=H[:, m, :], in0=ps[:], scalar=0.5, in1=H[:, m, :], op0=ALU.mult, op1=ALU.add),
                         reads=[pkey, hk(m)], writes=[hk(m)])
                stream_linear(w_d, c0, arhs_, 0, D, evac_d)

        def run(gen):
            for _ in gen:
                pass

        def pns_gen(ps_b, nrows, t_sq, t_r, n=T):
            P.op("act", I("activation", out=TMPb[t_sq][0:nrows, 0:n], in_=PS[ps_b][0:nrows, 0:n], func=AF.Square), reads=[pk(ps_b)], writes=[tk(t_sq)])
            yield
            b2 = bank()
            reserved.add(b2)
            P.pe_group([I("matmul", PS[b2][0:nrows, 0:n], lhsT=ONESB[0:nrows, 0:nrows], rhs=TMPb[t_sq][0:nrows, 0:n], start=True, stop=True)],
                       reads=[tk(t_sq), "onesb"], writes=[pk(b2)])
            yield
            P.op("act", I("activation", out=TMP[t_r][0:nrows, 0:n], in_=PS[b2][0:nrows, 0:n], func=AF.Sqrt, scale=1.0 / nrows, bias=EPS),
                 reads=[pk(b2)], writes=[tk(t_r)])
            reserved.discard(b2)
            yield
            P.op("dve", I("reciprocal", out=TMP[t_r][0:nrows, 0:n], in_=TMP[t_r][0:nrows, 0:n]), reads=[tk(t_r)], writes=[tk(t_r)])
            yield

        def partition_norm_stats(ps_b, nrows, t_sq, t_r, n=T):
            run(pns_gen(ps_b, nrows, t_sq, t_r, n))

        def rope_gen(ps_b, gcol, gscol, out_ap, out_key, gtile, gkey):
            P.op("act", I("copy", out=TMP[0][0:64, :], in_=PS[ps_b][0:64, :]), reads=[pk(ps_b)], writes=[tk(0)])
            yield
            b_sw = bank()
            reserved.add(b_sw)
            P.pe_group([I("matmul", PS[b_sw][0:64, :], lhsT=perm, rhs=TMP[0][0:64, :], start=True, stop=True)], reads=[tk(0), "cst"], writes=[pk(b_sw)])
            yield
            yield from pns_gen(ps_b, 64, 1, 2)
            P.op("dve", I("scalar_tensor_tensor", out=TMP[3][0:64, :], in0=TMP[0][0:64, :], scalar=gtile[0:64, gcol:gcol + 1], in1=COSD[0:64, :],
                          op0=ALU.mult, op1=ALU.mult), reads=[tk(0), "rope", gkey], writes=[tk(3)])
            yield
            P.op("dve", I("scalar_tensor_tensor", out=TMP[4][0:64, :], in0=PS[b_sw][0:64, :], scalar=gtile[0:64, gscol:gscol + 1], in1=SIND[0:64, :],
                          op0=ALU.mult, op1=ALU.mult), reads=[pk(b_sw), "rope", gkey], writes=[tk(4)])
            reserved.discard(b_sw)
            yield
            P.op("dve", I("tensor_tensor", out=TMP[3][0:64, :], in0=TMP[3][0:64, :], in1=TMP[4][0:64, :], op=ALU.add), reads=[tk(3), tk(4)], writes=[tk(3)])
            yield
            P.op("dve", I("tensor_tensor", out=out_ap, in0=TMP[3][0:64, :], in1=TMP[2][0:64, :], op=ALU.mult), reads=[tk(3), tk(2)], writes=[out_key])
            yield

        def rope_norm(ps_b, gcol, gscol, out_ap, out_key, gtile, gkey):
            run(rope_gen(ps_b, gcol, gscol, out_ap, out_key, gtile, gkey))

        def dump(slot, keys):
            if debug:
                misc_dma(I("dma_start", out=dbg_d[slot], in_=Hf), reads=keys, writes=[("dbg", slot)])

        def dumpb(slot, keys):
            if debug:
                misc_dma(I("dma_start", out=dbgb_d[slot], in_=SC[:]), reads=keys, writes=[("dbgb", slot)])

        def softmax_pv(blocks, qk_fn, v_fn, out_ap, out_key, extra_reads, bg=None):
            bo, bd = bank(), bank()
            reserved.add(bo); reserved.add(bd)
            nb_ = len(blocks)
            pend = []
            for i in range(nb_ + 2):
                if i < nb_:
                    blk = blocks[i]
                    if "loader" in blk:
                        blk["reads"] = [("ring", blk["loader"]())]
                    bs = bank()
                    pt = i % 3
                    P.pe_group(qk_fn(i, bs), reads=blk["reads"] + extra_reads, writes=[pk(bs)])
                    if blk["bias"] is not None:
                        P.op("act", I("activation", out=PTILE[pt], in_=PS[bs][:], func=AF.Exp, bias=blk["bias"], scale=1.0), reads=[pk(bs), "G"], writes=[sck(24 + pt)])
                    else:
                        P.op("act", I("activation", out=PTILE[pt], in_=PS[bs][:], func=AF.Exp), reads=[pk(bs)], writes=[sck(24 + pt)])
                    if blk["tri"] is not None:
                        P.op("dve", I("tensor_tensor", out=PTILE[pt], in0=PTILE[pt], in1=TRI[:, blk["tri"], :], op=ALU.mult),
                             reads=[sck(24 + pt), "tri"], writes=[sck(24 + pt)])
                if len(pend) >= 2 or (i >= nb_ and pend):
                    pi, ppt = pend.pop(0)
                    P.pe_group([I("matmul", PS[bo][:], lhsT=v_fn(pi), rhs=PTILE[ppt], start=(pi == 0), stop=(pi == nb_ - 1)),
                                I("matmul", PS[bd][:], lhsT=ONESB[:], rhs=PTILE[ppt], start=(pi == 0), stop=(pi == nb_ - 1))],
                               reads=blocks[pi]["reads"] + [sck(24 + ppt), "onesb"], writes=[pk(bo), pk(bd)])
                if i < nb_:
                    pend.append((i, pt))
                if bg is not None:
                    next(bg, None)
            assert not pend
            P.op("dve", I("reciprocal", out=TMP[5], in_=PS[bd][:]), reads=[pk(bd)], writes=[tk(5)])
            P.op("dve", I("tensor_tensor", out=out_ap, in0=PS[bo][:], in1=TMP[5], op=ALU.mult), reads=[pk(bo), tk(5)], writes=[out_key])
            reserved.discard(bo); reserved.discard(bd)

        misc_dma(I("dma_start", out=G[:], in_=g_d[:, :]), writes=["G"])
        misc_dma(I("dma_start", out=CST[:], in_=cst_d[:, :]), writes=["cst"])
        misc_dma(I("dma_start", out=TRI[:].rearrange("p a b -> p (a b)"), in_=tri_d[:, :]), writes=["tri"])
        misc_dma(I("dma_start", out=INVC[:], in_=invc_d[:, :]), writes=["invc"])
        P.op("dve", I("memset", ONESF[:], 1.0), writes=["onesf"])
        P.op("dve", I("memset", ONESB[:], 1.0), writes=["onesb"])
        P.op("dve", I("memset", PT[:], 0.0), writes=["PT"])
        sc_q = 192.0 ** -0.5
        P.op("dve", I("tensor_scalar", out=GS[:, 0:1], in0=G[:, G_QN:G_QN + 1], scalar1=sc_q, scalar2=None, op0=ALU.mult), reads=["G"], writes=["GS"])
        P.op("dve", I("tensor_scalar", out=GS[:, 1:3], in0=G[:, G_QR:G_QR + 2], scalar1=sc_q, scalar2=None, op0=ALU.mult), reads=["G"], writes=["GS"])
        P.op("dve", I("tensor_scalar", out=GS[:, 3:4], in0=G[:, G_XQ:G_XQ + 1], scalar1=128.0 ** -0.5, scalar2=None, op0=ALU.mult), reads=["G"], writes=["GS"])

        MNT = SC[:, 0:8192].rearrange("p (c t) -> p c t", t=256)
        MST = SCf[:, 4096:8192].rearrange("p (a f) -> p a f", f=2048)
        mstk = [sck(j) for j in range(16, 32)]
        bss = bank()
        reserved.add(bss)
        for pas in range(2):
            for fg in range(2):
                P.dma("sp", I("dma_start", out=MST, in_=mem_d[:, fg * 2048:(fg + 1) * 2048].rearrange("(a p) f -> p a f", p=128)), d_x[0], writes=mstk)
                for cl in range(16):
                    ci = fg * 16 + cl
                    b = bank()
                    P.pe_group([I("transpose", PS[b][:, a * 128:(a + 1) * 128], MST[:, a, cl * 128:(cl + 1) * 128], ident) for a in range(2)],
                               reads=mstk + ["cst"], writes=[pk(b)])
                    if pas == 0:
                        t = ci % 2
                        P.op("act", I("activation", out=TMPb[t][:, 0:256], in_=PS[b][:, 0:256], func=AF.Square), reads=[pk(b)], writes=[tk(t)])
                        P.pe_group([I("matmul", PS[bss][:, 0:256], lhsT=ONESB[:], rhs=TMPb[t][:, 0:256], start=(ci == 0), stop=(ci == DC - 1))],
                                   reads=[tk(t), "onesb"], writes=[pk(bss)])
                    else:
                        P.op("dve", I("scalar_tensor_tensor", out=MNT[:, ci, :], in0=PS[b][:, 0:256], scalar=G[:, G_MEM + ci:G_MEM + ci + 1],
                                      in1=TMP[2][:, 0:256], op0=ALU.mult, op1=ALU.mult), reads=[pk(b), tk(2), "G"], writes=[sck(ci // 2)])
            if pas == 0:
                rstd_from(bss, D, 2, n=256)
                reserved.discard(bss)
        mrhs = [(MNT[:, ci, :], sck(ci // 2)) for ci in range(DC)]

        def evac_xk(j, ps, pkey):
            partition_norm_stats(pkey[1], 128, 0, 1, n=256)
            P.op("dve", I("scalar_tensor_tensor", out=XK[:, j, :], in0=ps[:, 0:256], scalar=G[:, G_XK:G_XK + 1], in1=TMP[1][:, 0:256],
                          op0=ALU.mult, op1=ALU.mult), reads=[pkey, tk(1), "G"], writes=["XK"])
        stream_linear(w_xkv, 0, mrhs, 0, 512, evac_xk, ntok=256)
        banks = bank_set()
        for s0 in (0, 16):
            si, sv = w_slab(w_xkv, s0, 16, 512, 512)
            for a in range(2):
                P.pe_group([I("matmul", PS[banks[a]][:], lhsT=mrhs[s0 + kk][0][:, a * 128:(a + 1) * 128], rhs=sv[:, kk, :],
                              start=(s0 + kk == 0), stop=(s0 + kk == DC - 1)) for kk in range(16)],
                           reads=[("ring", si)] + [mrhs[s0 + kk][1] for kk in range(16)], writes=[pk(banks[a])])
        for a in range(2):
            P.op("act", I("copy", out=XV[:, a, :], in_=PS[banks[a]][:]), reads=[pk(banks[a])], writes=["XV"])

        rg = [[0, 1, 2, 3], [4, 5, 6, 7]]
        for s in range(NSTEP):
            P.new_epoch()
            for tc in range(4):
                xs = XST[tc % 2]
                P.dma("sp", I("dma_start", out=xs, in_=x_d[s * T + tc * 128: s * T + (tc + 1) * 128, :]), d_x[tc % 2], writes=xstk(tc % 2))
                for cg in range(8):
                    b = bank()
                    P.pe_group([I("transpose", PS[b][:, i * 128:(i + 1) * 128], xs[:, (4 * cg + i) * 128:(4 * cg + i + 1) * 128], ident) for i in range(4)],
                               reads=xstk(tc % 2) + ["cst"], writes=[pk(b)])
                    dst = H[:, 4 * cg:4 * cg + 4, tc * 128:(tc + 1) * 128]
                    srcv = PS[b][:].rearrange("p (i t) -> p i t", t=128)
                    if cg % 2 == 0:
                        P.op("act", I("copy", out=dst, in_=srcv), reads=[pk(b)], writes=[hk(4 * cg + i) for i in range(4)])
                    else:
                        P.op("dve", I("tensor_copy", out=dst, in_=srcv), reads=[pk(b)], writes=[hk(4 * cg + i) for i in range(4)])
            POSI = TMPA[:, 5 * T:6 * T].bitcast(I32)
            KI = TMPA[:, 4 * T:5 * T].bitcast(I32)
            misc_dma(I("dma_start", out=POSI[0:64, :], in_=pos_d[s].partition_broadcast(64)), writes=[tk(5)])
            P.op("dve", I("tensor_copy", out=TMP[0][0:64, :], in_=POSI[0:64, :]), reads=[tk(5)], writes=[tk(0)])
            P.op("dve", I("tensor_scalar", out=TMP[0][0:64, :], in0=TMP[0][0:64, :], scalar1=G[0:64, G_INV:G_INV + 1], scalar2=None, op0=ALU.mult),
                 reads=[tk(0), "G"], writes=[tk(0)])
            t1, t2 = TMP[1][0:64, :], TMP[2][0:64, :]
            for which, dst in ((0, SIND), (1, COSD)):
                shift = 0.0 if which == 0 else float(np.pi / 2)
                P.op("dve", I("tensor_scalar", out=t1, in0=TMP[0][0:64, :], scalar1=shift, scalar2=None, op0=ALU.add), reads=[tk(0)], writes=[tk(1)])
                P.op("dve", I("tensor_scalar", out=KI[0:64, :], in0=t1, scalar1=1.0 / TWO_PI, scalar2=None, op0=ALU.mult), reads=[tk(1)], writes=[tk(4)])
                P.op("dve", I("tensor_copy", out=t2, in_=KI[0:64, :]), reads=[tk(4)], writes=[tk(2)])
                P.op("dve", I("scalar_tensor_tensor", out=t1, in0=t2, scalar=-C1, in1=t1, op0=ALU.mult, op1=ALU.add), reads=[tk(1), tk(2)], writes=[tk(1)])
                P.op("dve", I("scalar_tensor_tensor", out=t1, in0=t2, scalar=-C2, in1=t1, op0=ALU.mult, op1=ALU.add), reads=[tk(1), tk(2)], writes=[tk(1)])
                P.op("dve", I("tensor_scalar", out=t2, in0=t1, scalar1=float(np.pi), scalar2=-TWO_PI, op0=ALU.is_gt, op1=ALU.mult), reads=[tk(1)], writes=[tk(2)])
                P.op("dve", I("tensor_tensor", out=t1, in0=t1, in1=t2, op=ALU.add), reads=[tk(1), tk(2)], writes=[tk(1)])
                P.op("dve", I("tensor_scalar", out=t2, in0=t1, scalar1=float(-np.pi), scalar2=TWO_PI, op0=ALU.is_lt, op1=ALU.mult), reads=[tk(1)], writes=[tk(2)])
                P.op("dve", I("tensor_tensor", out=t1, in0=t1, in1=t2, op=ALU.add), reads=[tk(1), tk(2)], writes=[tk(1)])
                P.op("dve", I("tensor_scalar", out=t1, in0=t1, scalar1=-PI_SAFE, scalar2=PI_SAFE, op0=ALU.max, op1=ALU.min), reads=[tk(1)], writes=[tk(1)])
                P.op("act", I("activation", out=dst[0:64, :], in_=t1, func=AF.Sin), reads=[tk(1)], writes=["rope"])
            P.op("dve", I("tensor_scalar", out=SIND[0:64, :], in0=SIND[0:64, :], scalar1=G[0:64, G_SGN:G_SGN + 1], scalar2=None, op0=ALU.mult),
                 reads=["rope", "G"], writes=["rope"])

            norm_full(G_F1)
            ffn(w_gu1, w_d1)
            if s == 0:
                dump(0, HALL)
            norm_full(G_MIX)
            misc_dma(I("dma_start", out=hbuf[:, :], in_=Hf), reads=HALL, writes=["hbuf"])

            def evac_pool(j, ps, pkey):
                P.op("act", I("copy", out=E[:, j, 16:528], in_=ps[:]), reads=[pkey], writes=ekeys(j))
            stream_linear(w_in, 0, Urhs, IN_POOL, 2048, evac_pool)
            misc_dma(I("dma_start", out=send_t[s][:, :].rearrange("p (c t) -> p c t", t=16), in_=E[:, :, 512:528]), reads=EALL, writes=[("send_t", s)])
            bkv = bank()
            reserved.add(bkv)

            def evac_kv(j, ps, pkey):
                P.op("act", I("copy", out=TMP[j], in_=ps[:]), reads=[pkey], writes=[tk(j)])
                t = 4 + j % 2
                P.op("act", I("activation", out=TMPb[t], in_=ps[:], func=AF.Square), reads=[pkey], writes=[tk(t)])
                P.pe_group([I("matmul", PS[bkv][:], lhsT=ONESB[:], rhs=TMPb[t], start=(j == 0), stop=(j == 3))], reads=[tk(t), "onesb"], writes=[pk(bkv)])
            stream_linear(w_in, 0, Urhs, IN_KV, 512, evac_kv)
            rstd_from(bkv, 512, 4)
            reserved.discard(bkv)
            for j in range(4):
                P.op("dve", I("scalar_tensor_tensor", out=SENDT[:, j, :], in0=TMP[j], scalar=G[:, G_KVL + j:G_KVL + j + 1], in1=TMP[4], op0=ALU.mult, op1=ALU.mult),
                     reads=[tk(j), tk(4), "G"], writes=[sck(8 + j)])
            bkr = bank()
            reserved.add(bkr)
            for s0 in (0, 16):
                si, sv = w_slab(w_in, s0, 16, IN_KR, 64)
                P.pe_group([I("matmul", PS[bkr][0:64, :], lhsT=sv[:, kk, :], rhs=Urhs[s0 + kk][0], start=(s0 + kk == 0), stop=(s0 + kk == DC - 1)) for kk in range(16)],
                           reads=[("ring", si)] + [Urhs[s0 + kk][1] for kk in range(16)], writes=[pk(bkr)])
            rope_norm(bkr, G_KR, G_KRS, SENDT[0:64, 4, :], sck(12), G, "G")
            reserved.discard(bkr)
            misc_dma(I("dma_start", out=send_l[s][:, 0:4 * T], in_=SC[:, 4096:4096 + 4 * T]), reads=[sck(8 + j) for j in range(4)], writes=[("send_l", s)])
            misc_dma(I("dma_start", out=send_l[s][0:64, 4 * T:5 * T], in_=SC[0:64, 4096 + 4 * T:4096 + 5 * T]), reads=[sck(12)], writes=[("send_l", s)])

            P.dma("pool", I("collective_compute", "AllGather", ALU.bypass, replica_groups=rg, ins=[send_l[s][:, :]], outs=[recv_l[s][:, :]]),
                  d_ccl, reads=[("send_l", s)], writes=[("recv_l", s)], inc=1)
            P.dma("pool", I("collective_compute", "AllGather", ALU.bypass, replica_groups=rg, ins=[send_t[s][:, :]], outs=[recv_t[s][:, :]]),
                  d_cct, reads=[("send_t", s)], writes=[("recv_t", s)], inc=1)

            bq = bank()
            reserved.add(bq)

            def evac_q(j, ps, pkey):
                P.op("act", I("copy", out=ZQ[:, j, :], in_=ps[:]), reads=[pkey], writes=[hk(17 + j)])
                t = j % 2
                P.op("act", I("activation", out=TMPb[t], in_=ps[:], func=AF.Square), reads=[pkey], writes=[tk(t)])
                P.pe_group([I("matmul", PS[bq][:], lhsT=ONESB[:], rhs=TMPb[t], start=(j == 0), stop=(j == 7))], reads=[tk(t), "onesb"], writes=[pk(bq)])
            stream_linear(w_in, 0, Urhs, IN_Q, 1024, evac_q)
            rstd_from(bq, 1024, 2)
            reserved.discard(bq)
            for j in range(8):
                P.op("dve", I("scalar_tensor_tensor", out=CQ[:, j, :], in0=ZQ[:, j, :], scalar=G[:, G_QL + j:G_QL + j + 1], in1=TMP[2], op0=ALU.mult, op1=ALU.mult),
                     reads=[hk(17 + j), tk(2), "G"], writes=[sck(j)])
            cqr = [(CQ[:, j, :], sck(j)) for j in range(8)]

            gtk = [sck(i) for i in range(24, 28)]
            misc_dma(I("dma_start", out=GT, in_=recv_t[s][:, :].rearrange("(r p) n -> p r n", p=128)), reads=[("recv_t", s)], writes=gtk)
            halo = E[:, :, 0:16]
            P.op("dve", I("tensor_scalar", out=halo, in0=PT[:].rearrange("p (c t) -> p c t", t=16), scalar1=G[:, G_SEL + 4:G_SEL + 5], scalar2=None, op0=ALU.mult),
                 reads=["PT", "G"], writes=EALL)
            for r in range(4):
                P.op("dve", I("scalar_tensor_tensor", out=halo, in0=GT[:, r, :].rearrange("p (c t) -> p c t", t=16), scalar=G[:, G_SEL + r:G_SEL + r + 1], in1=halo,
                              op0=ALU.mult, op1=ALU.add), reads=gtk + ["G"] + EALL, writes=EALL)
            P.op("dve", I("tensor_copy", out=PT[:], in_=GT[:, 3, :]), reads=gtk, writes=["PT"])
            BA = TMPA[:, 0:528]
            BB = TMPA[:, 1024:1552]
            bufs = [(BA, [tk(0), tk(1)]), (BB, [tk(2), tk(3)])]
            for ci in range(16):
                g = ci // 4
                w = 2 << g
                src = E[:, ci, :]
                ek = ekeys(ci)
                cur, curk = src, ek
                d = 1
                bi = 0
                while d < w:
                    dst, dstk = bufs[bi]
                    lo = 2 * d - 1
                    P.op("dve", I("tensor_tensor", out=dst[:, lo:528], in0=cur[:, lo:528], in1=cur[:, lo - d:528 - d], op=ALU.add), reads=curk, writes=dstk)
                    cur, curk = dst, dstk
                    bi ^= 1
                    d *= 2
                P.op("dve", I("scalar_tensor_tensor", out=AMX[:, ci, :], in0=cur[:, 16:528], scalar=1.0 / w, in1=src[:, 16:528], op0=ALU.mult, op1=ALU.subtract),
                     reads=curk + ek, writes=[sck(8 + ci)])
                if s == 0:
                    P.op("dve", I("tensor_tensor", out=TMP[4][:, 0:16], in0=cur[:, 16:32], in1=INVC[:, g * 16:(g + 1) * 16], op=ALU.mult), reads=curk + ["invc"], writes=[tk(4)])
                    P.op("dve", I("tensor_tensor", out=AMX[:, ci, 0:16], in0=TMP[4][:, 0:16], in1=src[:, 16:32], op=ALU.subtract), reads=[tk(4)] + ek, writes=[sck(8 + ci)])
            for g in range(4):
                banks = bank_set()
                si, sv = w_slab(w_pool, g * 4, 4, 0, 512)
                for j in range(4):
                    P.pe_group([I("matmul", PS[banks[j]][:], lhsT=sv[:, i, j * 128:(j + 1) * 128], rhs=AMX[:, 4 * g + i, :], start=(i == 0), stop=(i == 3)) for i in range(4)],
                               reads=[("ring", si)] + [sck(8 + 4 * g + i) for i in range(4)], writes=[pk(banks[j])])
                for j in range(4):
                    co = 4 * g + j
                    P.op("dve", I("tensor_scalar", out=AOUT[:, co, :], in0=PS[banks[j]][:], scalar1=G[:, G_PS + co:G_PS + co + 1], scalar2=None, op0=ALU.mult),
                         reads=[pk(banks[j]), "G"] + EALL, writes=[hk(co // 2)])

            for half in range(2):
                for gi in range(5):
                    pos0 = s * 2048 + gi * 512 if gi < 4 else SEQ
                    ck = ("Kc", s) if gi < 4 else "KcOwn"
                    if gi < 4:
                        src = recv_l[s][gi * 128:(gi + 1) * 128, 0:4 * T]
                        rk = ("recv_l", s)
                        srckr = recv_l[s][gi * 128:gi * 128 + 64, 4 * T:5 * T]
                    else:
                        src = send_l[s][:, 0:4 * T]
                        rk = ("send_l", s)
                        srckr = send_l[s][0:64, 4 * T:5 * T]
                    li = ring_load(lambda slot: [I("dma_start", out=slot[:, 0:4 * T], in_=src)], reads=[rk])
                    lat = RING[li][:, 0:4 * T].rearrange("p (k t) -> p k t", t=T)
                    wsi, wsv = w_slab(w_ukv, 0, 4, half * 2048, 2048)
                    if half == 0:
                        misc_dma(I("dma_start", out=Krc[:, pos0:pos0 + T], in_=srckr), reads=[rk], writes=[ck])
                    def emit_k(hh):
                        b = bank()
                        reserved.add(b)
                        P.pe_group([I("matmul", PS[b][:], lhsT=wsv[:, kk, hh * 256:hh * 256 + 128], rhs=lat[:, kk, :], start=(kk == 0), stop=(kk == 3)) for kk in range(4)],
                                   reads=[("ring", wsi), ("ring", li)], writes=[pk(b)])
                        return b

                    def emit_v(tc, cg):
                        b = bank()
                        P.pe_group([I("matmul", PS[b][:], lhsT=lat[:, kk, tc * 128:(tc + 1) * 128],
                                      rhs=wsv[:, kk, :].rearrange("p (h c) -> p h c", c=256)[:, cg * 4:(cg + 1) * 4, 128:256], start=(kk == 0), stop=(kk == 3))
                                    for kk in range(4)], reads=[("ring", wsi), ("ring", li)], writes=[pk(b)])
                        vi = (tc * 2 + cg) % 2
                        vst = QR[vi]
                        vstk = sck(29 + vi)
                        P.op("act", I("copy", out=vst, in_=PS[b][:]), reads=[pk(b)], writes=[vstk])
                        h0 = half * 8 + cg * 4
                        kb = pos0 // 128 + tc
                        P.dma("sp", I("dma_start", out=Vc[h0:h0 + 4, :, kb, :].rearrange("h p d -> p h d"), in_=vst.rearrange("p (h d) -> p h d", d=128)),
                              d_kst[5 + vi], reads=[vstk], writes=[ck])
                    vlist = [(tc, cg) for tc in range(4) for cg in range(2)]
                    b_next = emit_k(0)
                    for hh in range(8):
                        h = half * 8 + hh
                        b = b_next
                        if hh + 1 < 8:
                            b_next = emit_k(hh + 1)
                        emit_v(*vlist[hh])
                        tq, tr_ = (0, 1) if hh % 2 == 0 else (2, 3)
                        partition_norm_stats(b, 128, tq, tr_)
                        ki = hh % 5
                        kst = PTILE[ki] if ki < 3 else QN[ki - 3]
                        kstk = sck(24 + ki)
                        P.op("dve", I("scalar_tensor_tensor", out=kst, in0=PS[b][:], scalar=G[:, G_KN:G_KN + 1], in1=TMP[tr_], op0=ALU.mult, op1=ALU.mult),
                             reads=[pk(b), tk(tr_), "G"], writes=[kstk])
                        reserved.discard(b)
                        P.dma("sp", I("dma_start", out=Kc[h, :, pos0:pos0 + T], in_=kst), d_kst[ki], reads=[kstk], writes=[ck])
            for i_ in range(2):
                P.op("dve", I("memset", QR[i_][64:128, :], 0.0), writes=[sck(29 + i_)])
            def prep_q(h):
                qi = h % 2
                si, sv = w_slab(w_uq, 0, 8, h * 192, 192)
                bn = bank()
                reserved.add(bn)
                P.pe_group([I("matmul", PS[bn][:], lhsT=sv[:, kk, 0:128], rhs=cqr[kk][0], start=(kk == 0), stop=(kk == 7)) for kk in range(8)],
                           reads=[("ring", si)] + [cqr[kk][1] for kk in range(8)], writes=[pk(bn)])
                br = bank()
                reserved.add(br)
                P.pe_group([I("matmul", PS[br][0:64, :], lhsT=sv[:, kk, 128:192], rhs=cqr[kk][0], start=(kk == 0), stop=(kk == 7)) for kk in range(8)],
                           reads=[("ring", si)] + [cqr[kk][1] for kk in range(8)], writes=[pk(br)])
                yield
                yield from pns_gen(bn, 128, 0, 1)
                P.op("dve", I("scalar_tensor_tensor", out=QN[qi], in0=PS[bn][:], scalar=GS[:, 0:1], in1=TMP[1], op0=ALU.mult, op1=ALU.mult),
                     reads=[pk(bn), tk(1), "GS"], writes=[sck(27 + qi)])
                reserved.discard(bn)
                yield
                yield from rope_gen(br, 1, 2, QR[qi][0:64, :], sck(29 + qi), GS, "GS")
                reserved.discard(br)

            run(prep_q(0))
            for h in range(NH):
                qi = h % 2
                bgq = prep_q(h + 1) if h + 1 < NH else None
                chunks = [(ci * 2048, 16, ("Kc", ci), ci == s) for ci in range(s + 1)] + [(SEQ, 4, "KcOwn", None)]
                blocks = []
                for (k0, nkb, ckey, cur) in chunks:
                    nk = nkb * 128
                    holder = {}

                    def loader(holder=holder, k0=k0, nk=nk, nkb=nkb, ckey=ckey, h=h):
                        if "li" not in holder:
                            holder["li"] = ring_load(lambda slot: [
                                I("dma_start", out=slot[:, 0:nk], in_=Kc[h, :, k0:k0 + nk]),
                                I("dma_start", out=slot[:, 2048:2048 + nk].rearrange("p (k d) -> p k d", d=128), in_=Vc[h, :, k0 // 128:k0 // 128 + nkb, :]),
                                I("dma_start", out=slot[0:64, 4096:4096 + nk], in_=Krc[:, k0:k0 + nk]),
                            ], reads=[ckey])
                        return holder["li"]
                    for kb in range(nkb):
                        bias = None
                        tri = None
                        if cur is None:
                            tri = kb
                        elif cur:
                            gi = kb // 4
                            bias = G[:, G_BIAS + gi:G_BIAS + gi + 1]
                        blocks.append(dict(loader=loader, bias=bias, tri=tri, kb=kb))

                def qk_fn(i, bs, blocks=blocks, qi=qi):
                    slot, kb = RING[blocks[i]["loader"]()], blocks[i]["kb"]
                    return [I("matmul", PS[bs][:], lhsT=slot[:, kb * 128:(kb + 1) * 128], rhs=QN[qi], start=True, stop=False),
                            I("matmul", PS[bs][:], lhsT=slot[:, 4096 + kb * 128:4096 + (kb + 1) * 128], rhs=QR[qi], start=False, stop=True)]

                def v_fn(i, blocks=blocks):
                    slot, kb = RING[blocks[i]["loader"]()], blocks[i]["kb"]
                    return slot[:, 2048 + kb * 128:2048 + (kb + 1) * 128]
                softmax_pv(blocks, qk_fn, v_fn, BOUT[:, h, :], hk(8 + h // 2), [sck(27 + qi), sck(29 + qi)], bg=bgq)
                if bgq is not None:
                    run(bgq)
            if s == 0 and debug:
                dump(4, HALL)

            arhs = [(AOUT[:, k, :], hk(k // 2)) for k in range(16)]
            brhs = [(BOUT[:, k, :], hk(8 + k // 2)) for k in range(16)]
            for mb in range(8):
                Bps = {}

                def evac_gp(j, ps, pkey):
                    P.op("act", I("activation", out=TMP[j], in_=ps[:], func=AF.Sigmoid), reads=[pkey], writes=[tk(j)])

                def evac_a(j, ps, pkey):
                    P.op("dve", I("tensor_tensor", out=TMP[j], in0=TMP[j], in1=ps[:], op=ALU.mult), reads=[pkey, tk(j)], writes=[tk(j)])

                def evac_b(j, ps, pkey, Bps=Bps):
                    Bps[j] = (ps, pkey)
                    return True

                def evac_gm(j, ps, pkey, Bps=Bps, mb=mb):
                    t = 4 + j % 2
                    P.op("act", I("activation", out=TMP[t], in_=ps[:], func=AF.Sigmoid), reads=[pkey], writes=[tk(t)])
                    P.op("dve", I("tensor_tensor", out=TMP[t], in0=TMP[t], in1=Bps[j][0][:], op=ALU.mult), reads=[Bps[j][1], tk(t)], writes=[tk(t)])
                    P.op("dve", I("tensor_tensor", out=ACT_[:, 4 * mb + j, :], in0=TMP[j], in1=TMP[t], op=ALU.add), reads=[tk(j), tk(t)], writes=[sck(4 * mb + j)])
                    reserved.discard(Bps[j][1][1])
                stream_linear(w_in, 0, Urhs, IN_GP + mb * 512, 512, evac_gp)
                stream_linear(w_pa, 0, arhs, mb * 512, 512, evac_a)
                stream_linear(w_pb, 0, brhs, mb * 512, 512, evac_b)
                stream_linear(w_in, 0, Urhs, IN_GM + mb * 512, 512, evac_gm)
            if s == 0:
                dumpb(0, SCALL)

            for mb in range(8):
                misc_dma(I("dma_start", out=Hf[:, mb * 2048:(mb + 1) * 2048], in_=hbuf[:, mb * 2048:(mb + 1) * 2048]), reads=["hbuf"], writes=[hk(4 * mb + i) for i in range(4)])
            mrg = [(ACT_[:, k, :], sck(k)) for k in range(DC)]

            def evac_o(m, ps, pkey):
                P.op("dve", I("tensor_tensor", out=H[:, m, :], in0=ps[:], in1=H[:, m, :], op=ALU.add), reads=[pkey, hk(m)], writes=[hk(m)])
            stream_linear(w_out, 0, mrg, 0, D, evac_o)
            if s == 0:
                dump(1, HALL)

            norm_full(G_X)

            def evac_xq(j, ps, pkey):
                partition_norm_stats(pkey[1], 128, 0, 1)
                P.op("dve", I("scalar_tensor_tensor", out=XQT[:, j, :], in0=ps[:], scalar=GS[:, 3:4], in1=TMP[1], op0=ALU.mult, op1=ALU.mult),
                     reads=[pkey, tk(1), "GS"], writes=[sck(j)])
            stream_linear(w_xq, 0, Urhs, 0, 512, evac_xq)
            for hd in range(4):
                blocks = [dict(reads=["XK", "XV"], bias=None, tri=None, mc=mc) for mc in range(2)]

                def qk_fn(i, bs, hd=hd):
                    return [I("matmul", PS[bs][:], lhsT=XK[:, hd, i * 128:(i + 1) * 128], rhs=XQT[:, hd, :], start=True, stop=True)]

                def v_fn(i, hd=hd):
                    return XV[:, i, hd * 128:(hd + 1) * 128]
                softmax_pv(blocks, qk_fn, v_fn, XOT[:, hd, :], sck(4 + hd), [sck(hd)])
            xor = [(XOT[:, k, :], sck(4 + k)) for k in range(4)]
            stream_linear(w_xo, 0, xor, 0, D, evac_o)
            if s == 0:
                dump(2, HALL)

            norm_full(G_F2)
            ffn(w_gu2, w_d2)
            if s == 0:
                dump(3, HALL)

            for tc in range(4):
                ost = XST[tc % 2]
                for cg in range(8):
                    b = bank()
                    P.pe_group([I("transpose", PS[b][:, i * 128:(i + 1) * 128], H[:, 4 * cg + i, tc * 128:(tc + 1) * 128], ident) for i in range(4)],
                               reads=[hk(4 * cg + i) for i in range(4)] + ["cst"], writes=[pk(b)])
                    uk = [sck(16 * (tc % 2) + 2 * cg), sck(16 * (tc % 2) + 2 * cg + 1)]
                    if cg % 2 == 0:
                        P.op("act", I("copy", out=ost[:, cg * 512:(cg + 1) * 512], in_=PS[b][:]), reads=[pk(b)], writes=uk)
                    else:
                        P.op("dve", I("tensor_copy", out=ost[:, cg * 512:(cg + 1) * 512], in_=PS[b][:]), reads=[pk(b)], writes=uk)
                P.dma("sp", I("dma_start", out=out_d[s * T + tc * 128: s * T + (tc + 1) * 128, :], in_=ost), d_o[tc % 2], reads=xstk(tc % 2), writes=["out"])

        P.wait_all("sp", ["out"] + ([("dbg", i) for i in range(NDBG)] + [("dbgb", i) for i in range(NDBG)] if debug else []))
        P.run_block()
        print("instr counts", P.ninstr, "nsem", P.nsem, flush=True)
    return nc


def _fm(v):
    v = np.asarray(v, np.float32).reshape(-1, 128)
    return np.ascontiguousarray(v.T)


def make_in_maps(inputs, NSTEP=4, dff=DFF):
    SEQ = NSTEP * 2048
    x = np.asarray(inputs["x"]); mem = np.asarray(inputs["mem"]); positions = np.asarray(inputs["positions"])
    sq = lambda k: np.ascontiguousarray(np.asarray(inputs[k])[0])
    weights = {k: sq(k) for k in ["ffn1_w_gu", "ffn1_w_down", "ffn2_w_gu", "ffn2_w_down", "w_in", "w_uq", "w_ukv", "w_branch_pool", "w_branch_mla",
                                  "w_out", "w_xq", "w_xkv", "w_xo"]}
    weights["w_pool"] = np.ascontiguousarray(np.asarray(inputs["w_pool"])[0].reshape(2048, 512))
    if dff != DFF:
        for f in ("ffn1", "ffn2"):
            wg = weights[f + "_w_gu"]
            weights[f + "_w_gu"] = np.ascontiguousarray(np.concatenate([wg[:, 0:dff], wg[:, DFF:DFF + dff]], axis=1))
            weights[f + "_w_down"] = np.ascontiguousarray(weights[f + "_w_down"][0:dff])
    gbase = np.zeros((128, NG), np.float32)
    gbase[:, G_F1:G_F1 + 32] = _fm(sq("ffn1_norm")); gbase[:, G_MIX:G_MIX + 32] = _fm(sq("mix_norm"))
    gbase[:, G_X:G_X + 32] = _fm(sq("x_norm")); gbase[:, G_F2:G_F2 + 32] = _fm(sq("ffn2_norm"))
    gbase[:, G_MEM:G_MEM + 32] = _fm(sq("mem_norm")); gbase[:, G_PS:G_PS + 16] = _fm(sq("pool_scale"))
    gbase[:, G_QL:G_QL + 8] = _fm(sq("q_latent_norm")); gbase[:, G_KVL:G_KVL + 4] = _fm(sq("kv_latent_norm"))
    gbase[:, G_QN] = sq("q_nope_norm"); gbase[:, G_KN] = sq("k_nope_norm"); gbase[:, G_XQ] = sq("xq_norm"); gbase[:, G_XK] = sq("xk_norm")
    qr = sq("q_rope_norm"); kr = sq("k_rope_norm")
    gbase[0:64, G_QR] = qr; gbase[0:64, G_QRS] = np.concatenate([qr[32:], qr[:32]])
    gbase[0:64, G_KR] = kr; gbase[0:64, G_KRS] = np.concatenate([kr[32:], kr[:32]])
    half = 32
    inv = (1.0 / (np.float32(10000.0) ** (np.arange(half, dtype=np.float32) * np.float32(2.0 / 64)))).astype(np.float32)
    gbase[0:64, G_INV] = np.concatenate([inv, inv])
    gbase[0:32, G_SGN] = -1.0; gbase[32:64, G_SGN] = 1.0
    cst = np.zeros((128, 192), np.float32)
    cst[:, 0:128] = np.eye(128, dtype=np.float32)
    for m in range(64):
        cst[(m + 32) % 64, 128 + m] = 1.0
    p_ = np.arange(128)[:, None]
    q_ = np.arange(T)[None, :]
    tri = np.concatenate([(j * 128 + p_ <= q_).astype(np.float32) for j in range(4)], axis=1).astype(ml_dtypes.bfloat16)
    in_maps = []
    for core in range(8):
        b, c = core // 4, core % 4
        g = gbase.copy()
        for r in range(4):
            g[:, G_BIAS + r] = 0.0 if r < c else NEG
            g[:, G_SEL + r] = 1.0 if (c >= 1 and r == c - 1) else 0.0
        g[:, G_SEL + 4] = 1.0 if c == 0 else 0.0
        invc = np.zeros((128, 64), np.float32)
        for gg in range(4):
            w = 2 << gg
            t = np.arange(16)
            invc[:, gg * 16:(gg + 1) * 16] = (1.0 / np.minimum(t + 1, w)) if c == 0 else (1.0 / w)
        rows = np.concatenate([np.arange(s * 2048 + c * 512, s * 2048 + c * 512 + 512) for s in range(NSTEP)])
        m = {"x": np.ascontiguousarray(x[b, rows, :]), "pos": np.ascontiguousarray(positions[b, rows].reshape(NSTEP, T).astype(np.int32)),
             "mem": np.ascontiguousarray(mem[b]), "gpack": g, "cst": cst, "tri": tri, "invc": invc}
        m.update(weights)
        in_maps.append(m)
    return in_maps


def assemble(results, NSTEP=4):
    SEQ = NSTEP * 2048
    out = np.zeros((2, SEQ, D), np.float32)
    for core in range(8):
        b, c = core // 4, core % 4
        o = results[core]["out"]
        for s in range(NSTEP):
            out[b, s * 2048 + c * 512: s * 2048 + c * 512 + 512, :] = o[s * T:(s + 1) * T]
    return out


_NC_CACHE = {}


def kernel(**inputs):
    NSTEP = 4
    if NSTEP not in _NC_CACHE:
        _NC_CACHE[NSTEP] = build(NSTEP)
    nc = _NC_CACHE[NSTEP]
    in_maps = make_in_maps(inputs, NSTEP)
    res = run_bass_kernel_spmd(nc, in_maps, core_ids=list(range(8)))
    return assemble(res.results, NSTEP)
```

```python
import numpy as np
import ml_dtypes
import concourse.bass as bass
import concourse.mybir as mybir
from concourse.bass_utils import run_bass_kernel_spmd
from contextlib import ExitStack

F32 = mybir.dt.float32
BF16 = mybir.dt.bfloat16
I32 = mybir.dt.int32
AF = mybir.ActivationFunctionType
ALU = mybir.AluOpType

ENGS = ("pe", "act", "dve", "pool", "sp")

D = 4096
DC = 32
DFF = 11008
T = 512
NH = 16
EPS = 1e-6
IN_POOL, IN_Q, IN_KV, IN_KR, IN_GP, IN_GM = 0, 2048, 3072, 3584, 3648, 3648 + 4096
G_F1, G_MIX, G_X, G_F2, G_MEM, G_PS, G_QL, G_KVL = 0, 32, 64, 96, 128, 160, 176, 184
G_QN, G_KN, G_XQ, G_XK, G_QR, G_QRS, G_KR, G_KRS, G_INV, G_SGN, G_BIAS, G_SEL = 188, 189, 190, 191, 192, 193, 194, 195, 196, 197, 198, 202
NG = 208
NEG = -30000.0
TWO_PI = 6.283185307179586
C1 = 6.28125
C2 = TWO_PI - C1
PI_SAFE = 3.1415925


class Sem:
    __slots__ = ("h", "count", "name")

    def __init__(self, h, name):
        self.h = h
        self.count = 0
        self.name = name


class Prog:
    def __init__(self, nc, es):
        self.nc = nc
        self.es = es
        self.ops = {e: [] for e in ENGS}
        self.prog_sem = {}
        self.waited = {e: {} for e in ENGS}
        self.W = {}
        self.R = {}
        self.nsem = 0
        self.ninstr = {e: 0 for e in ENGS}
        for e in ENGS:
            self.prog_sem[e] = self.new_sem("p_" + e)

    def new_sem(self, name):
        self.nsem += 1
        h = self.es.enter_context(self.nc.semaphore(f"{name}_{self.nsem}"))
        return Sem(h, name)

    def new_epoch(self):
        for e in ENGS:
            self.prog_sem[e] = self.new_sem("p_" + e)

    def _deps(self, eng, reads, writes, skip_sem=None, extra=None):
        need = {}
        if extra:
            for s, v in extra:
                if need.get(s, 0) < v:
                    need[s] = v
        for k in reads:
            w = self.W.get(k)
            if w:
                for s, v in w.items():
                    if need.get(s, 0) < v:
                        need[s] = v
        for k in writes:
            for d in (self.W.get(k), self.R.get(k)):
                if d:
                    for s, v in d.items():
                        if need.get(s, 0) < v:
                            need[s] = v
        wd = self.waited[eng]
        out = []
        for s, v in need.items():
            if s is skip_sem:
                continue
            if wd.get(s, 0) < v:
                wd[s] = v
                out.append((s, v))
        return out

    def _record(self, reads, writes, sem, val):
        for k in reads:
            d = self.R.get(k)
            if d is None:
                d = self.R[k] = {}
            if d.get(sem, 0) < val:
                d[sem] = val
        for k in writes:
            d = self.W.get(k)
            if d is None:
                d = self.W[k] = {}
            if d.get(sem, 0) < val:
                d[sem] = val

    def op(self, eng, ins, reads=(), writes=()):
        ps = self.prog_sem[eng]
        waits = self._deps(eng, reads, writes, skip_sem=ps if eng == "pe" else None)
        ps.count += 1
        self._record(reads, writes, ps, ps.count)
        self.ops[eng].append((waits, ins, ps, 1))
        self.ninstr[eng] += 1

    def pe_group(self, inss, reads=(), writes=()):
        ps = self.prog_sem["pe"]
        waits = self._deps("pe", reads, writes, skip_sem=ps)
        ps.count += 1
        self._record(reads, writes, ps, ps.count)
        lst = self.ops["pe"]
        n = len(inss)
        for i, ins in enumerate(inss):
            lst.append((waits if i == 0 else (), ins, ps if i == n - 1 else None, 1))
        self.ninstr["pe"] += n

    def dma(self, eng, ins, dsem, reads=(), writes=(), inc=16, guard=False):
        extra = [(dsem, dsem.count)] if (guard and dsem.count > 0) else None
        waits = self._deps(eng, reads, writes, extra=extra)
        dsem.count += inc
        self._record(reads, writes, dsem, dsem.count)
        self.ops[eng].append((waits, ins, dsem, inc))
        self.ninstr[eng] += 1

    def wait_all(self, eng, keys):
        waits = self._deps(eng, keys, ())
        self.ops[eng].append((waits, None, None, 0))

    def replay(self, eng, e):
        for waits, ins, sem, inc in self.ops[eng]:
            for s, v in waits:
                e.wait_ge(s.h, v)
            if ins is None:
                continue
            meth, args, kw = ins
            r = getattr(e, meth)(*args, **kw)
            if sem is not None:
                r.then_inc(sem.h, inc)

    def run_block(self):
        nc = self.nc
        with nc.Block() as block:
            @block.tensor
            def _(e):
                self.replay("pe", e)

            @block.scalar
            def _(e):
                self.replay("act", e)

            @block.vector
            def _(e):
                self.replay("dve", e)

            @block.gpsimd
            def _(e):
                self.replay("pool", e)

            @block.sync
            def _(e):
                self.replay("sp", e)


def I(meth, *args, **kw):
    return (meth, args, kw)


def build(NSTEP=4, debug=False, dff=DFF):
    SEQ = NSTEP * 2048
    KTOT = SEQ + 512
    NKB = KTOT // 128
    nc = bass.Bass("TRN2", target_bir_lowering=False)
    din = lambda name, shape, dt=F32: nc.dram_tensor(name, shape, dt, kind="ExternalInput").ap()
    dint = lambda name, shape, dt=F32: nc.dram_tensor(name, shape, dt).ap()
    x_d = din("x", [NSTEP * T, D])
    pos_d = din("pos", [NSTEP, T], I32)
    mem_d = din("mem", [256, D])
    g_d = din("gpack", [128, NG])
    cst_d = din("cst", [128, 192])
    tri_d = din("tri", [128, 4 * T], BF16)
    invc_d = din("invc", [128, 64])
    w_gu1 = din("ffn1_w_gu", [D, 2 * dff]); w_d1 = din("ffn1_w_down", [dff, D])
    w_gu2 = din("ffn2_w_gu", [D, 2 * dff]); w_d2 = din("ffn2_w_down", [dff, D])
    w_in = din("w_in", [D, 11840]); w_pool = din("w_pool", [2048, 512])
    w_uq = din("w_uq", [1024, 3072]); w_ukv = din("w_ukv", [512, 4096])
    w_pa = din("w_branch_pool", [2048, D]); w_pb = din("w_branch_mla", [2048, D])
    w_out = din("w_out", [D, D]); w_xq = din("w_xq", [D, 512]); w_xkv = din("w_xkv", [D, 1024]); w_xo = din("w_xo", [512, D])
    out_d = nc.dram_tensor("out", [NSTEP * T, D], F32, kind="ExternalOutput").ap()
    NDBG = 6
    dbg_d = nc.dram_tensor("dbg", [NDBG, 128, DC * T], F32, kind="ExternalOutput").ap() if debug else None
    dbgb_d = nc.dram_tensor("dbgb", [NDBG, 128, 16384], BF16, kind="ExternalOutput").ap() if debug else None
    hbuf = dint("hbuf", [128, DC * T])
    send_l = [dint(f"send_l{s}", [128, 5 * T], BF16) for s in range(NSTEP)]
    recv_l = [dint(f"recv_l{s}", [4 * 128, 5 * T], BF16) for s in range(NSTEP)]
    send_t = [dint(f"send_t{s}", [128, 256]) for s in range(NSTEP)]
    recv_t = [dint(f"recv_t{s}", [4 * 128, 256]) for s in range(NSTEP)]
    Kc = dint("Kc", [NH, 128, KTOT], BF16)
    Krc = dint("Krc", [64, KTOT], BF16)
    Vc = dint("Vc", [NH, 128, NKB, 128], BF16)

    with ExitStack() as es:
        P = Prog(nc, es)
        sb = lambda name, shape, dt: es.enter_context(nc.sbuf_tensor(name, shape, dt))
        H = sb("H", [128, DC, T], F32)
        U = sb("U", [128, DC, T], BF16)
        SC = sb("SC", [128, 16384], BF16)
        RING = [sb(f"ring{i}", [128, 8192], BF16) for i in range(3)]
        G = sb("G", [128, NG], F32)
        CST = sb("CST", [128, 192], F32)
        ONESF = sb("ONESF", [128, 128], F32)
        ONESB = sb("ONESB", [128, 128], BF16)
        TRI = sb("TRI", [128, 4, T], BF16)
        XK = sb("XK", [128, 4, 256], BF16)
        XV = sb("XV", [128, 2, 512], BF16)
        COSD = sb("COSD", [128, T], F32)
        SIND = sb("SIND", [128, T], F32)
        TMPA = sb("TMPA", [128, 6 * T], F32)
        PT = sb("PT", [128, 256], F32)
        INVC = sb("INVC", [128, 64], F32)
        GS = sb("GS", [128, 4], F32)
        PS = [es.enter_context(nc.psum_tensor(f"ps{i}", [128, T], F32)) for i in range(8)]
        ident = CST[:, 0:128]
        perm = CST[0:64, 128:192]
        TMP = [TMPA[:, i * T:(i + 1) * T] for i in range(6)]
        TMPb = [TMPA[:, i * T:(i + 1) * T].bitcast(BF16)[:, 0:T] for i in range(6)]
        tk = lambda i: ("tmp", i)
        Hf = H[:].rearrange("p a b -> p (a b)")
        E = Hf[:, 0:16 * 528].rearrange("p (c t) -> p c t", t=528)
        AOUT = Hf[:, 0:4096].bitcast(BF16).rearrange("p (c t) -> p c t", t=T)
        BOUT = Hf[:, 4096:8192].bitcast(BF16).rearrange("p (c t) -> p c t", t=T)
        ZQ = H[:, 17:25, :]
        hk = lambda c: ("H", c)
        HALL = [hk(i) for i in range(DC)]

        def ekeys(c):
            lo, hi = c * 2112, (c + 1) * 2112 - 1
            return [hk(i) for i in range(lo // 2048, hi // 2048 + 1)]
        EALL = [hk(i) for i in range(17)]
        SCf = SC[:].bitcast(F32)
        ACT_ = SC[:].rearrange("p (c t) -> p c t", t=T)
        sck = lambda i: ("sc", i)
        SCALL = [sck(i) for i in range(32)]
        CQ = SC[:, 0:4096].rearrange("p (c t) -> p c t", t=T)
        AMX = SC[:, 4096:12288].rearrange("p (c t) -> p c t", t=T)
        SENDT = SC[:, 4096:4096 + 2560].rearrange("p (c t) -> p c t", t=T)
        PTILE = [SC[:, 12288 + i * T: 12288 + (i + 1) * T] for i in range(3)]
        QN = [SC[:, 13824 + i * T: 13824 + (i + 1) * T] for i in range(2)]
        QR = [SC[:, 14848 + i * T: 14848 + (i + 1) * T] for i in range(2)]
        GT = SCf[:, 6144:7168].rearrange("p (r n) -> p r n", n=256)
        XQT = SC[:, 0:2048].rearrange("p (c t) -> p c t", t=T)
        XOT = SC[:, 2048:4096].rearrange("p (c t) -> p c t", t=T)
        XST = [SCf[:, i * 4096:(i + 1) * 4096] for i in range(2)]
        xstk = lambda i: [sck(j) for j in range(16 * i, 16 * i + 16)]

        st = {"bank": 0, "set": 0, "ring": 0, "misc": 0}
        reserved = set()
        ring_sem = [P.new_sem(f"ring{i}") for i in range(3)]
        d_miscs = [P.new_sem(f"d_misc{i}") for i in range(10)]
        d_x = [P.new_sem("d_x0"), P.new_sem("d_x1")]
        d_o = [P.new_sem("d_o0"), P.new_sem("d_o1")]
        d_kst = [P.new_sem(f"d_kst{i}") for i in range(7)]
        d_ccl = P.new_sem("d_ccl")
        d_cct = P.new_sem("d_cct")

        def misc_dma(ins, reads=(), writes=(), eng="sp"):
            i = st["misc"]
            st["misc"] = (i + 1) % len(d_miscs)
            P.dma(eng, ins, d_miscs[i], reads=reads, writes=writes, guard=True)

        def bank():
            while True:
                b = st["bank"]
                st["bank"] = (b + 1) % 8
                if b not in reserved:
                    return b

        def bank_set():
            for _ in range(2):
                s_ = st["set"]
                st["set"] ^= 1
                bs = [4 * s_ + j for j in range(4)]
                if not (set(bs) & reserved):
                    return bs
            raise RuntimeError("no free bank set")

        pk = lambda b: ("ps", b)

        def ring_load(inss_fn, reads=()):
            i = st["ring"]
            st["ring"] = (i + 1) % 3
            for ins in inss_fn(RING[i]):
                P.dma("pool", ins, ring_sem[i], reads=reads, writes=[("ring", i)])
            return i

        def w_slab(w_ap, k0, nk, c0, ncols):
            src = w_ap[k0 * 128:(k0 + nk) * 128, c0:c0 + ncols].rearrange("(kc p) n -> p kc n", p=128)
            i = ring_load(lambda slot: [I("dma_start", out=slot[:, 0:nk * ncols].rearrange("p (k n) -> p k n", n=ncols), in_=src)])
            return i, RING[i][:, 0:nk * ncols].rearrange("p (k n) -> p k n", n=ncols)

        def stream_linear(w_ap, k0w, rhs, col0, ncols, evac, ntok=T):
            nk = len(rhs)
            for cb in range(0, ncols, 512):
                nb = min(512, ncols - cb)
                nj = nb // 128
                banks = bank_set()
                reserved.update(banks[0:nj])
                for s0 in range(0, nk, 16):
                    ns = min(16, nk - s0)
                    si, sv = w_slab(w_ap, k0w + s0, ns, col0 + cb, nb)
                    for j in range(nj):
                        b = banks[j]
                        inss = [I("matmul", PS[b][:, 0:ntok], lhsT=sv[:, kk, j * 128:(j + 1) * 128], rhs=rhs[s0 + kk][0],
                                  start=(s0 + kk == 0), stop=(s0 + kk == nk - 1)) for kk in range(ns)]
                        P.pe_group(inss, reads=[("ring", si)] + [rhs[s0 + kk][1] for kk in range(ns)], writes=[pk(b)])
                for j in range(nj):
                    keep = evac(cb // 128 + j, PS[banks[j]], pk(banks[j]))
                    if not keep:
                        reserved.discard(banks[j])

        def rstd_from(ps_b, nfeat, t_out, rows=128, n=T):
            P.op("act", I("activation", out=TMP[t_out][0:rows, 0:n], in_=PS[ps_b][0:rows, 0:n], func=AF.Sqrt, scale=1.0 / nfeat, bias=EPS),
                 reads=[pk(ps_b)], writes=[tk(t_out)])
            P.op("dve", I("reciprocal", out=TMP[t_out][0:rows, 0:n], in_=TMP[t_out][0:rows, 0:n]), reads=[tk(t_out)], writes=[tk(t_out)])

        def norm_full(gcol):
            b = bank()
            reserved.add(b)
            for ci in range(DC):
                t = ci % 2
                P.op("act", I("activation", out=TMPb[t], in_=H[:, ci, :], func=AF.Square), reads=[hk(ci)], writes=[tk(t)])
                P.pe_group([I("matmul", PS[b][:], lhsT=ONESB[:], rhs=TMPb[t], start=(ci == 0), stop=(ci == DC - 1))],
                           reads=[tk(t), "onesb"], writes=[pk(b)])
            rstd_from(b, D, 2)
            reserved.discard(b)
            for ci in range(DC):
                P.op("dve", I("scalar_tensor_tensor", out=U[:, ci, :], in0=H[:, ci, :], scalar=G[:, gcol + ci:gcol + ci + 1], in1=TMP[2],
                              op0=ALU.mult, op1=ALU.mult), reads=[hk(ci), tk(2), "G"], writes=[("U", ci)])

        Urhs = [(U[:, ci, :], ("U", ci)) for ci in range(DC)]

        def ffn(w_gu, w_d):
            nch = dff // 128
            parts = [(i, min(i + 32, nch)) for i in range(0, nch, 32)]
            for (c0, c1) in parts:
                for cb in range(c0 * 128, c1 * 128, 512):
                    nb = min(512, c1 * 128 - cb)

                    def evac_g(j, ps, pkey):
                        P.op("act", I("activation", out=TMP[j], in_=ps[:], func=AF.Silu), reads=[pkey], writes=[tk(j)])

                    def evac_u(j, ps, pkey, cb=cb, c0=c0):
                        a = cb // 128 + j - c0
                        P.op("dve", I("tensor_tensor", out=ACT_[:, a, :], in0=TMP[j], in1=ps[:], op=ALU.mult), reads=[pkey, tk(j)], writes=[sck(a)])
                    stream_linear(w_gu, 0, Urhs, cb, nb, evac_g)
                    stream_linear(w_gu, 0, Urhs, dff + cb, nb, evac_u)
                arhs_ = [(ACT_[:, a, :], sck(a)) for a in range(c1 - c0)]

                def evac_d(m, ps, pkey):
                    P.op("dve", I("scalar_tensor_tensor", out=H[:, m, :], in0=ps[:], scalar=0.5, in1=H[:, m, :], op0=ALU.mult, op1=ALU.add),
                         reads=[pkey, hk(m)], writes=[hk(m)])
                stream_linear(w_d, c0, arhs_, 0, D, evac_d)

        def run(gen):
            for _ in gen:
                pass

        def pns_gen(ps_b, nrows, t_sq, t_r, n=T):
            P.op("act", I("activation", out=TMPb[t_sq][0:nrows, 0:n], in_=PS[ps_b][0:nrows, 0:n], func=AF.Square), reads=[pk(ps_b)], writes=[tk(t_sq)])
            yield
            b2 = bank()
            reserved.add(b2)
            P.pe_group([I("matmul", PS[b2][0:nrows, 0:n], lhsT=ONESB[0:nrows, 0:nrows], rhs=TMPb[t_sq][0:nrows, 0:n], start=True, stop=True)],
                       reads=[tk(t_sq), "onesb"], writes=[pk(b2)])
            yield
            P.op("act", I("activation", out=TMP[t_r][0:nrows, 0:n], in_=PS[b2][0:nrows, 0:n], func=AF.Sqrt, scale=1.0 / nrows, bias=EPS),
                 reads=[pk(b2)], writes=[tk(t_r)])
            reserved.discard(b2)
            yield
            P.op("dve", I("reciprocal", out=TMP[t_r][0:nrows, 0:n], in_=TMP[t_r][0:nrows, 0:n]), reads=[tk(t_r)], writes=[tk(t_r)])
            yield

        def partition_norm_stats(ps_b, nrows, t_sq, t_r, n=T):
            run(pns_gen(ps_b, nrows, t_sq, t_r, n))

        def rope_gen(ps_b, gcol, gscol, out_ap, out_key, gtile, gkey):
            P.op("act", I("copy", out=TMP[0][0:64, :], in_=PS[ps_b][0:64, :]), reads=[pk(ps_b)], writes=[tk(0)])
            yield
            b_sw = bank()
            reserved.add(b_sw)
            P.pe_group([I("matmul", PS[b_sw][0:64, :], lhsT=perm, rhs=TMP[0][0:64, :], start=True, stop=True)], reads=[tk(0), "cst"], writes=[pk(b_sw)])
            yield
            yield from pns_gen(ps_b, 64, 1, 2)
            P.op("dve", I("scalar_tensor_tensor", out=TMP[3][0:64, :], in0=TMP[0][0:64, :], scalar=gtile[0:64, gcol:gcol + 1], in1=COSD[0:64, :],
                          op0=ALU.mult, op1=ALU.mult), reads=[tk(0), "rope", gkey], writes=[tk(3)])
            yield
            P.op("dve", I("scalar_tensor_tensor", out=TMP[4][0:64, :], in0=PS[b_sw][0:64, :], scalar=gtile[0:64, gscol:gscol + 1], in1=SIND[0:64, :],
                          op0=ALU.mult, op1=ALU.mult), reads=[pk(b_sw), "rope", gkey], writes=[tk(4)])
            reserved.discard(b_sw)
            yield
            P.op("dve", I("tensor_tensor", out=TMP[3][0:64, :], in0=TMP[3][0:64, :], in1=TMP[4][0:64, :], op=ALU.add), reads=[tk(3), tk(4)], writes=[tk(3)])
            yield
            P.op("dve", I("tensor_tensor", out=out_ap, in0=TMP[3][0:64, :], in1=TMP[2][0:64, :], op=ALU.mult), reads=[tk(3), tk(2)], writes=[out_key])
            yield

        def rope_norm(ps_b, gcol, gscol, out_ap, out_key, gtile, gkey):
            run(rope_gen(ps_b, gcol, gscol, out_ap, out_key, gtile, gkey))

        def dump(slot, keys):
            if debug:
                misc_dma(I("dma_start", out=dbg_d[slot], in_=Hf), reads=keys, writes=[("dbg", slot)])

        def dumpb(slot, keys):
            if debug:
                misc_dma(I("dma_start", out=dbgb_d[slot], in_=SC[:]), reads=keys, writes=[("dbgb", slot)])

        def softmax_pv(blocks, qk_fn, v_fn, out_ap, out_key, extra_reads, bg=None):
            bo, bd = bank(), bank()
            reserved.add(bo); reserved.add(bd)
            nb_ = len(blocks)
            pend = []
            for i in range(nb_ + 2):
                if i < nb_:
                    blk = blocks[i]
                    if "loader" in blk:
                        blk["reads"] = [("ring", blk["loader"]())]
                    bs = bank()
                    pt = i % 3
                    P.pe_group(qk_fn(i, bs), reads=blk["reads"] + extra_reads, writes=[pk(bs)])
                    if blk["bias"] is not None:
                        P.op("act", I("activation", out=PTILE[pt], in_=PS[bs][:], func=AF.Exp, bias=blk["bias"], scale=1.0), reads=[pk(bs), "G"], writes=[sck(24 + pt)])
                    else:
                        P.op("act", I("activation", out=PTILE[pt], in_=PS[bs][:], func=AF.Exp), reads=[pk(bs)], writes=[sck(24 + pt)])
                    if blk["tri"] is not None:
                        P.op("dve", I("tensor_tensor", out=PTILE[pt], in0=PTILE[pt], in1=TRI[:, blk["tri"], :], op=ALU.mult),
                             reads=[sck(24 + pt), "tri"], writes=[sck(24 + pt)])
                if len(pend) >= 2 or (i >= nb_ and pend):
                    pi, ppt = pend.pop(0)
                    P.pe_group([I("matmul", PS[bo][:], lhsT=v_fn(pi), rhs=PTILE[ppt], start=(pi == 0), stop=(pi == nb_ - 1)),
                                I("matmul", PS[bd][:], lhsT=ONESB[:], rhs=PTILE[ppt], start=(pi == 0), stop=(pi == nb_ - 1))],
                               reads=blocks[pi]["reads"] + [sck(24 + ppt), "onesb"], writes=[pk(bo), pk(bd)])
                if i < nb_:
                    pend.append((i, pt))
                if bg is not None:
                    next(bg, None)
            assert not pend
            P.op("dve", I("reciprocal", out=TMP[5], in_=PS[bd][:]), reads=[pk(bd)], writes=[tk(5)])
            P.op("dve", I("tensor_tensor", out=out_ap, in0=PS[bo][:], in1=TMP[5], op=ALU.mult), reads=[pk(bo), tk(5)], writes=[out_key])
            reserved.discard(bo); reserved.discard(bd)

        misc_dma(I("dma_start", out=G[:], in_=g_d[:, :]), writes=["G"])
        misc_dma(I("dma_start", out=CST[:], in_=cst_d[:, :]), writes=["cst"])
        misc_dma(I("dma_start", out=TRI[:].rearrange("p a b -> p (a b)"), in_=tri_d[:, :]), writes=["tri"])
        misc_dma(I("dma_start", out=INVC[:], in_=invc_d[:, :]), writes=["invc"])
        P.op("dve", I("memset", ONESF[:], 1.0), writes=["onesf"])
        P.op("dve", I("memset", ONESB[:], 1.0), writes=["onesb"])
        P.op("dve", I("memset", PT[:], 0.0), writes=["PT"])
        sc_q = 192.0 ** -0.5
        P.op("dve", I("tensor_scalar", out=GS[:, 0:1], in0=G[:, G_QN:G_QN + 1], scalar1=sc_q, scalar2=None, op0=ALU.mult), reads=["G"], writes=["GS"])
        P.op("dve", I("tensor_scalar", out=GS[:, 1:3], in0=G[:, G_QR:G_QR + 2], scalar1=sc_q, scalar2=None, op0=ALU.mult), reads=["G"], writes=["GS"])
        P.op("dve", I("tensor_scalar", out=GS[:, 3:4], in0=G[:, G_XQ:G_XQ + 1], scalar1=128.0 ** -0.5, scalar2=None, op0=ALU.mult), reads=["G"], writes=["GS"])

        MNT = SC[:, 0:8192].rearrange("p (c t) -> p c t", t=256)
        MST = SCf[:, 4096:8192].rearrange("p (a f) -> p a f", f=2048)
        mstk = [sck(j) for j in range(16, 32)]
        bss = bank()
        reserved.add(bss)
        for pas in range(2):
            for fg in range(2):
                P.dma("sp", I("dma_start", out=MST, in_=mem_d[:, fg * 2048:(fg + 1) * 2048].rearrange("(a p) f -> p a f", p=128)), d_x[0], writes=mstk)
                for cl in range(16):
                    ci = fg * 16 + cl
                    b = bank()
                    P.pe_group([I("transpose", PS[b][:, a * 128:(a + 1) * 128], MST[:, a, cl * 128:(cl + 1) * 128], ident) for a in range(2)],
                               reads=mstk + ["cst"], writes=[pk(b)])
                    if pas == 0:
                        t = ci % 2
                        P.op("act", I("activation", out=TMPb[t][:, 0:256], in_=PS[b][:, 0:256], func=AF.Square), reads=[pk(b)], writes=[tk(t)])
                        P.pe_group([I("matmul", PS[bss][:, 0:256], lhsT=ONESB[:], rhs=TMPb[t][:, 0:256], start=(ci == 0), stop=(ci == DC - 1))],
                                   reads=[tk(t), "onesb"], writes=[pk(bss)])
                    else:
                        P.op("dve", I("scalar_tensor_tensor", out=MNT[:, ci, :], in0=PS[b][:, 0:256], scalar=G[:, G_MEM + ci:G_MEM + ci + 1],
                                      in1=TMP[2][:, 0:256], op0=ALU.mult, op1=ALU.mult), reads=[pk(b), tk(2), "G"], writes=[sck(ci // 2)])
            if pas == 0:
                rstd_from(bss, D, 2, n=256)
                reserved.discard(bss)
        mrhs = [(MNT[:, ci, :], sck(ci // 2)) for ci in range(DC)]

        def evac_xk(j, ps, pkey):
            partition_norm_stats(pkey[1], 128, 0, 1, n=256)
            P.op("dve", I("scalar_tensor_tensor", out=XK[:, j, :], in0=ps[:, 0:256], scalar=G[:, G_XK:G_XK + 1], in1=TMP[1][:, 0:256],
                          op0=ALU.mult, op1=ALU.mult), reads=[pkey, tk(1), "G"], writes=["XK"])
        stream_linear(w_xkv, 0, mrhs, 0, 512, evac_xk, ntok=256)
        banks = bank_set()
        for s0 in (0, 16):
            si, sv = w_slab(w_xkv, s0, 16, 512, 512)
            for a in range(2):
                P.pe_group([I("matmul", PS[banks[a]][:], lhsT=mrhs[s0 + kk][0][:, a * 128:(a + 1) * 128], rhs=sv[:, kk, :],
                              start=(s0 + kk == 0), stop=(s0 + kk == DC - 1)) for kk in range(16)],
                           reads=[("ring", si)] + [mrhs[s0 + kk][1] for kk in range(16)], writes=[pk(banks[a])])
        for a in range(2):
            P.op("act", I("copy", out=XV[:, a, :], in_=PS[banks[a]][:]), reads=[pk(banks[a])], writes=["XV"])

        rg = [[0, 1, 2, 3], [4, 5, 6, 7]]
        for s in range(NSTEP):
            P.new_epoch()
            for tc in range(4):
                xs = XST[tc % 2]
                P.dma("sp", I("dma_start", out=xs, in_=x_d[s * T + tc * 128: s * T + (tc + 1) * 128, :]), d_x[tc % 2], writes=xstk(tc % 2))
                for cg in range(8):
                    b = bank()
                    P.pe_group([I("transpose", PS[b][:, i * 128:(i + 1) * 128], xs[:, (4 * cg + i) * 128:(4 * cg + i + 1) * 128], ident) for i in range(4)],
                               reads=xstk(tc % 2) + ["cst"], writes=[pk(b)])
                    dst = H[:, 4 * cg:4 * cg + 4, tc * 128:(tc + 1) * 128]
                    srcv = PS[b][:].rearrange("p (i t) -> p i t", t=128)
                    if cg % 2 == 0:
                        P.op("act", I("copy", out=dst, in_=srcv), reads=[pk(b)], writes=[hk(4 * cg + i) for i in range(4)])
                    else:
                        P.op("dve", I("tensor_copy", out=dst, in_=srcv), reads=[pk(b)], writes=[hk(4 * cg + i) for i in range(4)])
            POSI = TMPA[:, 5 * T:6 * T].bitcast(I32)
            KI = TMPA[:, 4 * T:5 * T].bitcast(I32)
            misc_dma(I("dma_start", out=POSI[0:64, :], in_=pos_d[s].partition_broadcast(64)), writes=[tk(5)])
            P.op("dve", I("tensor_copy", out=TMP[0][0:64, :], in_=POSI[0:64, :]), reads=[tk(5)], writes=[tk(0)])
            P.op("dve", I("tensor_scalar", out=TMP[0][0:64, :], in0=TMP[0][0:64, :], scalar1=G[0:64, G_INV:G_INV + 1], scalar2=None, op0=ALU.mult),
                 reads=[tk(0), "G"], writes=[tk(0)])
            t1, t2 = TMP[1][0:64, :], TMP[2][0:64, :]
            for which, dst in ((0, SIND), (1, COSD)):
                shift = 0.0 if which == 0 else float(np.pi / 2)
                P.op("dve", I("tensor_scalar", out=t1, in0=TMP[0][0:64, :], scalar1=shift, scalar2=None, op0=ALU.add), reads=[tk(0)], writes=[tk(1)])
                P.op("dve", I("tensor_scalar", out=KI[0:64, :], in0=t1, scalar1=1.0 / TWO_PI, scalar2=None, op0=ALU.mult), reads=[tk(1)], writes=[tk(4)])
                P.op("dve", I("tensor_copy", out=t2, in_=KI[0:64, :]), reads=[tk(4)], writes=[tk(2)])
                P.op("dve", I("scalar_tensor_tensor", out=t1, in0=t2, scalar=-C1, in1=t1, op0=ALU.mult, op1=ALU.add), reads=[tk(1), tk(2)], writes=[tk(1)])
                P.op("dve", I("scalar_tensor_tensor", out=t1, in0=t2, scalar=-C2, in1=t1, op0=ALU.mult, op1=ALU.add), reads=[tk(1), tk(2)], writes=[tk(1)])
                P.op("dve", I("tensor_scalar", out=t2, in0=t1, scalar1=float(np.pi), scalar2=-TWO_PI, op0=ALU.is_gt, op1=ALU.mult), reads=[tk(1)], writes=[tk(2)])
                P.op("dve", I("tensor_tensor", out=t1, in0=t1, in1=t2, op=ALU.add), reads=[tk(1), tk(2)], writes=[tk(1)])
                P.op("dve", I("tensor_scalar", out=t2, in0=t1, scalar1=float(-np.pi), scalar2=TWO_PI, op0=ALU.is_lt, op1=ALU.mult), reads=[tk(1)], writes=[tk(2)])
                P.op("dve", I("tensor_tensor", out=t1, in0=t1, in1=t2, op=ALU.add), reads=[tk(1), tk(2)], writes=[tk(1)])
                P.op("dve", I("tensor_scalar", out=t1, in0=t1, scalar1=-PI_SAFE, scalar2=PI_SAFE, op0=ALU.max, op1=ALU.min), reads=[tk(1)], writes=[tk(1)])
                P.op("act", I("activation", out=dst[0:64, :], in_=t1, func=AF.Sin), reads=[tk(1)], writes=["rope"])
            P.op("dve", I("tensor_scalar", out=SIND[0:64, :], in0=SIND[0:64, :], scalar1=G[0:64, G_SGN:G_SGN + 1], scalar2=None, op0=ALU.mult),
                 reads=["rope", "G"], writes=["rope"])

            norm_full(G_F1)
            ffn(w_gu1, w_d1)
            if s == 0:
                dump(0, HALL)
            norm_full(G_MIX)
            misc_dma(I("dma_start", out=hbuf[:, :], in_=Hf), reads=HALL, writes=["hbuf"])

            def evac_pool(j, ps, pkey):
                P.op("act", I("copy", out=E[:, j, 16:528], in_=ps[:]), reads=[pkey], writes=ekeys(j))
            stream_linear(w_in, 0, Urhs, IN_POOL, 2048, evac_pool)
            misc_dma(I("dma_start", out=send_t[s][:, :].rearrange("p (c t) -> p c t", t=16), in_=E[:, :, 512:528]), reads=EALL, writes=[("send_t", s)])
            bkv = bank()
            reserved.add(bkv)

            def evac_kv(j, ps, pkey):
                P.op("act", I("copy", out=TMP[j], in_=ps[:]), reads=[pkey], writes=[tk(j)])
                t = 4 + j % 2
                P.op("act", I("activation", out=TMPb[t], in_=ps[:], func=AF.Square), reads=[pkey], writes=[tk(t)])
                P.pe_group([I("matmul", PS[bkv][:], lhsT=ONESB[:], rhs=TMPb[t], start=(j == 0), stop=(j == 3))], reads=[tk(t), "onesb"], writes=[pk(bkv)])
            stream_linear(w_in, 0, Urhs, IN_KV, 512, evac_kv)
            rstd_from(bkv, 512, 4)
            reserved.discard(bkv)
            for j in range(4):
                P.op("dve", I("scalar_tensor_tensor", out=SENDT[:, j, :], in0=TMP[j], scalar=G[:, G_KVL + j:G_KVL + j + 1], in1=TMP[4], op0=ALU.mult, op1=ALU.mult),
                     reads=[tk(j), tk(4), "G"], writes=[sck(8 + j)])
            bkr = bank()
            reserved.add(bkr)
            for s0 in (0, 16):
                si, sv = w_slab(w_in, s0, 16, IN_KR, 64)
                P.pe_group([I("matmul", PS[bkr][0:64, :], lhsT=sv[:, kk, :], rhs=Urhs[s0 + kk][0], start=(s0 + kk == 0), stop=(s0 + kk == DC - 1)) for kk in range(16)],
                           reads=[("ring", si)] + [Urhs[s0 + kk][1] for kk in range(16)], writes=[pk(bkr)])
            rope_norm(bkr, G_KR, G_KRS, SENDT[0:64, 4, :], sck(12), G, "G")
            reserved.discard(bkr)
            misc_dma(I("dma_start", out=send_l[s][:, 0:4 * T], in_=SC[:, 4096:4096 + 4 * T]), reads=[sck(8 + j) for j in range(4)], writes=[("send_l", s)])
            misc_dma(I("dma_start", out=send_l[s][0:64, 4 * T:5 * T], in_=SC[0:64, 4096 + 4 * T:4096 + 5 * T]), reads=[sck(12)], writes=[("send_l", s)])

            P.dma("pool", I("collective_compute", "AllGather", ALU.bypass, replica_groups=rg, ins=[send_l[s][:, :]], outs=[recv_l[s][:, :]]),
                  d_ccl, reads=[("send_l", s)], writes=[("recv_l", s)], inc=1)
            P.dma("pool", I("collective_compute", "AllGather", ALU.bypass, replica_groups=rg, ins=[send_t[s][:, :]], outs=[recv_t[s][:, :]]),
                  d_cct, reads=[("send_t", s)], writes=[("recv_t", s)], inc=1)

            gtk = [sck(i) for i in range(24, 28)]
            misc_dma(I("dma_start", out=GT, in_=recv_t[s][:, :].rearrange("(r p) n -> p r n", p=128)), reads=[("recv_t", s)], writes=gtk)
            halo = E[:, :, 0:16]
            P.op("dve", I("tensor_scalar", out=halo, in0=PT[:].rearrange("p (c t) -> p c t", t=16), scalar1=G[:, G_SEL + 4:G_SEL + 5], scalar2=None, op0=ALU.mult),
                 reads=["PT", "G"], writes=EALL)
            for r in range(4):
                P.op("dve", I("scalar_tensor_tensor", out=halo, in0=GT[:, r, :].rearrange("p (c t) -> p c t", t=16), scalar=G[:, G_SEL + r:G_SEL + r + 1], in1=halo,
                              op0=ALU.mult, op1=ALU.add), reads=gtk + ["G"] + EALL, writes=EALL)
            P.op("dve", I("tensor_copy", out=PT[:], in_=GT[:, 3, :]), reads=gtk, writes=["PT"])
            BA = TMPA[:, 0:528]
            BB = TMPA[:, 1024:1552]
            bufs = [(BA, [tk(0), tk(1)]), (BB, [tk(2), tk(3)])]
            for ci in range(16):
                g = ci // 4
                w = 2 << g
                src = E[:, ci, :]
                ek = ekeys(ci)
                cur, curk = src, ek
                d = 1
                bi = 0
                while d < w:
                    dst, dstk = bufs[bi]
                    lo = 2 * d - 1
                    P.op("dve", I("tensor_tensor", out=dst[:, lo:528], in0=cur[:, lo:528], in1=cur[:, lo - d:528 - d], op=ALU.add), reads=curk, writes=dstk)
                    cur, curk = dst, dstk
                    bi ^= 1
                    d *= 2
                P.op("dve", I("scalar_tensor_tensor", out=AMX[:, ci, :], in0=cur[:, 16:528], scalar=1.0 / w, in1=src[:, 16:528], op0=ALU.mult, op1=ALU.subtract),
                     reads=curk + ek, writes=[sck(8 + ci)])
                if s == 0:
                    P.op("dve", I("tensor_tensor", out=TMP[4][:, 0:16], in0=cur[:, 16:32], in1=INVC[:, g * 16:(g + 1) * 16], op=ALU.mult), reads=curk + ["invc"], writes=[tk(4)])
                    P.op("dve", I("tensor_tensor", out=AMX[:, ci, 0:16], in0=TMP[4][:, 0:16], in1=src[:, 16:32], op=ALU.subtract), reads=[tk(4)] + ek, writes=[sck(8 + ci)])
            bq = bank()
            reserved.add(bq)

            def evac_q(j, ps, pkey):
                P.op("act", I("copy", out=ZQ[:, j, :], in_=ps[:]), reads=[pkey], writes=[hk(17 + j)])
                t = 5
                P.op("act", I("activation", out=TMPb[t], in_=ps[:], func=AF.Square), reads=[pkey], writes=[tk(t)])
                P.pe_group([I("matmul", PS[bq][:], lhsT=ONESB[:], rhs=TMPb[t], start=(j == 0), stop=(j == 7))], reads=[tk(t), "onesb"], writes=[pk(bq)])
            stream_linear(w_in, 0, Urhs, IN_Q, 1024, evac_q)
            rstd_from(bq, 1024, 5)
            reserved.discard(bq)
            for j in range(8):
                P.op("dve", I("scalar_tensor_tensor", out=CQ[:, j, :], in0=ZQ[:, j, :], scalar=G[:, G_QL + j:G_QL + j + 1], in1=TMP[5], op0=ALU.mult, op1=ALU.mult),
                     reads=[hk(17 + j), tk(5), "G"], writes=[sck(j)])
            cqr = [(CQ[:, j, :], sck(j)) for j in range(8)]

            for g in range(4):
                banks = bank_set()
                si, sv = w_slab(w_pool, g * 4, 4, 0, 512)
                for j in range(4):
                    P.pe_group([I("matmul", PS[banks[j]][:], lhsT=sv[:, i, j * 128:(j + 1) * 128], rhs=AMX[:, 4 * g + i, :], start=(i == 0), stop=(i == 3)) for i in range(4)],
                               reads=[("ring", si)] + [sck(8 + 4 * g + i) for i in range(4)], writes=[pk(banks[j])])
                for j in range(4):
                    co = 4 * g + j
                    P.op("dve", I("tensor_scalar", out=AOUT[:, co, :], in0=PS[banks[j]][:], scalar1=G[:, G_PS + co:G_PS + co + 1], scalar2=None, op0=ALU.mult),
                         reads=[pk(banks[j]), "G"] + EALL, writes=[hk(co // 2)])

            for half in range(2):
                for gi in range(5):
                    pos0 = s * 2048 + gi * 512 if gi < 4 else SEQ
                    ck = ("Kc", s) if gi < 4 else "KcOwn"
                    if gi < 4:
                        src = recv_l[s][gi * 128:(gi + 1) * 128, 0:4 * T]
                        rk = ("recv_l", s)
                        srckr = recv_l[s][gi * 128:gi * 128 + 64, 4 * T:5 * T]
                    else:
                        src = send_l[s][:, 0:4 * T]
                        rk = ("send_l", s)
                        srckr = send_l[s][0:64, 4 * T:5 * T]
                    li = ring_load(lambda slot: [I("dma_start", out=slot[:, 0:4 * T], in_=src)], reads=[rk])
                    lat = RING[li][:, 0:4 * T].rearrange("p (k t) -> p k t", t=T)
                    wsi, wsv = w_slab(w_ukv, 0, 4, half * 2048, 2048)
                    if half == 0:
                        misc_dma(I("dma_start", out=Krc[:, pos0:pos0 + T], in_=srckr), reads=[rk], writes=[ck])
                    def emit_k(hh):
                        b = bank()
                        reserved.add(b)
                        P.pe_group([I("matmul", PS[b][:], lhsT=wsv[:, kk, hh * 256:hh * 256 + 128], rhs=lat[:, kk, :], start=(kk == 0), stop=(kk == 3)) for kk in range(4)],
                                   reads=[("ring", wsi), ("ring", li)], writes=[pk(b)])
                        return b

                    def emit_v(tc, cg):
                        b = bank()
                        P.pe_group([I("matmul", PS[b][:], lhsT=lat[:, kk, tc * 128:(tc + 1) * 128],
                                      rhs=wsv[:, kk, :].rearrange("p (h c) -> p h c", c=256)[:, cg * 4:(cg + 1) * 4, 128:256], start=(kk == 0), stop=(kk == 3))
                                    for kk in range(4)], reads=[("ring", wsi), ("ring", li)], writes=[pk(b)])
                        vi = (tc * 2 + cg) % 2
                        vst = QR[vi]
                        vstk = sck(29 + vi)
                        P.op("act", I("copy", out=vst, in_=PS[b][:]), reads=[pk(b)], writes=[vstk])
                        h0 = half * 8 + cg * 4
                        kb = pos0 // 128 + tc
                        P.dma("sp", I("dma_start", out=Vc[h0:h0 + 4, :, kb, :].rearrange("h p d -> p h d"), in_=vst.rearrange("p (h d) -> p h d", d=128)),
                              d_kst[5 + vi], reads=[vstk], writes=[ck])
                    vlist = [(tc, cg) for tc in range(4) for cg in range(2)]
                    b_next = emit_k(0)
                    for hh in range(8):
                        h = half * 8 + hh
                        b = b_next
                        if hh + 1 < 8:
                            b_next = emit_k(hh + 1)
                        emit_v(*vlist[hh])
                        tq, tr_ = (0, 1) if hh % 2 == 0 else (2, 3)
                        partition_norm_stats(b, 128, tq, tr_)
                        ki = hh % 5
                        kst = PTILE[ki] if ki < 3 else QN[ki - 3]
                        kstk = sck(24 + ki)
                        P.op("dve", I("scalar_tensor_tensor", out=kst, in0=PS[b][:], scalar=G[:, G_KN:G_KN + 1], in1=TMP[tr_], op0=ALU.mult, op1=ALU.mult),
                             reads=[pk(b), tk(tr_), "G"], writes=[kstk])
                        reserved.discard(b)
                        P.dma("sp", I("dma_start", out=Kc[h, :, pos0:pos0 + T], in_=kst), d_kst[ki], reads=[kstk], writes=[ck])
            for i_ in range(2):
                P.op("dve", I("memset", QR[i_][64:128, :], 0.0), writes=[sck(29 + i_)])
            def prep_q(h):
                qi = h % 2
                si, sv = w_slab(w_uq, 0, 8, h * 192, 192)
                bn = bank()
                reserved.add(bn)
                P.pe_group([I("matmul", PS[bn][:], lhsT=sv[:, kk, 0:128], rhs=cqr[kk][0], start=(kk == 0), stop=(kk == 7)) for kk in range(8)],
                           reads=[("ring", si)] + [cqr[kk][1] for kk in range(8)], writes=[pk(bn)])
                br = bank()
                reserved.add(br)
                P.pe_group([I("matmul", PS[br][0:64, :], lhsT=sv[:, kk, 128:192], rhs=cqr[kk][0], start=(kk == 0), stop=(kk == 7)) for kk in range(8)],
                           reads=[("ring", si)] + [cqr[kk][1] for kk in range(8)], writes=[pk(br)])
                yield
                yield from pns_gen(bn, 128, 0, 1)
                P.op("dve", I("scalar_tensor_tensor", out=QN[qi], in0=PS[bn][:], scalar=GS[:, 0:1], in1=TMP[1], op0=ALU.mult, op1=ALU.mult),
                     reads=[pk(bn), tk(1), "GS"], writes=[sck(27 + qi)])
                reserved.discard(bn)
                yield
                yield from rope_gen(br, 1, 2, QR[qi][0:64, :], sck(29 + qi), GS, "GS")
                reserved.discard(br)

            run(prep_q(0))
            for h in range(NH):
                qi = h % 2
                bgq = prep_q(h + 1) if h + 1 < NH else None
                chunks = [(ci * 2048, 16, ("Kc", ci), ci == s) for ci in range(s + 1)] + [(SEQ, 4, "KcOwn", None)]
                blocks = []
                for (k0, nkb, ckey, cur) in chunks:
                    nk = nkb * 128
                    holder = {}

                    def loader(holder=holder, k0=k0, nk=nk, nkb=nkb, ckey=ckey, h=h):
                        if "li" not in holder:
                            holder["li"] = ring_load(lambda slot: [
                                I("dma_start", out=slot[:, 0:nk], in_=Kc[h, :, k0:k0 + nk]),
                                I("dma_start", out=slot[:, 2048:2048 + nk].rearrange("p (k d) -> p k d", d=128), in_=Vc[h, :, k0 // 128:k0 // 128 + nkb, :]),
                                I("dma_start", out=slot[0:64, 4096:4096 + nk], in_=Krc[:, k0:k0 + nk]),
                            ], reads=[ckey])
                        return holder["li"]
                    for kb in range(nkb):
                        bias = None
                        tri = None
                        if cur is None:
                            tri = kb
                        elif cur:
                            gi = kb // 4
                            bias = G[:, G_BIAS + gi:G_BIAS + gi + 1]
                        blocks.append(dict(loader=loader, bias=bias, tri=tri, kb=kb))

                def qk_fn(i, bs, blocks=blocks, qi=qi):
                    slot, kb = RING[blocks[i]["loader"]()], blocks[i]["kb"]
                    return [I("matmul", PS[bs][:], lhsT=slot[:, kb * 128:(kb + 1) * 128], rhs=QN[qi], start=True, stop=False),
                            I("matmul", PS[bs][:], lhsT=slot[:, 4096 + kb * 128:4096 + (kb + 1) * 128], rhs=QR[qi], start=False, stop=True)]

                def v_fn(i, blocks=blocks):
                    slot, kb = RING[blocks[i]["loader"]()], blocks[i]["kb"]
                    return slot[:, 2048 + kb * 128:2048 + (kb + 1) * 128]
                softmax_pv(blocks, qk_fn, v_fn, BOUT[:, h, :], hk(8 + h // 2), [sck(27 + qi), sck(29 + qi)], bg=bgq)
                if bgq is not None:
                    run(bgq)
            if s == 0 and debug:
                dump(4, HALL)

            arhs = [(AOUT[:, k, :], hk(k // 2)) for k in range(16)]
            brhs = [(BOUT[:, k, :], hk(8 + k // 2)) for k in range(16)]
            for mb in range(8):
                Bps = {}

                def evac_gp(j, ps, pkey):
                    P.op("act", I("activation", out=TMP[j], in_=ps[:], func=AF.Sigmoid), reads=[pkey], writes=[tk(j)])

                def evac_a(j, ps, pkey):
                    P.op("dve", I("tensor_tensor", out=TMP[j], in0=TMP[j], in1=ps[:], op=ALU.mult), reads=[pkey, tk(j)], writes=[tk(j)])

                def evac_b(j, ps, pkey, Bps=Bps):
                    Bps[j] = (ps, pkey)
                    return True

                def evac_gm(j, ps, pkey, Bps=Bps, mb=mb):
                    t = 4 + j % 2
                    P.op("act", I("activation", out=TMP[t], in_=ps[:], func=AF.Sigmoid), reads=[pkey], writes=[tk(t)])
                    P.op("dve", I("tensor_tensor", out=TMP[t], in0=TMP[t], in1=Bps[j][0][:], op=ALU.mult), reads=[Bps[j][1], tk(t)], writes=[tk(t)])
                    P.op("dve", I("tensor_tensor", out=ACT_[:, 4 * mb + j, :], in0=TMP[j], in1=TMP[t], op=ALU.add), reads=[tk(j), tk(t)], writes=[sck(4 * mb + j)])
                    reserved.discard(Bps[j][1][1])
                stream_linear(w_in, 0, Urhs, IN_GP + mb * 512, 512, evac_gp)
                stream_linear(w_pa, 0, arhs, mb * 512, 512, evac_a)
                stream_linear(w_pb, 0, brhs, mb * 512, 512, evac_b)
                stream_linear(w_in, 0, Urhs, IN_GM + mb * 512, 512, evac_gm)
            if s == 0:
                dumpb(0, SCALL)

            for mb in range(8):
                misc_dma(I("dma_start", out=Hf[:, mb * 2048:(mb + 1) * 2048], in_=hbuf[:, mb * 2048:(mb + 1) * 2048]), reads=["hbuf"], writes=[hk(4 * mb + i) for i in range(4)])
            mrg = [(ACT_[:, k, :], sck(k)) for k in range(DC)]

            def evac_o(m, ps, pkey):
                P.op("dve", I("tensor_tensor", out=H[:, m, :], in0=ps[:], in1=H[:, m, :], op=ALU.add), reads=[pkey, hk(m)], writes=[hk(m)])
            stream_linear(w_out, 0, mrg, 0, D, evac_o)
            if s == 0:
                dump(1, HALL)

            norm_full(G_X)

            def evac_xq(j, ps, pkey):
                partition_norm_stats(pkey[1], 128, 0, 1)
                P.op("dve", I("scalar_tensor_tensor", out=XQT[:, j, :], in0=ps[:], scalar=GS[:, 3:4], in1=TMP[1], op0=ALU.mult, op1=ALU.mult),
                     reads=[pkey, tk(1), "GS"], writes=[sck(j)])
            stream_linear(w_xq, 0, Urhs, 0, 512, evac_xq)
            for hd in range(4):
                blocks = [dict(reads=["XK", "XV"], bias=None, tri=None, mc=mc) for mc in range(2)]

                def qk_fn(i, bs, hd=hd):
                    return [I("matmul", PS[bs][:], lhsT=XK[:, hd, i * 128:(i + 1) * 128], rhs=XQT[:, hd, :], start=True, stop=True)]

                def v_fn(i, hd=hd):
                    return XV[:, i, hd * 128:(hd + 1) * 128]
                softmax_pv(blocks, qk_fn, v_fn, XOT[:, hd, :], sck(4 + hd), [sck(hd)])
            xor = [(XOT[:, k, :], sck(4 + k)) for k in range(4)]
            stream_linear(w_xo, 0, xor, 0, D, evac_o)
            if s == 0:
                dump(2, HALL)

            norm_full(G_F2)
            ffn(w_gu2, w_d2)
            if s == 0:
                dump(3, HALL)

            for tc in range(4):
                ost = XST[tc % 2]
                for cg in range(8):
                    b = bank()
                    P.pe_group([I("transpose", PS[b][:, i * 128:(i + 1) * 128], H[:, 4 * cg + i, tc * 128:(tc + 1) * 128], ident) for i in range(4)],
                               reads=[hk(4 * cg + i) for i in range(4)] + ["cst"], writes=[pk(b)])
                    uk = [sck(16 * (tc % 2) + 2 * cg), sck(16 * (tc % 2) + 2 * cg + 1)]
                    if cg % 2 == 0:
                        P.op("act", I("copy", out=ost[:, cg * 512:(cg + 1) * 512], in_=PS[b][:]), reads=[pk(b)], writes=uk)
                    else:
                        P.op("dve", I("tensor_copy", out=ost[:, cg * 512:(cg + 1) * 512], in_=PS[b][:]), reads=[pk(b)], writes=uk)
                P.dma("sp", I("dma_start", out=out_d[s * T + tc * 128: s * T + (tc + 1) * 128, :], in_=ost), d_o[tc % 2], reads=xstk(tc % 2), writes=["out"])

        P.wait_all("sp", ["out"] + ([("dbg", i) for i in range(NDBG)] + [("dbgb", i) for i in range(NDBG)] if debug else []))
        P.run_block()
        print("instr counts", P.ninstr, "nsem", P.nsem, flush=True)
    return nc


def _fm(v):
    v = np.asarray(v, np.float32).reshape(-1, 128)
    return np.ascontiguousarray(v.T)


def make_in_maps(inputs, NSTEP=4, dff=DFF):
    SEQ = NSTEP * 2048
    x = np.asarray(inputs["x"]); mem = np.asarray(inputs["mem"]); positions = np.asarray(inputs["positions"])
    sq = lambda k: np.ascontiguousarray(np.asarray(inputs[k])[0])
    weights = {k: sq(k) for k in ["ffn1_w_gu", "ffn1_w_down", "ffn2_w_gu", "ffn2_w_down", "w_in", "w_uq", "w_ukv", "w_branch_pool", "w_branch_mla",
                                  "w_out", "w_xq", "w_xkv", "w_xo"]}
    weights["w_pool"] = np.ascontiguousarray(np.asarray(inputs["w_pool"])[0].reshape(2048, 512))
    if dff != DFF:
        for f in ("ffn1", "ffn2"):
            wg = weights[f + "_w_gu"]
            weights[f + "_w_gu"] = np.ascontiguousarray(np.concatenate([wg[:, 0:dff], wg[:, DFF:DFF + dff]], axis=1))
            weights[f + "_w_down"] = np.ascontiguousarray(weights[f + "_w_down"][0:dff])
    gbase = np.zeros((128, NG), np.float32)
    gbase[:, G_F1:G_F1 + 32] = _fm(sq("ffn1_norm")); gbase[:, G_MIX:G_MIX + 32] = _fm(sq("mix_norm"))
    gbase[:, G_X:G_X + 32] = _fm(sq("x_norm")); gbase[:, G_F2:G_F2 + 32] = _fm(sq("ffn2_norm"))
    gbase[:, G_MEM:G_MEM + 32] = _fm(sq("mem_norm")); gbase[:, G_PS:G_PS + 16] = _fm(sq("pool_scale"))
    gbase[:, G_QL:G_QL + 8] = _fm(sq("q_latent_norm")); gbase[:, G_KVL:G_KVL + 4] = _fm(sq("kv_latent_norm"))
    gbase[:, G_QN] = sq("q_nope_norm"); gbase[:, G_KN] = sq("k_nope_norm"); gbase[:, G_XQ] = sq("xq_norm"); gbase[:, G_XK] = sq("xk_norm")
    qr = sq("q_rope_norm"); kr = sq("k_rope_norm")
    gbase[0:64, G_QR] = qr; gbase[0:64, G_QRS] = np.concatenate([qr[32:], qr[:32]])
    gbase[0:64, G_KR] = kr; gbase[0:64, G_KRS] = np.concatenate([kr[32:], kr[:32]])
    half = 32
    inv = (1.0 / (np.float32(10000.0) ** (np.arange(half, dtype=np.float32) * np.float32(2.0 / 64)))).astype(np.float32)
    gbase[0:64, G_INV] = np.concatenate([inv, inv])
    gbase[0:32, G_SGN] = -1.0; gbase[32:64, G_SGN] = 1.0
    cst = np.zeros((128, 192), np.float32)
    cst[:, 0:128] = np.eye(128, dtype=np.float32)
    for m in range(64):
        cst[(m + 32) % 64, 128 + m] = 1.0
    p_ = np.arange(128)[:, None]
    q_ = np.arange(T)[None, :]
    tri = np.concatenate([(j * 128 + p_ <= q_).astype(np.float32) for j in range(4)], axis=1).astype(ml_dtypes.bfloat16)
    in_maps = []
    for core in range(8):
        b, c = core // 4, core % 4
        g = gbase.copy()
        for r in range(4):
            g[:, G_BIAS + r] = 0.0 if r < c else NEG
            g[:, G_SEL + r] = 1.0 if (c >= 1 and r == c - 1) else 0.0
        g[:, G_SEL + 4] = 1.0 if c == 0 else 0.0
        invc = np.zeros((128, 64), np.float32)
        for gg in range(4):
            w = 2 << gg
            t = np.arange(16)
            invc[:, gg * 16:(gg + 1) * 16] = (1.0 / np.minimum(t + 1, w)) if c == 0 else (1.0 / w)
        rows = np.concatenate([np.arange(s * 2048 + c * 512, s * 2048 + c * 512 + 512) for s in range(NSTEP)])
        m = {"x": np.ascontiguousarray(x[b, rows, :]), "pos": np.ascontiguousarray(positions[b, rows].reshape(NSTEP, T).astype(np.int32)),
             "mem": np.ascontiguousarray(mem[b]), "gpack": g, "cst": cst, "tri": tri, "invc": invc}
        m.update(weights)
        in_maps.append(m)
    return in_maps


def assemble(results, NSTEP=4):
    SEQ = NSTEP * 2048
    out = np.zeros((2, SEQ, D), np.float32)
    for core in range(8):
        b, c = core // 4, core % 4
        o = results[core]["out"]
        for s in range(NSTEP):
            out[b, s * 2048 + c * 512: s * 2048 + c * 512 + 512, :] = o[s * T:(s + 1) * T]
    return out


_NC_CACHE = {}


def kernel(**inputs):
    NSTEP = 4
    if NSTEP not in _NC_CACHE:
        _NC_CACHE[NSTEP] = build(NSTEP)
    nc = _NC_CACHE[NSTEP]
    in_maps = make_in_maps(inputs, NSTEP)
    res = run_bass_kernel_spmd(nc, in_maps, core_ids=list(range(8)))
    return assemble(res.results, NSTEP)
```
